# Optimizing a Trainium2 kernel written in Bass

```python
import jax, jax.numpy as jnp
from jax import lax
import numpy as np

D_MODEL = 1024
BATCH = 32
SEQ = 256
DEPTH = 1
DEC_BATCH = 8
DEC_SEQ = 1024
PAST_LEN = 512

GRID_W = 64
EPS = 1e-6
HG_HEADS = 4
HG_DK = 128
HG_DV = 128
HG_WIDTH = HG_HEADS * HG_DK
HG_CHUNK = 32
N_HEADS = 8
N_KV = 2
HEAD_DIM = 64
Q_WIDTH = N_HEADS * HEAD_DIM
KV_WIDTH = N_KV * HEAD_DIM
ATTN_BLOCK = 128
ROPE_THETA = 10000.0
N_BRANCH = 2
D_FF = 4 * D_MODEL
N_MOD = 6
D_IN = 5 * HG_WIDTH + Q_WIDTH + 2 * KV_WIDTH + N_BRANCH * D_MODEL

kernel_name = "hybrid_hgrn2_gqa_diffusion_step"

F32 = jnp.float32


def _split_points():
    sizes = [HG_WIDTH] * 5 + [Q_WIDTH, KV_WIDTH, KV_WIDTH]
    return [int(v) for v in np.cumsum(sizes)]


def rms_norm(x, w):
    xf = x.astype(F32)
    y = xf * lax.rsqrt(jnp.mean(xf * xf, axis=-1, keepdims=True) + EPS)
    return (y * w.astype(F32)).astype(x.dtype)


def adaln(cond, w_ada, b_ada):
    m = jax.nn.silu(cond.astype(F32)) @ w_ada.astype(F32) + b_ada.astype(F32)
    return jnp.split(m[..., None, :], N_MOD, axis=-1)


def _rope_1d(x, cos, sin):
    x1, x2 = jnp.split(x, 2, axis=-1)
    c = cos[:, None, :]
    s = sin[:, None, :]
    return jnp.concatenate([x1 * c - x2 * s, x1 * s + x2 * c], axis=-1)


def axial_rope(x):
    n_tokens = x.shape[1]
    rows = n_tokens // GRID_W
    row = jnp.repeat(jnp.arange(rows, dtype=F32), GRID_W)
    col = jnp.tile(jnp.arange(GRID_W, dtype=F32), rows)
    axis_dim = HEAD_DIM // 2
    freqs = ROPE_THETA ** (-jnp.arange(0, axis_dim, 2, dtype=F32) / axis_dim)
    ang_r = row[:, None] * freqs
    ang_c = col[:, None] * freqs
    xf = x.astype(F32)
    x_row, x_col = jnp.split(xf, 2, axis=-1)
    out = jnp.concatenate([_rope_1d(x_row, jnp.cos(ang_r), jnp.sin(ang_r)),
                           _rope_1d(x_col, jnp.cos(ang_c), jnp.sin(ang_c))], axis=-1)
    return out.astype(x.dtype)


def block_attention(q, k, v):
    b, lq = q.shape[:2]
    nb = lq // ATTN_BLOCK
    qb = q.astype(F32).reshape(b, nb, ATTN_BLOCK, N_KV, N_HEADS // N_KV, HEAD_DIM)
    qb = jnp.moveaxis(qb, 1, 0)
    kf = k.astype(F32)
    vf = v.astype(F32)
    scale = HEAD_DIM ** -0.5

    def one_block(qi):
        s = jnp.einsum('bqkgd,bskd->bkgqs', qi, kf) * scale
        p = jax.nn.softmax(s, axis=-1)
        return jnp.einsum('bkgqs,bskd->bqkgd', p, vf)

    o = lax.map(one_block, qb)
    return jnp.moveaxis(o, 0, 1).reshape(b, lq, Q_WIDTH)


def hgrn_scan(q, k, v, log_f, s0):
    b, l = q.shape[:2]
    n = l // HG_CHUNK

    def chunks(t):
        return t.astype(F32).reshape(b, n, HG_CHUNK, HG_HEADS, -1).transpose(0, 3, 1, 2, 4)

    q, k, v, log_f = chunks(q), chunks(k), chunks(v), chunks(log_f)
    a = jnp.cumsum(log_f, axis=3)
    a_last = a[:, :, :, -1]
    q_dec = q * jnp.exp(a)
    k_dec = k * jnp.exp(-a)
    mask = jnp.tril(jnp.ones((HG_CHUNK, HG_CHUNK), dtype=bool))
    scores = jnp.where(mask, jnp.einsum('bhnik,bhnjk->bhnij', q_dec, k_dec), 0.0)
    o_intra = jnp.einsum('bhnij,bhnjv->bhniv', scores, v)
    k_end = k * jnp.exp(a_last[:, :, :, None, :] - a)
    kv = jnp.einsum('bhnck,bhncv->bhnkv', k_end, v)
    decay = jnp.exp(a_last)

    def step(s, inp):
        dec, kv_n = inp
        return dec[..., None] * s + kv_n, s

    s_final, s_starts = lax.scan(step, s0.astype(F32),
                                 (jnp.moveaxis(decay, 2, 0), jnp.moveaxis(kv, 2, 0)))
    s_starts = jnp.moveaxis(s_starts, 0, 2)
    o_inter = jnp.einsum('bhnck,bhnkv->bhncv', q_dec, s_starts)
    o = (o_intra + o_inter).transpose(0, 2, 3, 1, 4).reshape(b, l, HG_HEADS, HG_DV)
    return o, s_final


def hgrn_branch(hq, hf_fwd, hf_bwd, hi, hg, lb_fwd, lb_bwd, norm_w, s0_fwd, s0_bwd):
    b, l = hq.shape[:2]

    def heads(t):
        return t.astype(F32).reshape(b, l, HG_HEADS, -1)

    q = heads(hq) * HG_DK ** -0.5
    v = heads(hi)

    def gate(h_f, lb):
        f = lb + (1.0 - lb) * jax.nn.sigmoid(h_f.astype(F32))
        return heads(jnp.log(f)), heads(1.0 - f)

    logf_f, k_f = gate(hf_fwd, lb_fwd)
    logf_b, k_b = gate(hf_bwd, lb_bwd)
    o_f, s_f = hgrn_scan(q, k_f, v, logf_f, s0_fwd)

    def flip(t):
        return jnp.flip(t, axis=1)

    o_b, s_b = hgrn_scan(flip(q), flip(k_b), flip(v), flip(logf_b), s0_bwd)
    o = o_f + flip(o_b)
    o = rms_norm(o, norm_w) * jax.nn.silu(heads(hg))
    return o.reshape(b, l, HG_WIDTH), jnp.stack([s_f, s_b], axis=1)


def token_mixer(h, w_in, q_norm_w, k_norm_w, lb_fwd, lb_bwd, hg_norm_w, w_ho, w_ao, w_out,
                ctx_k, ctx_v, s0_fwd, s0_bwd, latent):
    b, l = h.shape[:2]
    z = h @ w_in
    hq, hf_f, hf_b, hi, hg, aq, ak, av, gl = jnp.split(z, _split_points(), axis=-1)
    q = rms_norm(aq.reshape(b, l, N_HEADS, HEAD_DIM), q_norm_w)
    k = rms_norm(ak.reshape(b, l, N_KV, HEAD_DIM), k_norm_w)
    v = av.reshape(b, l, N_KV, HEAD_DIM)
    if latent:
        q = axial_rope(q)
        k_lat = axial_rope(k)
        k_all = jnp.concatenate([ctx_k.astype(k_lat.dtype), k_lat], axis=1)
        v_all = jnp.concatenate([ctx_v.astype(v.dtype), v], axis=1)
    else:
        k_all, v_all = k, v
    o_a = block_attention(q, k_all, v_all)
    o_h, s_final = hgrn_branch(hq, hf_f, hf_b, hi, hg, lb_fwd, lb_bwd, hg_norm_w, s0_fwd, s0_bwd)
    g = jax.nn.sigmoid(gl.astype(F32)).reshape(b, l, N_BRANCH, D_MODEL)
    merged = g[:, :, 0] * (o_h @ w_ho) + g[:, :, 1] * (o_a @ w_ao)
    return merged @ w_out, k, v, s_final


def trunk_layer(x, mod, norm_mix_w, mix_w, ffn_w, ctx_k, ctx_v, s0_fwd, s0_bwd, latent):
    sh1, sc1, g1, sh2, sc2, g2 = mod
    h = rms_norm(x, norm_mix_w) * (1.0 + sc1) + sh1
    out, k, v, s = token_mixer(h, *mix_w, ctx_k, ctx_v, s0_fwd, s0_bwd, latent)
    x = x + g1 * out
    norm_ffn_w, w_ff1, w_ff2 = ffn_w
    h = rms_norm(x, norm_ffn_w) * (1.0 + sc2) + sh2
    x = x + g2 * (jnp.square(jax.nn.relu(h @ w_ff1)) @ w_ff2)
    return x, k, v, s


def setup_inputs(seed: int = 0) -> dict:
    key = jax.random.key(seed)
    ks = jax.random.split(key, 24)
    nrm = jax.random.normal
    return {
        "x_prompt": nrm(ks[0], (BATCH, SEQ, D_MODEL), F32),
        "x_sample": nrm(ks[1], (DEC_BATCH, DEC_SEQ, D_MODEL), F32),
        "cache_k": nrm(ks[2], (DEC_BATCH, DEPTH, PAST_LEN, N_KV, HEAD_DIM), F32),
        "cache_v": nrm(ks[3], (DEC_BATCH, DEPTH, PAST_LEN, N_KV, HEAD_DIM), F32),
        "state_hgrn": 0.5 * nrm(ks[4], (DEC_BATCH, DEPTH, 2, HG_HEADS, HG_DK, HG_DV), F32),
        "c": nrm(ks[5], (DEC_BATCH, D_MODEL), F32),
        "c_ctx": nrm(ks[6], (D_MODEL,), F32),
        "w_ada": nrm(ks[7], (DEPTH, D_MODEL, N_MOD * D_MODEL), F32) * D_MODEL ** -0.5,
        "b_ada": 0.02 * nrm(ks[8], (DEPTH, N_MOD * D_MODEL), F32),
        "norm_mix_w": 1.0 + 0.05 * nrm(ks[9], (DEPTH, D_MODEL), F32),
        "w_in": nrm(ks[10], (DEPTH, D_MODEL, D_IN), F32) * D_MODEL ** -0.5,
        "q_norm_w": 1.0 + 0.05 * nrm(ks[11], (DEPTH, HEAD_DIM), F32),
        "k_norm_w": 1.0 + 0.05 * nrm(ks[12], (DEPTH, HEAD_DIM), F32),
        "hgrn_lb_logits": 0.1 * nrm(ks[13], (2, DEPTH + 1, HG_WIDTH), F32),
        "hgrn_norm_w": 1.0 + 0.05 * nrm(ks[14], (DEPTH, HG_DV), F32),
        "w_hgrn_out": nrm(ks[15], (DEPTH, HG_WIDTH, D_MODEL), F32) * HG_WIDTH ** -0.5,
        "w_attn_out": nrm(ks[16], (DEPTH, Q_WIDTH, D_MODEL), F32) * Q_WIDTH ** -0.5,
        "w_out": nrm(ks[17], (DEPTH, D_MODEL, D_MODEL), F32) * D_MODEL ** -0.5,
        "norm_ffn_w": 1.0 + 0.05 * nrm(ks[18], (DEPTH, D_MODEL), F32),
        "w_ff1": nrm(ks[19], (DEPTH, D_MODEL, D_FF), F32) * D_MODEL ** -0.5,
        "w_ff2": nrm(ks[20], (DEPTH, D_FF, D_MODEL), F32) * D_FF ** -0.5,
        "final_norm_w": 1.0 + 0.05 * nrm(ks[21], (D_MODEL,), F32),
    }


def reference(x_prompt, x_sample, cache_k, cache_v, state_hgrn, c, c_ctx, w_ada, b_ada,
              norm_mix_w, w_in, q_norm_w, k_norm_w, hgrn_lb_logits, hgrn_norm_w,
              w_hgrn_out, w_attn_out, w_out, norm_ffn_w, w_ff1, w_ff2, final_norm_w):
    lb = jnp.cumsum(jax.nn.softmax(hgrn_lb_logits.astype(F32), axis=1), axis=1)
    xp, xs = x_prompt, x_sample
    zero_state = jnp.zeros((xp.shape[0], HG_HEADS, HG_DK, HG_DV), F32)
    new_k, new_v, new_s = [], [], []
    for l in range(DEPTH):
        mix_w = (w_in[l], q_norm_w[l], k_norm_w[l], lb[0, l], lb[1, l], hgrn_norm_w[l],
                 w_hgrn_out[l], w_attn_out[l], w_out[l])
        ffn_w = (norm_ffn_w[l], w_ff1[l], w_ff2[l])
        mod_p = adaln(c_ctx, w_ada[l], b_ada[l])
        xp, k_ctx, v_ctx, s_ctx = trunk_layer(xp, mod_p, norm_mix_w[l], mix_w, ffn_w,
                                              None, None, zero_state, zero_state, False)
        new_k.append(k_ctx)
        new_v.append(v_ctx)
        new_s.append(s_ctx)
        mod_s = adaln(c, w_ada[l], b_ada[l])
        xs, _, _, _ = trunk_layer(xs, mod_s, norm_mix_w[l], mix_w, ffn_w,
                                  cache_k[:, l], cache_v[:, l],
                                  state_hgrn[:, l, 0], state_hgrn[:, l, 1], True)
    y_prompt = rms_norm(xp, final_norm_w)
    y_sample = rms_norm(xs, final_norm_w)
    new_cache_k = jnp.stack(new_k, axis=1)
    new_cache_v = jnp.stack(new_v, axis=1)
    new_state_hgrn = jnp.stack(new_s, axis=1)
    return (y_prompt, y_sample, new_cache_k, new_cache_v, new_state_hgrn)
```

```python
from contextlib import ExitStack
import numpy as np
import concourse.bass as bass
import concourse.mybir as mybir
from concourse.bass_utils import run_bass_kernel_spmd

F32 = mybir.dt.float32
BF16 = mybir.dt.bfloat16
AF = mybir.ActivationFunctionType
ALU = mybir.AluOpType
AX = mybir.AxisListType

PE, ACT, DVE, POOL, SP = "tensor", "scalar", "vector", "gpsimd", "sync"
MARKS = []
STRICT_SAME_ENGINE = True
COMPUTE = (PE, ACT, DVE, POOL)
N_DMA_SEMS = 24

D = 1024
EPS = 1e-6
NT = 8
PASS_TOK = 1024
CH = 64
D_IN = 5376


class Op:
    __slots__ = ("idx", "eng", "fn", "reads", "writes", "is_dma", "deps", "need_inc",
                 "sem", "semval", "dma_prev", "phase", "real_writes")

    def __init__(self, idx, eng, fn, reads, writes, is_dma):
        self.phase = 0
        self.idx = idx
        self.eng = eng
        self.fn = fn
        self.reads = reads
        self.writes = writes
        self.is_dma = is_dma
        self.deps = []
        self.need_inc = False
        self.sem = None
        self.semval = 0
        self.dma_prev = None


class Prog:
    def __init__(self, nc):
        self.nc = nc
        self.ops = []
        self.stack = ExitStack()
        self.phase = 0
        self.max_ops = None

    def barrier(self):
        self.phase += 1

    def sb(self, name, shape, dtype):
        return self.stack.enter_context(self.nc.sbuf_tensor("sb_" + name, list(shape), dtype))

    def ps(self, name, shape, dtype):
        return self.stack.enter_context(self.nc.psum_tensor(name, list(shape), dtype))

    def op(self, eng, fn, reads=(), writes=(), dma=False):
        rw = tuple(writes)
        weff = rw + tuple(k for k in reads if isinstance(k, tuple) and k and k[0] == "ps" and k not in rw)
        o = Op(len(self.ops), eng, fn, tuple(reads), weff, dma)
        o.real_writes = rw
        o.phase = self.phase
        if self.max_ops is not None and len(self.ops) >= self.max_ops:
            return o
        self.ops.append(o)
        return o

    def dma(self, out, in_, reads=(), writes=(), eng=SP, **kw):
        return self.op(eng, lambda e: e.dma_start(out=out, in_=in_, **kw), reads, writes, dma=True)

    def finalize(self):
        nc = self.nc
        last_w = {}
        readers = {}
        for o in self.ops:
            deps = set()
            for k in o.reads:
                w = last_w.get(k)
                if w is not None:
                    deps.add(w)
            for k in o.writes:
                w = last_w.get(k)
                if w is not None:
                    deps.add(w)
                for r in readers.get(k, ()):
                    deps.add(r)
            for k in o.writes:
                last_w[k] = o
                readers[k] = []
            for k in o.reads:
                if k in o.writes:
                    continue
                lst = readers.setdefault(k, [])
                if not o.is_dma:
                    lst[:] = [r for r in lst if r.is_dma or r.eng != o.eng]
                lst.append(o)
            deps.discard(o)
            for d in deps:
                if (not d.is_dma) and (not o.is_dma) and d.eng == o.eng:
                    if o.eng == PE:
                        continue
                    raw = any(k in d.real_writes for k in o.reads)
                    if not raw and not STRICT_SAME_ENGINE:
                        continue
                o.deps.append(d)
                d.need_inc = True

        last_comp = {}
        last_dma = {}
        nd0 = 0
        seen_phase = {}
        snap = {}
        cur_phase = 0
        for o in self.ops:
            if o.phase != cur_phase:
                cur_phase = o.phase
                snap[cur_phase] = (dict(last_comp), dict(last_dma))
            if o.phase > 0 and seen_phase.get(o.eng, 0) < o.phase:
                seen_phase[o.eng] = o.phase
                lc, ld = snap[o.phase]
                for d in list(lc.values()) + list(ld.values()):
                    if d is not o and d not in o.deps:
                        if (not d.is_dma) and d.eng == o.eng and not o.is_dma:
                            continue
                        o.deps.append(d)
                        d.need_inc = True
            if o.is_dma:
                last_dma[nd0 % N_DMA_SEMS] = o
                nd0 += 1
            else:
                last_comp[o.eng] = o

        eng_sem = {e: self.stack.enter_context(nc.semaphore("s_" + e)) for e in COMPUTE}
        dma_sems = [self.stack.enter_context(nc.semaphore("d%d" % i)) for i in range(N_DMA_SEMS)]
        cnt = {e: 0 for e in COMPUTE}
        dcnt = [0] * N_DMA_SEMS
        dlast = [None] * N_DMA_SEMS
        nd = 0
        for o in self.ops:
            if o.is_dma:
                s = nd % N_DMA_SEMS
                nd += 1
                o.dma_prev = dlast[s]
                dcnt[s] += 16
                o.sem = dma_sems[s]
                o.semval = dcnt[s]
                dlast[s] = o
                o.need_inc = True
            elif o.need_inc:
                cnt[o.eng] += 1
                o.sem = eng_sem[o.eng]
                o.semval = cnt[o.eng]

        per_eng = {}
        for o in self.ops:
            per_eng.setdefault(o.eng, []).append(o)
        final_waits = [(s, c) for s, c in zip(dma_sems, dcnt) if c > 0]

        def emit(eng_name, engine):
            waited = {}
            for o in per_eng.get(eng_name, []):
                need = {}
                for d in o.deps:
                    if need.get(d.sem, 0) < d.semval:
                        need[d.sem] = d.semval
                if o.is_dma and o.dma_prev is not None:
                    d = o.dma_prev
                    if need.get(d.sem, 0) < d.semval:
                        need[d.sem] = d.semval
                for sem, val in need.items():
                    if waited.get(sem, 0) >= val:
                        continue
                    engine.wait_ge(sem, val)
                    waited[sem] = val
                ins = o.fn(engine)
                if o.need_inc:
                    ins.then_inc(o.sem, 16 if o.is_dma else 1)
            if eng_name == SP:
                for s, c in final_waits:
                    engine.wait_ge(s, c)

        with nc.Block() as block:
            @block.sync
            def _(e):
                emit(SP, e)

            @block.tensor
            def _(e):
                emit(PE, e)

            @block.scalar
            def _(e):
                emit(ACT, e)

            @block.vector
            def _(e):
                emit(DVE, e)

            @block.gpsimd
            def _(e):
                emit(POOL, e)
        self.stack.close()


def build_program(stop_after=None, dbg=(), max_ops=None):
    nc = bass.Bass("TRN2", target_bir_lowering=False)
    P = Prog(nc)
    P.max_ops = max_ops
    MARKS.clear()

    def mark(name):
        MARKS.append((name, sum(1 for o in P.ops if o.eng == PE)))

    def din(name, shape):
        return nc.dram_tensor(name, list(shape), F32, kind="ExternalInput").ap()

    def dout(name, shape):
        return nc.dram_tensor(name, list(shape), F32, kind="ExternalOutput").ap()

    x_d = din("x", [2048, D])
    ck_d = din("ck", [512, 128])
    cv_d = din("cv", [512, 128])
    s0_d = din("s0", [8, 128, 128])
    condT_d = din("condT", [128, 16])
    w_ada_d = din("w_ada", [D, 6 * D])
    b_adaT_d = din("b_adaT", [128, 48])
    nmwT_d = din("nmwT", [128, 8])
    nfwT_d = din("nfwT", [128, 8])
    fnw_d = din("fnw", [1, D])
    w_in_d = din("w_in", [D, D_IN])
    qnw_d = din("qnw", [1, 64])
    knw_d = din("knw", [1, 64])
    lblT_d = din("lblT", [128, 16])
    hnw_d = din("hnw", [128, 1])
    w_ho_d = din("w_ho", [512, D])
    w_ao_d = din("w_ao", [512, D])
    w_out_d = din("w_out", [D, D])
    w_ff1_d = din("w_ff1", [D, 4 * D])
    w_ff2_d = din("w_ff2", [4 * D, D])
    ident_d = din("ident", [128, 128])
    maskf_d = din("maskf", [128, 128])
    maskb_d = din("maskb", [128, 128])
    ropeC_d = din("ropeC", [1024, 64])
    ropeS_d = din("ropeS", [1024, 64])

    y_d = dout("y", [2048, D])
    nk_d = dout("nk", [1024, 128])
    nv_d = dout("nv", [1024, 128])
    ns_d = dout("ns", [4, 8, 128, 128])
    dbg_out = {}

    def dbg_dump(name, ap, shape, reads):
        if name in dbg:
            d = nc.dram_tensor("dbg_" + name, list(shape), ap.dtype, kind="ExternalOutput").ap()
            dbg_out[name] = d
            P.dma(d, ap, reads=reads)

    NSLOT = 6
    W = [P.sb("W%d" % i, [128, 4096], BF16) for i in range(NSLOT)]
    NSTG = 3
    STG = [P.sb("stg%d" % i, [128, 1024], F32) for i in range(NSTG)]
    hT = P.sb("hT", [128, 8, PASS_TOK], BF16)
    o_aT = P.sb("o_aT", [128, 4, PASS_TOK], BF16)
    o_hT = P.sb("o_hT", [128, 4, PASS_TOK], BF16)
    mods = P.sb("mods", [128, 48, 2], F32)
    bcA = P.sb("bcA", [128, D], F32)
    bcS = P.sb("bcS", [128, D], F32)
    vecT = P.sb("vecT", [128, 8], F32)
    ident_f = P.sb("ident_f", [128, 128], F32)
    ident_b = P.sb("ident_b", [128, 128], BF16)
    ones_f = P.sb("ones_f", [128, 128], F32)
    ones_b = P.sb("ones_b", [128, 128], BF16)
    maskf = P.sb("maskf", [128, 128], BF16)
    maskb = P.sb("maskb", [128, 128], BF16)
    mstage = P.sb("mstage", [128, 128], F32)
    mhalf = P.sb("mhalf", [128, 16], F32)
    epsc = P.sb("epsc", [128, 1], F32)
    bcG2 = P.sb("bcG2", [128, D], F32)
    smask_f = P.sb("smask_f", [128, PASS_TOK], F32)
    dg = P.sb("dg", [128, 4, 128], F32)
    condT = P.sb("condT", [128, 16], F32)
    sil = P.sb("sil", [128, 16], BF16)
    b_adaT = P.sb("b_adaT", [128, 48], F32)
    nmwT = P.sb("nmwT", [128, 8], F32)
    nfwT = P.sb("nfwT", [128, 8], F32)
    qnw_bc = P.sb("qnw_bc", [128, 64], F32)
    knw_bc = P.sb("knw_bc", [128, 64], F32)
    lblT = P.sb("lblT", [128, 16], F32)
    oml = P.sb("oml", [128, 8], F32)
    hnw = P.sb("hnw", [128, 1], F32)
    xt = [P.sb("xt%d" % i, [128, D], F32) for i in range(2)]
    junk = P.sb("junk", [128, D], BF16)
    ssq = P.sb("ssq", [128, 16], F32)
    rstd = P.sb("rstd", [128, 16], F32)
    ssq10_l = [P.sb("ssq10_%d" % i, [128, 10], F32) for i in range(2)]
    rstd10_l = [P.sb("rstd10_%d" % i, [128, 10], F32) for i in range(2)]
    decs = [P.sb("decs%d" % i, [128, 16], F32) for i in range(4)]
    htok = [P.sb("htok%d" % i, [128, D], BF16) for i in range(2)]
    ARENA_COLS = 14848 + 1024 + 3584 + 512
    arena = P.sb("arena", [128, ARENA_COLS], F32)

    class Carver:
        def __init__(self):
            self.off = 0

        def _shape(self, v, shape):
            if len(shape) == 2:
                return v
            names = ["a", "b", "c", "d"][:len(shape) - 1]
            pat = "p (%s) -> p %s" % (" ".join(names), " ".join(names))
            kw = {n: s for n, s in zip(names[1:], shape[2:])}
            return v.rearrange(pat, **kw)

        def f32(self, shape):
            n = int(np.prod(shape[1:]))
            v = arena[:, self.off:self.off + n]
            self.off += n
            assert self.off <= ARENA_COLS, self.off
            return self._shape(v, shape)

        def bf16(self, shape):
            n = int(np.prod(shape[1:]))
            nf = (n + 1) // 2
            v = arena[:, self.off:self.off + nf].bitcast(BF16)[:, 0:n]
            self.off += nf
            assert self.off <= ARENA_COLS, self.off
            return self._shape(v, shape)

    PSALL = P.ps("psall", [128, 8 * 512], F32)
    PSB = [PSALL[:, i * 512:(i + 1) * 512] for i in range(8)]
    ps_rot = {"mm": [0, 1], "mm2": [2, 3], "kv": [4, 5], "tp": [4, 5], "h1": [6, 2, 3], "h2": [7, 5], "acc": [2, 3, 6, 7], "kvr": [2, 3], "tp1": [4], "st4": [0, 4, 1, 5]}
    ps_idx = {k: 0 for k in ps_rot}

    def ps_get(group):
        lst = ps_rot[group]
        b = lst[ps_idx[group] % len(lst)]
        ps_idx[group] += 1
        return PSB[b], ("ps", b)

    cv_ = Carver()
    cv_.off = 17920
    h1_a = [cv_.f32([128, D]) for i in range(2)]
    cv_ = Carver()
    v_tok = cv_.bf16([128, NT, 512])
    qT = [cv_.bf16([128, 4, PASS_TOK]) for i in range(2)]
    kT2 = cv_.bf16([128, 2, 1536])
    v_ext = cv_.bf16([128, 12, 2, 192])
    ropeC_l = [cv_.f32([128, 64]) for i in range(2)]
    ropeS_l = [cv_.f32([128, 64]) for i in range(2)]
    sq512_l = [cv_.f32([128, 640]) for i in range(2)]
    qn_l = [cv_.f32([128, 512]) for i in range(2)]
    kn_l = [cv_.f32([128, 128]) for i in range(2)]
    vraw_l = [cv_.f32([128, 128]) for i in range(2)]
    rt1_l = [cv_.f32([128, 512]) for i in range(2)]
    rt2_l = [cv_.f32([128, 512]) for i in range(2)]
    qr_l = [cv_.bf16([128, 512]) for i in range(2)]
    kt1_l = [cv_.f32([128, 128]) for i in range(2)]
    kt2_l = [cv_.f32([128, 128]) for i in range(2)]
    kdup_l = [cv_.bf16([128, 2, 2, 64]) for i in range(2)]
    ckst = cv_.f32([128, 4, 128])
    ckdup = cv_.bf16([128, 4, 2, 128])
    pt = [cv_.bf16([128, 512]) for i in range(4)]
    recs = cv_.f32([128, 512])
    cv_ = Carver()
    _v_tok_same = cv_.bf16([128, NT, 512])
    qTh = cv_.bf16([128, PASS_TOK])
    hgTh = [cv_.bf16([128, PASS_TOK]) for i in range(2)]
    uT = cv_.f32([128, PASS_TOK])
    lf = cv_.f32([128, PASS_TOK])
    aT = cv_.f32([128, PASS_TOK])
    qd = [cv_.bf16([128, PASS_TOK]) for i in range(4)]
    kd = [cv_.bf16([128, PASS_TOK]) for i in range(4)]
    kdtok = [cv_.bf16([128, NT, 128]) for i in range(4)]
    oT = cv_.f32([128, PASS_TOK])
    osq = cv_.bf16([128, 512])
    ors = cv_.f32([128, 512])
    ot1 = cv_.f32([128, 512])
    Tst = [cv_.f32([128, 128]) for i in range(8)]
    Sall = cv_.bf16([128, 2, 16, 128])
    Sinit = [cv_.bf16([128, 128]) for i in range(2)]
    S0f = [cv_.f32([128, 128]) for i in range(2)]
    sfin_l = [cv_.f32([128, 128]) for i in range(4)]
    sfin_i = [0]
    smk = [cv_.bf16([128, 128]) for i in range(4)]
    cv_ = Carver()
    x1 = cv_.f32([128, NT, D])
    g0 = cv_.f32([128, 512])
    g1 = cv_.f32([128, 512])
    m0 = cv_.f32([128, 512])
    m1 = cv_.f32([128, 512])
    mT = cv_.bf16([128, 8, PASS_TOK])
    aTf = mT
    rl = [cv_.bf16([128, 512]) for i in range(2)]
    h1_d = [cv_.f32([128, D]) for i in range(2)]
    fnw_bc = cv_.f32([128, D])
    assert cv_.off <= 17920, cv_.off

    stg_i = [0]
    free_slots = list(range(NSLOT))

    def free_block(wkey):
        free_slots.append(wkey[1])

    pending = []

    def pump(n=1):
        for _ in range(n):
            if pending:
                pending.pop(0)()

    def flush():
        while pending:
            pending.pop(0)()

    def load_block(src3, nk, ncols, fold_bc=None, fold_key=None, defer=False, cast_eng=DVE, queue=None):
        slot = free_slots.pop(0)
        wv = W[slot][:, 0:nk * ncols].rearrange("p (k n) -> p k n", n=ncols)
        kper = max(1, 1024 // ncols)
        wkey = ("W", slot)

        def step(k0, kk):
            gi = stg_i[0] % NSTG
            stg_i[0] += 1
            sv = STG[gi][:, 0:kk * ncols].rearrange("p (k n) -> p k n", n=ncols)
            P.dma(sv, src3[:, k0:k0 + kk, :], writes=[("stg", gi)])
            if fold_bc is None:
                if cast_eng == ACT:
                    P.op(ACT, (lambda e, o=wv[:, k0:k0 + kk, :], i=sv: e.activation(out=o, in_=i, func=AF.Copy)),
                         [("stg", gi)], [wkey])
                else:
                    P.op(DVE, (lambda e, o=wv[:, k0:k0 + kk, :], i=sv: e.tensor_copy(out=o, in_=i)),
                         [("stg", gi)], [wkey])
            else:
                fb = fold_bc.unsqueeze(1).broadcast_to([128, kk, ncols])
                P.op(DVE, (lambda e, o=wv[:, k0:k0 + kk, :], i=sv, f=fb: e.tensor_tensor(out=o, in0=i, in1=f, op=ALU.mult)),
                     [("stg", gi), fold_key], [wkey])

        for k0 in range(0, nk, kper):
            kk = min(kper, nk - k0)
            if defer:
                (pending if queue is None else queue).append(lambda k0=k0, kk=kk: step(k0, kk))
            else:
                step(k0, kk)
        return wv, wkey

    def wcols(w_d, c0, ncols, r0=0, nrows=1024):
        return w_d[r0:r0 + nrows, c0:c0 + ncols].rearrange("(k p) n -> p k n", p=128)

    P.dma(ident_f[:], ident_d, writes=["ident_f"])
    P.op(DVE, lambda e: e.tensor_copy(out=ident_b[:], in_=ident_f[:]), ["ident_f"], ["ident_b"])
    P.op(POOL, lambda e: e.memset(ones_f[:], 1.0), [], ["ones_f"])
    P.op(POOL, lambda e: e.memset(ones_b[:], 1.0), [], ["ones_b"])
    P.op(POOL, lambda e: e.memset(mhalf[:], -0.5), [], ["mhalf"])
    P.op(POOL, lambda e: e.memset(epsc[:], EPS), [], ["epsc"])
    P.dma(mstage[:], maskf_d, writes=["mstage"])
    P.op(DVE, lambda e: e.tensor_copy(out=maskf[:], in_=mstage[:]), ["mstage"], ["maskf"])
    P.dma(mstage[:], maskb_d, reads=[], writes=["mstage"])
    P.op(DVE, lambda e: e.tensor_copy(out=maskb[:], in_=mstage[:]), ["mstage"], ["maskb"])
    P.op(POOL, lambda e: e.memset(smask_f[:], 1.0), [], ["smask_f"])
    P.op(POOL, lambda e: e.memset(smask_f[:].rearrange("p (c t) -> p c t", t=CH)[:, :, 0:1], 0.0), ["smask_f"], ["smask_f"])
    P.dma(condT[:], condT_d, writes=["condT"])
    P.dma(b_adaT[:], b_adaT_d, writes=["b_adaT"])
    P.dma(nmwT[:], nmwT_d, writes=["nmwT"])
    P.dma(nfwT[:], nfwT_d, writes=["nfwT"])
    P.dma(qnw_bc[:], qnw_d.partition_broadcast(128), writes=["qnw_bc"])
    P.dma(knw_bc[:], knw_d.partition_broadcast(128), writes=["knw_bc"])
    P.dma(lblT[:], lblT_d, writes=["lblT"])
    P.dma(hnw[:], hnw_d, writes=["hnw"])
    lv = lblT[:].rearrange("p (d l h) -> p d l h", d=2, l=2)
    P.op(DVE, lambda e: e.tensor_tensor(out=oml[:].rearrange("p (d h) -> p d h", d=2), in0=lv[:, :, 1, :], in1=lv[:, :, 0, :], op=ALU.subtract),
         ["lblT"], ["oml"])
    P.op(ACT, lambda e: e.activation(out=oml[:], in_=oml[:], func=AF.Sigmoid), ["oml"], ["oml"])

    P.op(ACT, lambda e: e.activation(out=sil[:], in_=condT[:], func=AF.Silu), ["condT"], ["sil"])
    silv = sil[:].rearrange("p (k j) -> p k j", j=2)
    pM, pMk = ps_get("h2")

    def ada_block(blk, pM_, pMk_, first_chunk, slack=0):
        q_ = []
        wv, wkey = load_block(wcols(w_ada_d, blk * 512, 512), 8, 512, defer=True, queue=q_)
        for st_ in q_:
            st_()
            yield None
        for _ in range(slack):
            yield None
        for cc in range(4):
            chunk = blk * 4 + cc
            for kc in range(8):
                P.op(PE, (lambda e, o=pM_[:, chunk * 2:chunk * 2 + 2], l=wv[:, kc, cc * 128:(cc + 1) * 128], r=silv[:, kc, :], st=(kc == 0 and chunk == first_chunk), sp=(kc == 7):
                          e.matmul(o, lhsT=l, rhs=r, start=st, stop=sp, skip_group_check=True)),
                     [wkey, "sil"], [pMk_])
            yield None
        free_block(wkey)

    for blk in range(4):
        for _ in ada_block(blk, pM, pMk, 0):
            pass
    P.op(DVE, lambda e: e.tensor_tensor(out=mods[:, 0:16, :], in0=pM[:, 0:32].rearrange("p (c j) -> p c j", j=2),
                                        in1=b_adaT[:, 0:16].unsqueeze(2).broadcast_to([128, 16, 2]), op=ALU.add),
         [pMk, "b_adaT"], ["mods_a"])
    pM2, pM2k = PSB[6], ("ps", 6)

    def ada2_gen():
        for blk in range(4, 12):
            for _ in ada_block(blk, pM2, pM2k, 16, slack=4):
                yield None
        P.op(DVE, lambda e: e.tensor_tensor(out=mods[:, 16:48, :], in0=pM2[:, 32:96].rearrange("p (c j) -> p c j", j=2),
                                            in1=b_adaT[:, 16:48].unsqueeze(2).broadcast_to([128, 32, 2]), op=ALU.add),
             [pM2k, "b_adaT"], ["mods_b"])

    ada2 = ada2_gen()
    if "mods" in dbg:
        for _ in ada2:
            pass
    dbg_dump("mods", mods[:].rearrange("p c j -> p (c j)"), [128, 96], ["mods_a", "mods_b"])

    def expand(vec_ap, vec_keys, dst, dst_key):
        for half in range(2):
            pE, pEk = ps_get("h2")
            for c4 in range(4):
                fc = half * 4 + c4
                P.op(DVE, (lambda e, o=dg[:, c4, :], s=vec_ap[:, fc:fc + 1]: e.tensor_scalar(out=o, in0=ident_f[:], scalar1=s, scalar2=None, op0=ALU.mult)),
                     ["ident_f"] + list(vec_keys), [("dg", c4)])
                P.op(PE, (lambda e, o=pE[:, c4 * 128:(c4 + 1) * 128], r=dg[:, c4, :]: e.matmul(o, lhsT=ones_f[:], rhs=r, start=True, stop=True)),
                     ["ones_f", ("dg", c4)], [pEk])
            P.op(ACT, (lambda e, o=dst[:, half * 512:(half + 1) * 512], i=pE[:]: e.activation(out=o, in_=i, func=AF.Copy)),
                 [pEk], [dst_key])

    mark("adaln end")
    if stop_after == "adaln":
        P.finalize()
        return nc, dbg_out

    def pipeline_gen(make_gen, n, gap):
        active = []
        started = 0
        rnd = 0
        while started < n or active:
            if started < n and rnd % gap == 0:
                active.append(make_gen(started))
                started += 1
            for g_ in list(active):
                try:
                    next(g_)
                except StopIteration:
                    active.remove(g_)
            rnd += 1
            yield None

    def pipeline(make_gen, n, gap):
        for _ in pipeline_gen(make_gen, n, gap):
            pass

    def norm_mod_transpose(t, src_ap, src_keys, h1l, pre=None):
        p = t % 2
        h1 = h1l[p]
        if pre is not None:
            pre()
        P.op(ACT, (lambda e, i=src_ap: e.activation(out=junk[:], in_=i, func=AF.Square, accum_out=ssq[:, p:p + 1])), src_keys, ["junk", ("ssq", p)])
        yield None
        P.op(POOL, lambda e: e.tensor_scalar(out=rstd[:, 2 + p:3 + p], in0=ssq[:, p:p + 1], scalar1=1.0 / D, scalar2=EPS, op0=ALU.mult, op1=ALU.add), [("ssq", p)], [("rstd1", p)])
        P.op(POOL, lambda e: e.tensor_tensor(out=rstd[:, p:p + 1], in0=rstd[:, 2 + p:3 + p], in1=mhalf[:, 0:1], op=ALU.pow), [("rstd1", p), "mhalf"], [("rstd", p)])
        P.op(DVE, (lambda e, i=src_ap: e.scalar_tensor_tensor(out=h1, in0=i, scalar=rstd[:, p:p + 1], in1=bcA[:], op0=ALU.mult, op1=ALU.mult)),
             list(src_keys) + [("rstd", p), "bcA"], [("h1", p)])
        yield None
        hb = htok[p]
        hk = ("htok", p)
        P.op(DVE, (lambda e, o=hb[:]: e.tensor_tensor(out=o, in0=h1, in1=bcS[:], op=ALU.add)), [("h1", p), "bcS"], [hk])
        yield None
        pT, pTk = ps_get("tp")
        pTb = pT.bitcast(BF16)
        for kc in range(8):
            P.op(PE, (lambda e, o=pTb[:, kc * 128:(kc + 1) * 128], i=hb[:, kc * 128:(kc + 1) * 128]: e.transpose(o, i, ident_b[:])),
                 [hk, "ident_b"], [pTk])
        P.op(ACT, (lambda e, o=hT[:, :, t * 128:(t + 1) * 128], i=pTb[:, 0:1024].rearrange("p (k n) -> p k n", n=128): e.activation(out=o, in_=i, func=AF.Copy)),
             [pTk], [("hT", t)])

    def rms_rstd_only(src_ap, src_keys):
        P.op(ACT, (lambda e, i=src_ap: e.activation(out=junk[:], in_=i, func=AF.Square, accum_out=ssq[:, 8:9])), src_keys, ["junk", "ssqF"])
        P.op(POOL, lambda e: e.tensor_scalar(out=rstd[:, 9:10], in0=ssq[:, 8:9], scalar1=1.0 / D, scalar2=EPS, op0=ALU.mult, op1=ALU.add), ["ssqF"], ["rstd1F"])
        P.op(POOL, lambda e: e.tensor_tensor(out=rstd[:, 8:9], in0=rstd[:, 9:10], in1=mhalf[:, 0:1], op=ALU.pow), ["rstd1F", "mhalf"], ["rstdF"])

    hT_all = [("hT", t) for t in range(NT)]
    a0_res = {}

    def a0_stage_gen(pid):
        jj = pid
        r0 = pid * PASS_TOK
        P.op(DVE, (lambda e: e.scalar_tensor_tensor(out=vecT[:], in0=mods[:, 8:16, jj], scalar=1.0, in1=nmwT[:], op0=ALU.add, op1=ALU.mult)),
             ["mods_a", "nmwT"], ["vecT"])
        expand(vecT, ["vecT"], bcA, "bcA")
        expand(mods[:, 0:8, jj], ["mods_a"], bcS, "bcS")
        yield None
        a0_res[pid] = (load_block(wcols(w_in_d, 2560, 512), 8, 512, defer=True), load_block(wcols(w_in_d, 3072, 256), 8, 256, defer=True),
                       load_block(wcols(w_in_d, 1536, 512), 8, 512, defer=True))

        def a0_gen(t):
            xb = xt[t % 2]
            xk = ("xt", t % 2)
            return norm_mod_transpose(t, xb[:], [xk], h1_a,
                                      pre=lambda: P.dma(xb[:], x_d[r0 + t * 128:r0 + (t + 1) * 128, :], writes=[xk]))
        for _ in pipeline_gen(a0_gen, NT, 2):
            pump(1)
            yield None
        flush()

    a0_next = None

    for ps_id in range(2):
        latent = (ps_id == 0)
        j = ps_id
        row0 = ps_id * PASS_TOK
        seqs = [(0, 8)] if latent else [(0, 2), (2, 2), (4, 2), (6, 2)]
        koff = 512 if latent else 0
        ktoff = 4 if latent else 0

        if a0_next is None:
            a0_next = a0_stage_gen(ps_id)
        for _ in a0_next:
            pass
        a0_next = None
        (waq, waqk), (wakv, wakvk), (whi, whik) = a0_res[ps_id]
        if ps_id == 1:
            dbg_dump("hT1", hT[:].rearrange("p k n -> p (k n)"), [128, 8 * PASS_TOK], hT_all)
        P.barrier()
        P.op(POOL, lambda e: e.memset(v_ext[:], 1.0), [], ["v_ext_all"])
        P.op(POOL, lambda e: e.memset(qT[0][64:128], 0.0), [], ["qT_zero0"])
        P.op(POOL, lambda e: e.memset(qT[1][0:64], 0.0), [], ["qT_zero1"])
        if ps_id == 0:
            dbg_dump("hT", hT[:].rearrange("p k n -> p (k n)"), [128, 8 * PASS_TOK], hT_all)
        mark("p%d A0 end" % ps_id)
        if stop_after == "A0":
            break

        if latent:
            P.dma(ckst[:], ck_d.rearrange("(t p) c -> p t c", p=128), writes=["ckst"])
            ckv = ckst.rearrange("p t (g d) -> p t g d", g=2)
            for dup in range(2):
                P.op(DVE, (lambda e, o=ckdup[:, :, :, dup * 64:(dup + 1) * 64]: e.tensor_copy(out=o, in_=ckv)), ["ckst"], [("ckdup", dup)])
            pT, pTk = ps_get("tp")
            pTb = pT.bitcast(BF16)
            for kt in range(4):
                for g in range(2):
                    P.op(PE, (lambda e, o=pTb[:, (kt * 2 + g) * 128:(kt * 2 + g + 1) * 128], i=ckdup[:, kt, g, :]: e.transpose(o, i, ident_b[:])),
                         [("ckdup", 0), ("ckdup", 1), "ident_b"], [pTk])
            P.op(ACT, (lambda e, o=kT2[:, :, 0:512].rearrange("p g (t n) -> p g t n", n=128),
                       i=pTb[:, 0:1024].rearrange("p (t g n) -> p g t n", g=2, n=128): e.activation(out=o, in_=i, func=AF.Copy)),
                 [pTk], [("kT2", kt) for kt in range(4)])
            P.dma(ckst[:], cv_d.rearrange("(t p) c -> p t c", p=128), writes=["ckst"])
            P.op(DVE, lambda e: e.tensor_copy(out=v_ext[:, 0:4, :, 64:128], in_=ckst.rearrange("p t (g d) -> p t g d", g=2)),
                 ["ckst", "v_ext_all"], [("v_ext", kt) for kt in range(4)])
        def aproj_tile(t):
            hk = ("hT", t)
            tp_ = t % 2
            ropeC, ropeS, sq512, qn, kn, vraw, rt1, rt2, qr, kt1, kt2, kdup = (ropeC_l[tp_], ropeS_l[tp_], sq512_l[tp_], qn_l[tp_], kn_l[tp_], vraw_l[tp_],
                                                                               rt1_l[tp_], rt2_l[tp_], qr_l[tp_], kt1_l[tp_], kt2_l[tp_], kdup_l[tp_])
            ssq10, rstd10 = ssq10_l[tp_], rstd10_l[tp_]
            pq, pqk = ps_get("mm")
            for kc in range(8):
                P.op(PE, (lambda e, p_=pq, l=hT[:, kc, t * 128:(t + 1) * 128], r=waq[:, kc, :], st=(kc == 0), sp=(kc == 7): e.matmul(p_[:], lhsT=l, rhs=r, start=st, stop=sp)),
                     [hk, waqk], [pqk])
            pkv, pkvk = ps_get("mm2")
            for kc in range(8):
                P.op(PE, (lambda e, p_=pkv, l=hT[:, kc, t * 128:(t + 1) * 128], r=wakv[:, kc, :], st=(kc == 0), sp=(kc == 7): e.matmul(p_[:, 0:256], lhsT=l, rhs=r, start=st, stop=sp)),
                     [hk, wakvk], [pkvk])
            pv, pvk = ps_get("h2")
            for kc in range(8):
                P.op(PE, (lambda e, p_=pv, l=hT[:, kc, t * 128:(t + 1) * 128], r=whi[:, kc, :], st=(kc == 0), sp=(kc == 7): e.matmul(p_[:], lhsT=l, rhs=r, start=st, stop=sp)),
                     [hk, whik], [pvk])
            P.op(ACT, (lambda e, o=v_tok[:, t, :], i=pv[:]: e.activation(out=o, in_=i, func=AF.Copy)), [pvk], [("v_tok", t)])
            yield None
            P.op(ACT, (lambda e, p_=pq: e.activation(out=sq512[:, 0:512], in_=p_[:], func=AF.Square)), [pqk], [("sq512a", tp_)])
            P.op(ACT, (lambda e, p_=pkv: e.activation(out=sq512[:, 512:640], in_=p_[:, 0:128], func=AF.Square)), [pkvk], [("sq512b", tp_)])
            P.op(DVE, lambda e: e.tensor_reduce(out=ssq10[:], in_=sq512.rearrange("p (g d) -> p g d", d=64), axis=AX.X, op=ALU.add),
                 [("sq512a", tp_), ("sq512b", tp_)], [("ssq10", tp_)])
            yield None
            P.op(POOL, lambda e: e.tensor_scalar(out=ssq10[:], in0=ssq10[:], scalar1=1.0 / 64, scalar2=EPS, op0=ALU.mult, op1=ALU.add), [("ssq10", tp_)], [("ssq10", tp_)])
            P.op(POOL, lambda e: e.tensor_tensor(out=rstd10[:], in0=ssq10[:], in1=mhalf[:, 0:10], op=ALU.pow), [("ssq10", tp_), "mhalf"], [("rstd10", tp_)])
            yield None
            qn3 = qn.rearrange("p (g d) -> p g d", d=64)
            kn3 = kn.rearrange("p (g d) -> p g d", d=64)
            P.op(DVE, (lambda e, p_=pq: e.tensor_tensor(out=qn3, in0=p_[:].rearrange("p (g d) -> p g d", d=64),
                                                        in1=rstd10[:, 0:8].unsqueeze(2).broadcast_to([128, 8, 64]), op=ALU.mult)), [pqk, ("rstd10", tp_)], [("qn", tp_)])
            P.op(DVE, lambda e: e.tensor_tensor(out=qn3, in0=qn3, in1=qnw_bc[:].unsqueeze(1).broadcast_to([128, 8, 64]), op=ALU.mult), [("qn", tp_), "qnw_bc"], [("qn", tp_)])
            P.op(DVE, (lambda e, p_=pkv: e.tensor_tensor(out=kn3, in0=p_[:, 0:128].rearrange("p (g d) -> p g d", d=64),
                                                         in1=rstd10[:, 8:10].unsqueeze(2).broadcast_to([128, 2, 64]), op=ALU.mult)), [pkvk, ("rstd10", tp_)], [("kn", tp_)])
            P.op(DVE, lambda e: e.tensor_tensor(out=kn3, in0=kn3, in1=knw_bc[:].unsqueeze(1).broadcast_to([128, 2, 64]), op=ALU.mult), [("kn", tp_), "knw_bc"], [("kn", tp_)])
            kt_i = t + ktoff
            P.op(ACT, (lambda e, p_=pkv, o=v_ext[:, kt_i, :, 64:128]: e.activation(out=o, in_=p_[:, 128:256].rearrange("p (g d) -> p g d", g=2), func=AF.Copy)),
                 [pkvk, "v_ext_all"], [("v_ext", kt_i)])
            yield None
            if not latent:
                P.op(ACT, (lambda e, p_=pkv: e.activation(out=vraw, in_=p_[:, 128:256], func=AF.Copy)), [pkvk], [("vraw", tp_)])
                P.dma(nk_d[t * 128:(t + 1) * 128, :], kn, reads=[("kn", tp_)], eng=POOL)
                P.dma(nv_d[t * 128:(t + 1) * 128, :], vraw, reads=[("vraw", tp_)], eng=POOL)
                P.op(DVE, lambda e: e.tensor_copy(out=qr, in_=qn), [("qn", tp_)], [("qr", tp_)])
                for dup in range(2):
                    P.op(ACT, (lambda e, o=kdup[:, :, dup, :]: e.activation(out=o, in_=kn.rearrange("p (g d) -> p g d", g=2), func=AF.Copy)), [("kn", tp_)], [("kdup", dup, tp_)])
            else:
                P.dma(ropeC, ropeC_d[t * 128:(t + 1) * 128, :], writes=[("ropeC", tp_)])
                P.dma(ropeS, ropeS_d[t * 128:(t + 1) * 128, :], writes=[("ropeS", tp_)])
                q5 = qn.rearrange("p (g r h f) -> p g r h f", g=8, r=2, h=2)
                t25 = rt2.rearrange("p (g r h f) -> p g r h f", g=8, r=2, h=2)
                S4 = ropeS.rearrange("p (r h f) -> p r h f", r=2, h=2)
                P.op(DVE, lambda e: e.tensor_tensor(out=rt1.rearrange("p (g d) -> p g d", d=64), in0=qn3,
                                                    in1=ropeC.unsqueeze(1).broadcast_to([128, 8, 64]), op=ALU.mult), [("qn", tp_), ("ropeC", tp_)], [("rt1", tp_)])
                for hh in range(2):
                    P.op(DVE, (lambda e, o=t25[:, :, :, hh, :], i=q5[:, :, :, 1 - hh, :], s=S4[:, :, hh, :].unsqueeze(1).broadcast_to([128, 8, 2, 16]):
                               e.tensor_tensor(out=o, in0=i, in1=s, op=ALU.mult)), [("qn", tp_), ("ropeS", tp_)], [("rt2", hh, tp_)])
                P.op(DVE, lambda e: e.tensor_tensor(out=qr, in0=rt1, in1=rt2, op=ALU.add), [("rt1", tp_), ("rt2", 0, tp_), ("rt2", 1, tp_)], [("qr", tp_)])
                k5 = kn.rearrange("p (g r h f) -> p g r h f", g=2, r=2, h=2)
                k25 = kt2.rearrange("p (g r h f) -> p g r h f", g=2, r=2, h=2)
                P.op(DVE, lambda e: e.tensor_tensor(out=kt1.rearrange("p (g d) -> p g d", d=64), in0=kn3,
                                                     in1=ropeC.unsqueeze(1).broadcast_to([128, 2, 64]), op=ALU.mult), [("kn", tp_), ("ropeC", tp_)], [("kt1", tp_)])
                for hh in range(2):
                    P.op(DVE, (lambda e, o=k25[:, :, :, hh, :], i=k5[:, :, :, 1 - hh, :], s=S4[:, :, hh, :].unsqueeze(1).broadcast_to([128, 2, 2, 16]):
                                e.tensor_tensor(out=o, in0=i, in1=s, op=ALU.mult)), [("kn", tp_), ("ropeS", tp_)], [("kt2", hh, tp_)])
                for dup in range(2):
                    P.op(DVE, (lambda e, o=kdup[:, :, dup, :]: e.tensor_tensor(out=o, in0=kt1.rearrange("p (g d) -> p g d", g=2),
                                                                                in1=kt2.rearrange("p (g d) -> p g d", g=2), op=ALU.add)),
                         [("kt1", tp_), ("kt2", 0, tp_), ("kt2", 1, tp_)], [("kdup", dup, tp_)])
            yield None
            pT, pTk = ps_get("tp")
            pTb = pT.bitcast(BF16)
            for jp in range(4):
                P.op(PE, (lambda e, o=pTb[:, jp * 128:(jp + 1) * 128], i=qr[:, jp * 128:(jp + 1) * 128]: e.transpose(o, i, ident_b[:])), [("qr", tp_), "ident_b"], [pTk])
            for g in range(2):
                P.op(PE, (lambda e, o=pTb[:, (4 + g) * 128:(5 + g) * 128], i=kdup[:, g, :, :].rearrange("p a d -> p (a d)"): e.transpose(o, i, ident_b[:])),
                     [("kdup", 0, tp_), ("kdup", 1, tp_), "ident_b"], [pTk])
            P.op(ACT, (lambda e, o=qT[0][0:64, :, t * 128:(t + 1) * 128], i=pTb[0:64, 0:512].rearrange("p (k n) -> p k n", n=128): e.activation(out=o, in_=i, func=AF.Copy)),
                 [pTk, "qT_zero0"], [("qT", t, 0)])
            P.op(ACT, (lambda e, o=qT[1][64:128, :, t * 128:(t + 1) * 128], i=pTb[64:128, 0:512].rearrange("p (k n) -> p k n", n=128): e.activation(out=o, in_=i, func=AF.Copy)),
                 [pTk, "qT_zero1"], [("qT", t, 1)])
            P.op(DVE, (lambda e, o=kT2[:, :, koff + t * 128:koff + (t + 1) * 128], i=pTb[:, 512:768].rearrange("p (k n) -> p k n", n=128): e.tensor_copy(out=o, in_=i)),
                 [pTk], [("kT2", kt_i)])
        pipeline(aproj_tile, NT, 3)
        if ps_id == 0:
            dbg_dump("qT", qT[0].rearrange("p k n -> p (k n)"), [128, 4 * PASS_TOK], [("qT", t, 0) for t in range(NT)])
            dbg_dump("kT2", kT2.rearrange("p k n -> p (k n)"), [128, 2 * 1536], [("kT2", t) for t in range(12)])
        mark("p%d Aproj end" % ps_id)
        if stop_after == "Aproj":
            break
        free_block(waqk)
        free_block(wakvk)
        free_block(whik)
        whq, whqk = load_block(wcols(w_in_d, 0, 512), 8, 512, defer=True)
        whf = [None, None]
        whfk = [None, None]
        whf[0], whfk[0] = load_block(wcols(w_in_d, 512, 512), 8, 512, defer=True)
        whf[1], whfk[1] = load_block(wcols(w_in_d, 1024, 512), 8, 512, defer=True)
        whg, whgk = load_block(wcols(w_in_d, 2048, 512), 8, 512, defer=True)
        wgl = [None] * 4
        wglk = [None] * 4

        pti = 0
        stpair = [0]
        groups = []
        for (t0, ntl) in seqs:
            ktiles = (list(range(0, 4)) if latent else []) + [t0 + i + ktoff for i in range(ntl)]
            for g in range(2):
                for tq in range(t0, t0 + ntl):
                    groups.append((g, tq, ktiles))
        asteps = [(gi, si) for gi, (g, tq, kt_) in enumerate(groups) for si in range(len(kt_))]

        def emit_scores(step):
            gi, si = step
            g, tq, ktiles = groups[gi]
            s = ktiles[si]
            nonlocal_pti = pti_box[0]
            pb = pt[nonlocal_pti % 4]
            pbk = ("pt", nonlocal_pti % 4)
            pti_box[0] += 1
            kcol = slice(s * 128, (s + 1) * 128)
            st, stk = ps_get("st4")
            stpair[0] += 1
            for par in range(2):
                P.op(PE, (lambda e, o=st[:, par * 256:(par + 1) * 256], l=kT2[:, g, kcol], r=qT[par][:, 2 * g:2 * g + 2, tq * 128:(tq + 1) * 128]:
                          e.matmul(o, lhsT=l, rhs=r, start=True, stop=True, skip_group_check=True)),
                     [("kT2", s), ("qT", tq, par)], [stk])
            P.op(ACT, (lambda e, o=pb[:, 0:512], i=st[:, 0:512]: e.activation(out=o, in_=i, func=AF.Exp, scale=0.125)),
                 [stk], [(pbk, 0), (pbk, 1)])
            return pb, pbk

        pti_box = [0]
        LOOK = 3
        inflight = [emit_scores(asteps[k]) for k in range(min(LOOK, len(asteps)))]
        acc = acck = None
        for i, (gi, si) in enumerate(asteps):
            g, tq, ktiles = groups[gi]
            nkt = len(ktiles)
            s = ktiles[si]
            if si == 0:
                pump(2)
                acc, acck = ps_get("mm2")
            if i % 2 == 1:
                next(ada2, None)
            if i + LOOK < len(asteps):
                inflight.append(emit_scores(asteps[i + LOOK]))
            pb, pbk = inflight.pop(0)
            P.op(PE, (lambda e, a=acc, l=v_ext[:, s, g, 64:192], r=pb[:, 0:256], st_=(si == 0), sp=(si == nkt - 1):
                      e.matmul(a[:, 0:256], lhsT=l, rhs=r, start=st_, stop=sp, skip_group_check=True)),
                 [("v_ext", s), "v_ext_all", (pbk, 0)], [acck])
            P.op(PE, (lambda e, a=acc, l=v_ext[:, s, g, 0:128], r=pb[:, 256:512], sp=(si == nkt - 1):
                      e.matmul(a[:, 256:512], lhsT=l, rhs=r, start=False, stop=sp, skip_group_check=True)),
                 [("v_ext", s), "v_ext_all", (pbk, 1)], [acck])
            if si == nkt - 1:
                tqs = slice(tq * 128, (tq + 1) * 128)
                if latent:
                    P.op(DVE, lambda e, a=acc: e.reciprocal(out=recs[64:128, 0:256], in_=a[64:128, 0:256]), [acck], ["recs_a"])
                else:
                    P.op(ACT, lambda e, a=acc: e.activation(out=recs[64:128, 0:256], in_=a[64:128, 0:256], func=AF.Ln), [acck], ["recs_a"])
                    P.op(ACT, lambda e: e.activation(out=recs[64:128, 0:256], in_=recs[64:128, 0:256], func=AF.Exp, scale=-1.0), ["recs_a"], ["recs_a"])
                P.op(DVE, lambda e, a=acc: e.reciprocal(out=recs[0:64, 256:512], in_=a[0:64, 256:512]), [acck], ["recs_b"])
                P.op(DVE, (lambda e, a=acc, o=o_aT[0:64, 2 * g:2 * g + 2, tqs]: e.tensor_tensor(out=o, in0=a[0:64, 0:256].rearrange("p (k n) -> p k n", n=128),
                                                                                             in1=recs[64:128, 0:256].rearrange("p (k n) -> p k n", n=128), op=ALU.mult)),
                     [acck, "recs_a"], [("o_aT", tq, g, 0)])
                P.op(DVE, (lambda e, a=acc, o=o_aT[64:128, 2 * g:2 * g + 2, tqs]: e.tensor_tensor(out=o, in0=a[64:128, 256:512].rearrange("p (k n) -> p k n", n=128),
                                                                                               in1=recs[0:64, 256:512].rearrange("p (k n) -> p k n", n=128), op=ALU.mult)),
                     [acck, "recs_b"], [("o_aT", tq, g, 1)])
        flush()
        for _ in ada2:
            pass
        o_aT_keys = [("o_aT", tq, g, p_) for tq in range(NT) for g in range(2) for p_ in range(2)]
        if ps_id == 0:
            dbg_dump("o_aT", o_aT[:].rearrange("p k n -> p (k n)"), [128, 4 * PASS_TOK], o_aT_keys)
        mark("p%d attn end" % ps_id)
        if stop_after == "attn":
            break

        P.barrier()
        wgl[0], wglk[0] = load_block(wcols(w_in_d, 3328, 512), 8, 512, defer=True)
        wgl[2], wglk[2] = load_block(wcols(w_in_d, 3328 + 2 * 512, 512), 8, 512, defer=True)

        def proj_fm(wv, wk, h, half, group="mm"):
            pp, ppk = ps_get(group)
            for kc in range(8):
                P.op(PE, (lambda e, p_=pp, l=wv[:, kc, h * 128:(h + 1) * 128], r=hT[:, kc, half * 512:(half + 1) * 512], st=(kc == 0), sp=(kc == 7):
                          e.matmul(p_[:], lhsT=l, rhs=r, start=st, stop=sp)),
                     [wk] + hT_all[half * 4:half * 4 + 4], [ppk])
            return pp, ppk

        def prep_gen(h):
            hb = h % 2
            for half in range(2):
                hs = slice(half * 512, (half + 1) * 512)
                pp, ppk = proj_fm(whq, whqk, h, half)
                P.op(ACT, (lambda e, o=qTh[:, hs], i=pp[:]: e.activation(out=o, in_=i, func=AF.Copy, scale=128.0 ** -0.5)), [ppk], [("qTh", half)])
                yield None
            for d in range(2):
                for half in range(2):
                    hs = slice(half * 512, (half + 1) * 512)
                    pp, ppk = proj_fm(whf[d], whfk[d], h, half)
                    P.op(ACT, (lambda e, o=uT[:, hs], i=pp[:]: e.activation(out=o, in_=i, func=AF.Sigmoid, scale=-1.0)), [ppk], [("uT", half)])
                    yield None
                ukeys = [("uT", 0), ("uT", 1)]
                P.op(DVE, (lambda e, s=oml[:, d * 4 + h:d * 4 + h + 1]: e.tensor_scalar(out=uT, in0=uT, scalar1=s, scalar2=None, op0=ALU.mult)),
                     ukeys + ["oml"], ukeys)
                P.op(ACT, lambda e: e.activation(out=lf, in_=uT, func=AF.Ln, scale=-1.0, bias=1.0), ukeys, ["lf"])
                yield None
                if d == 0:
                    P.op(DVE, lambda e: e.tensor_tensor_scan(out=aT, data0=smask_f[:], data1=lf, initial=0.0, op0=ALU.mult, op1=ALU.add),
                         ["smask_f", "lf"], ["aT"])
                else:
                    P.op(DVE, lambda e: e.tensor_tensor_scan(out=aT[:, ::-1], data0=smask_f[:], data1=lf[:, ::-1], initial=0.0, op0=ALU.mult, op1=ALU.add),
                         ["smask_f", "lf"], ["aT"])
                yield None
                P.op(ACT, lambda e: e.activation(out=lf, in_=aT, func=AF.Exp, scale=-1.0), ["aT"], ["lf"])
                P.op(ACT, lambda e: e.activation(out=aT, in_=aT, func=AF.Exp), ["aT", "lf"], ["aT"])
                yield None
                a3 = aT.rearrange("p (c t) -> p c t", t=CH)
                dsel = (CH - 1) if d == 0 else 0
                P.op(POOL, (lambda e, o=decs[hb * 2 + d][:], i=a3[:, :, dsel]: e.tensor_copy(out=o, in_=i)), ["aT"], [("decs", hb, d)])
                P.op(DVE, (lambda e, o=qd[hb * 2 + d]: e.tensor_tensor(out=o, in0=qTh, in1=aT, op=ALU.mult)), [("qTh", 0), ("qTh", 1), "aT"], [("qd", hb, d)])
                yield None
                P.op(DVE, (lambda e, o=kd[hb * 2 + d]: e.tensor_tensor(out=o, in0=uT, in1=lf, op=ALU.mult)), ukeys + ["lf"], [("kd", hb, d)])
                yield None
                pT, pTk = ps_get("tp1")
                pTb = pT.bitcast(BF16)
                for t in range(NT):
                    P.op(PE, (lambda e, o=pTb[:, t * 128:(t + 1) * 128], i=kd[hb * 2 + d][:, t * 128:(t + 1) * 128]: e.transpose(o, i, ident_b[:])), [("kd", hb, d), "ident_b"], [pTk])
                P.op(ACT, (lambda e, o=kdtok[hb * 2 + d], i=pTb[:, 0:1024].rearrange("p (k n) -> p k n", n=128): e.activation(out=o, in_=i, func=AF.Copy)), [pTk], [("kdtok", hb, d)])
                yield None
            assert hg_owner.get(hb) in (None, h - 2) and (h < 2 or norm_done.get(h - 2)), (h, hg_owner, norm_done)
            hg_owner[hb] = h
            for half in range(2):
                hs = slice(half * 512, (half + 1) * 512)
                pp, ppk = proj_fm(whg, whgk, h, half)
                P.op(ACT, (lambda e, o=hgTh[hb][:, hs], i=pp[:]: e.activation(out=o, in_=i, func=AF.Silu)), [ppk], [("hgTh", hb, half)])
                yield None

        def scan_gen(h, pre_norm=None):
            hb = h % 2
            nround = [0]

            def state_chain(d, si_, t0, ntl, slot):
                Tl = Tst[2 * slot:2 * slot + 2]
                dk_ = ("decs", hb, d)
                chunks = list(range(t0 * 2, (t0 + ntl) * 2))
                if d == 1:
                    chunks = chunks[::-1]
                Sin = Sinit[d]
                Sink = ("Sinit", d)
                S0 = S0f[d]
                S0k = ("S0f", d)
                if latent:
                    P.dma(S0, s0_d[d * 4 + h, :, :], writes=[S0k])
                    P.op(ACT, (lambda e, o=Sin, i=S0: e.activation(out=o, in_=i, func=AF.Copy)), [S0k], [Sink])
                tcur = None
                tcurk = None
                prev_dec = None
                for i, c in enumerate(chunks):
                    t, n = c // 2, c % 2
                    pk, pkk = ps_get("kvr")
                    P.op(PE, (lambda e, p_=pk, l=kdtok[hb * 2 + d][n * CH:(n + 1) * CH, t, :], r=v_tok[n * CH:(n + 1) * CH, t, h * 128:(h + 1) * 128]:
                              e.matmul(p_[:, 0:128], lhsT=l, rhs=r, start=True, stop=True)),
                         [("kdtok", hb, d), ("v_tok", t)], [pkk])
                    dec = decs[hb * 2 + d][:, c:c + 1]
                    tn = Tl[i % 2]
                    tnk = ("Tst", slot, i % 2)
                    if i == 0:
                        if latent:
                            P.op(DVE, (lambda e, o=tn, p_=pk, i_=S0: e.tensor_tensor(out=o, in0=p_[:, 0:128], in1=i_, op=ALU.add)), [pkk, S0k], [tnk])
                        else:
                            P.op(DVE, (lambda e, o=tn, p_=pk: e.tensor_copy(out=o, in_=p_[:, 0:128])), [pkk], [tnk])
                    else:
                        P.op(DVE, (lambda e, o=tn, i_=tcur, s=prev_dec, p_=pk: e.scalar_tensor_tensor(out=o, in0=i_, scalar=s, in1=p_[:, 0:128], op0=ALU.mult, op1=ALU.add)),
                             [tcurk, dk_, pkk], [tnk])
                    tcur, tcurk, prev_dec = tn, tnk, dec
                    if i % 2 == 0:
                        P.op(ACT, (lambda e, o=Sall[:, d, c, :], i_=tcur, s=dec: e.activation(out=o, in_=i_, func=AF.Copy, scale=s)), [tcurk, dk_], [("Sall", d, c)])
                    else:
                        P.op(DVE, (lambda e, o=Sall[:, d, c, :], i_=tcur, s=dec: e.tensor_scalar(out=o, in0=i_, scalar1=s, scalar2=None, op0=ALU.mult)), [tcurk, dk_], [("Sall", d, c)])
                    yield None
                if not latent:
                    sfin = sfin_l[sfin_i[0] % 4]
                    sfk = ("sfin", sfin_i[0] % 4)
                    sfin_i[0] += 1
                    P.op(DVE, (lambda e, o=sfin, i_=tcur, s=prev_dec: e.tensor_scalar(out=o, in0=i_, scalar1=s, scalar2=None, op0=ALU.mult)), [tcurk, dk_], [sfk])
                    P.dma(ns_d[si_, d * 4 + h, :, :], sfin, reads=[sfk], eng=POOL)

            seq_groups = [seqs] if latent else [seqs[0:2], seqs[2:4]]
            sbase = 0
            for grp in seq_groups:
                gens = []
                for gi_, (t0, ntl) in enumerate(grp):
                    for d in range(2):
                        gens.append(state_chain(d, sbase + gi_, t0, ntl, gi_ * 2 + d))
                sbase += len(grp)
                active = list(gens)
                while active:
                    for gi2, gch in enumerate(list(active)):
                        try:
                            next(gch)
                        except StopIteration:
                            active.remove(gch)
                        if gi2 == 1 and len(gens) > 2:
                            yield None
                    pump(1)
                    nround[0] += 1
                    if nround[0] == 2 and pre_norm is not None:
                        pre_norm()
                    yield None

            steps = [(t0, ntl, t, d) for (t0, ntl) in seqs for t in range(t0, t0 + ntl) for d in range(2)]
            smi = [0]

            def emit_scores(step):
                t0, ntl, t, d = step
                ts_ = slice(t * 128, (t + 1) * 128)
                mask = maskf if d == 0 else maskb
                mkey = "maskf" if d == 0 else "maskb"
                sc, sck = ps_get("h1")
                P.op(PE, (lambda e, p_=sc, l=kd[hb * 2 + d][:, ts_], r=qd[hb * 2 + d][:, ts_]: e.matmul(p_[:, 0:128], lhsT=l, rhs=r, start=True, stop=True)),
                     [("kd", hb, d), ("qd", hb, d)], [sck])
                sm = smk[smi[0] % 4]
                smkk = ("smk", smi[0] % 4)
                smi[0] += 1
                P.op(DVE, (lambda e, o=sm, p_=sc, m=mask: e.tensor_tensor(out=o, in0=p_[:, 0:128], in1=m[:], op=ALU.mult)), [sck, mkey], [smkk])
                return sm, smkk

            OLOOK = 2
            infl = [emit_scores(steps[k]) for k in range(min(OLOOK, len(steps)))]
            po = pok = None
            for i, (t0, ntl, t, d) in enumerate(steps):
                if i + OLOOK < len(steps):
                    infl.append(emit_scores(steps[i + OLOOK]))
                sm, smkk = infl.pop(0)
                ts_ = slice(t * 128, (t + 1) * 128)
                if d == 0:
                    po, pok = ps_get("h2")
                P.op(PE, (lambda e, p_=po, l=v_tok[:, t, h * 128:(h + 1) * 128], r=sm, st=(d == 0): e.matmul(p_[:, 0:128], lhsT=l, rhs=r, start=st, stop=False, skip_group_check=True)),
                     [("v_tok", t), smkk], [pok])
                for n in range(2):
                    c = t * 2 + n
                    cs = slice(c * CH, (c + 1) * CH)
                    if d == 0:
                        cprev = c - 1 if c > t0 * 2 else None
                    else:
                        cprev = c + 1 if c < (t0 + ntl) * 2 - 1 else None
                    if cprev is None:
                        if not latent:
                            continue
                        Sb, Sbk = Sinit[d], ("Sinit", d)
                    else:
                        Sb, Sbk = Sall[:, d, cprev, :], ("Sall", d, cprev)
                    P.op(PE, (lambda e, o=po[:, n * CH:(n + 1) * CH], l=Sb, r=qd[hb * 2 + d][:, cs]:
                              e.matmul(o, lhsT=l, rhs=r, start=False, stop=True, skip_group_check=True)),
                         [Sbk, ("qd", hb, d)], [pok])
                if d == 1:
                    P.op(ACT, (lambda e, o=oT[:, ts_], p_=po: e.activation(out=o, in_=p_[:, 0:128], func=AF.Copy)), [pok], [("oT", t)])
                    pump(2)
                    yield None

        def norm_head(h):
            hb = h % 2
            assert hg_owner.get(hb) == h, (h, hg_owner)
            norm_done[h] = True
            for half in range(2):
                hs = slice(half * 512, (half + 1) * 512)
                okeys = [("oT", t) for t in range(half * 4, half * 4 + 4)]
                P.op(POOL, (lambda e, i=oT[:, hs]: e.tensor_tensor(out=osq, in0=i, in1=i, op=ALU.mult)), okeys, ["osq"])
                pn, pnk = ps_get("mm")
                P.op(PE, (lambda e, p_=pn: e.matmul(p_[:], lhsT=ones_b[:], rhs=osq, start=True, stop=True)), ["ones_b", "osq"], [pnk])
                P.op(ACT, (lambda e, p_=pn: e.activation(out=ors, in_=p_[:], func=AF.Ln, scale=1.0 / 128, bias=epsc[:, 0:1])), [pnk, "epsc"], ["ors"])
                P.op(ACT, lambda e: e.activation(out=ors, in_=ors, func=AF.Exp, scale=-0.5), ["ors"], ["ors"])
                P.op(DVE, (lambda e, i=oT[:, hs]: e.scalar_tensor_tensor(out=ot1, in0=i, scalar=hnw[:, 0:1], in1=ors, op0=ALU.mult, op1=ALU.mult)),
                     okeys + ["hnw", "ors"], ["ot1"])
                P.op(DVE, (lambda e, o=o_hT[:, h, hs], g_=hgTh[hb][:, hs]: e.tensor_tensor(out=o, in0=ot1, in1=g_, op=ALU.mult)), ["ot1", ("hgTh", hb, half)], [("o_hT", h, half)])

        def drain(g):
            for _ in g:
                pass

        def interleave(ga, gb, na=1, nb=1):
            da = db = False
            while not (da and db):
                for _ in range(na):
                    if not da:
                        try:
                            next(ga)
                        except StopIteration:
                            da = True
                for _ in range(nb):
                    if not db:
                        try:
                            next(gb)
                        except StopIteration:
                            db = True

        hg_owner = {}
        norm_done = {}
        drain(prep_gen(0))
        if ps_id == 0:
            dbg_dump("qd0", qd[0], [128, PASS_TOK], [("qd", 0, 0)])
            dbg_dump("kd1", kd[1], [128, PASS_TOK], [("kd", 0, 1)])
            dbg_dump("decs1", decs[1][:], [128, 16], [("decs", 0, 1)])
        for h in range(4):
            pn_ = (lambda hh=h - 1: norm_head(hh)) if h > 0 else None
            if h < 3:
                interleave(scan_gen(h, pn_), prep_gen(h + 1), 1, 1)
            else:
                for k_ in (whqk, whfk[0], whfk[1], whgk):
                    free_block(k_)
                who, whok = load_block(w_ho_d.rearrange("(k p) n -> p k n", p=128), 4, 1024, defer=True)
                wao, waok = load_block(w_ao_d.rearrange("(k p) n -> p k n", p=128), 4, 1024, defer=True)
                wgl[1], wglk[1] = load_block(wcols(w_in_d, 3328 + 1 * 512, 512), 8, 512, defer=True)
                wgl[3], wglk[3] = load_block(wcols(w_in_d, 3328 + 3 * 512, 512), 8, 512, defer=True)
                drain(scan_gen(h, pn_))
        norm_head(3)
        o_hT_keys = [("o_hT", h, half) for h in range(4) for half in range(2)]
        if ps_id == 0:
            dbg_dump("o_hT", o_hT[:].rearrange("p k n -> p (k n)"), [128, 4 * PASS_TOK], o_hT_keys)
        mark("p%d hgrn end" % ps_id)
        if stop_after == "hgrn":
            break

        P.barrier()
        expand(mods[:, 16:24, j], ["mods_b"], bcS, "bcS")
        wo = [None, None]
        wok = [None, None]
        for f in range(8):
            if f == 4:
                flush()
                free_block(wglk[0])
                free_block(wglk[2])
                for chh in range(2):
                    wo[chh], wok[chh] = load_block(wcols(w_out_d, chh * 512, 512), 8, 512, fold_bc=bcS[:, chh * 512:(chh + 1) * 512], fold_key="bcS", defer=True)
            for half in range(2):
                hs = slice(half * 512, (half + 1) * 512)
                hkeys = hT_all[half * 4:half * 4 + 4]
                fcs = slice((f % 4) * 128, (f % 4 + 1) * 128)
                pg0, pg0k = ps_get("mm")
                for kc in range(8):
                    P.op(PE, (lambda e, p_=pg0, l=wgl[f // 4][:, kc, fcs], r=hT[:, kc, hs], st=(kc == 0), sp=(kc == 7): e.matmul(p_[:], lhsT=l, rhs=r, start=st, stop=sp)),
                         [wglk[f // 4]] + hkeys, [pg0k])
                P.op(ACT, (lambda e, p_=pg0: e.activation(out=g0, in_=p_[:], func=AF.Sigmoid)), [pg0k], ["g0"])
                pg1, pg1k = ps_get("mm")
                for kc in range(8):
                    P.op(PE, (lambda e, p_=pg1, l=wgl[2 + f // 4][:, kc, fcs], r=hT[:, kc, hs], st=(kc == 0), sp=(kc == 7): e.matmul(p_[:], lhsT=l, rhs=r, start=st, stop=sp)),
                         [wglk[2 + f // 4]] + hkeys, [pg1k])
                P.op(ACT, (lambda e, p_=pg1: e.activation(out=g1, in_=p_[:], func=AF.Sigmoid)), [pg1k], ["g1"])
                ph, phk = ps_get("mm2")
                for hh in range(4):
                    P.op(PE, (lambda e, p_=ph, l=who[:, hh, f * 128:(f + 1) * 128], r=o_hT[:, hh, hs], st=(hh == 0), sp=(hh == 3): e.matmul(p_[:], lhsT=l, rhs=r, start=st, stop=sp)),
                         [whok, ("o_hT", hh, half)], [phk])
                pa, pak = ps_get("kv")
                for pr_ in range(4):
                    P.op(PE, (lambda e, p_=pa, l=wao[:, pr_, f * 128:(f + 1) * 128], r=o_aT[:, pr_, hs], st=(pr_ == 0), sp=(pr_ == 3): e.matmul(p_[:], lhsT=l, rhs=r, start=st, stop=sp)),
                         [waok] + [k for k in o_aT_keys if k[1] // 4 == half], [pak])
                P.op(DVE, (lambda e, p_=ph: e.tensor_tensor(out=m0, in0=p_[:], in1=g0, op=ALU.mult)), [phk, "g0"], ["m0"])
                P.op(DVE, (lambda e, p_=pa: e.tensor_tensor(out=m1, in0=p_[:], in1=g1, op=ALU.mult)), [pak, "g1"], ["m1"])
                P.op(POOL, (lambda e, o=mT[:, f, hs]: e.tensor_tensor(out=o, in0=m0, in1=m1, op=ALU.add)), ["m0", "m1"], [("mT", f, half)])
                pump(1)
        flush()
        if ps_id == 0:
            dbg_dump("mT", mT.rearrange("p k n -> p (k n)"), [128, 8 * PASS_TOK], [("mT", f, half) for f in range(8) for half in range(2)])
        mark("p%d C1 end" % ps_id)
        if stop_after == "C1":
            break

        for k_ in (wglk[1], wglk[3], whok, waok):
            free_block(k_)
        expand(mods[:, 40:48, j], ["mods_b"], bcG2, "bcG2")

        def load_w1(fg):
            r = [load_block(wcols(w_ff1_d, fg * 1024 + b * 512, 512), 8, 512, defer=True, cast_eng=ACT) for b in range(2)]
            return [x[0] for x in r], [x[1] for x in r]

        def load_w2(fg):
            r = [load_block(wcols(w_ff2_d, chh * 512, 512, r0=fg * 1024), 8, 512, fold_bc=bcG2[:, chh * 512:(chh + 1) * 512], fold_key="bcG2", defer=True) for chh in range(2)]
            return [x[0] for x in r], [x[1] for x in r]

        w1_next = load_w1(0)
        w2_next = load_w2(0)
        P.op(DVE, (lambda e, j=j: e.scalar_tensor_tensor(out=vecT[:], in0=mods[:, 32:40, j], scalar=1.0, in1=nfwT[:], op0=ALU.add, op1=ALU.mult)),
             ["mods_b", "nfwT"], ["vecT"])
        expand(vecT, ["vecT"], bcA, "bcA")
        expand(mods[:, 24:32, j], ["mods_b"], bcS, "bcS")
        n2 = pipeline_gen(lambda t: norm_mod_transpose(t, x1[:, t, :], [("x1", t, 0), ("x1", t, 1)], h1_d), NT, 2)
        for t in range(NT):
            xb = xt[t % 2]
            xk = ("xt", t % 2)
            P.dma(xb[:], x_d[row0 + t * 128:row0 + (t + 1) * 128, :], writes=[xk])
            if t > 0:
                next(n2, None)
                next(n2, None)
            for chh in range(2):
                po, pok = ps_get("mm")
                for kc in range(8):
                    P.op(PE, (lambda e, p_=po, l=mT[:, kc, t * 128:(t + 1) * 128], r=wo[chh][:, kc, :], st=(kc == 0), sp=(kc == 7): e.matmul(p_[:], lhsT=l, rhs=r, start=st, stop=sp)),
                         [("mT", kc, t // 4), wok[chh]], [pok])
                P.op(DVE, (lambda e, p_=po, o=x1[:, t, chh * 512:(chh + 1) * 512], i=xb[:, chh * 512:(chh + 1) * 512]: e.tensor_tensor(out=o, in0=p_[:], in1=i, op=ALU.add)),
                     [pok, xk], [("x1", t, chh)])
                pump(1)
        flush()
        for _ in n2:
            pass
        if ps_id == 0:
            dbg_dump("x1", x1.rearrange("p k n -> p (k n)"), [128, NT * D], [("x1", t, c) for t in range(NT) for c in range(2)])
        mark("p%d C2 end" % ps_id)
        if stop_after == "C2":
            break

        free_block(wok[0])
        free_block(wok[1])
        rli = 0

        def final_norm_tile(t):
            p = t % 2
            xk2 = [("x1", t, 0), ("x1", t, 1)]
            P.op(ACT, (lambda e, i=x1[:, t, :]: e.activation(out=junk[:], in_=i, func=AF.Square, accum_out=ssq[:, 8 + p:9 + p])), xk2, ["junk", ("ssqF", p)])
            P.op(POOL, lambda e: e.tensor_scalar(out=rstd[:, 10 + 2 * p:11 + 2 * p], in0=ssq[:, 8 + p:9 + p], scalar1=1.0 / D, scalar2=EPS, op0=ALU.mult, op1=ALU.add), [("ssqF", p)], [("rstd1F", p)])
            P.op(POOL, lambda e: e.tensor_tensor(out=rstd[:, 11 + 2 * p:12 + 2 * p], in0=rstd[:, 10 + 2 * p:11 + 2 * p], in1=mhalf[:, 0:1], op=ALU.pow), [("rstd1F", p), "mhalf"], [("rstdF", p)])
            yb = h1_d[p]
            yk = ("h1", p)
            P.op(DVE, (lambda e, i=x1[:, t, :], o=yb[:]: e.scalar_tensor_tensor(out=o, in0=i, scalar=rstd[:, 11 + 2 * p:12 + 2 * p], in1=fnw_bc, op0=ALU.mult, op1=ALU.mult)),
                 xk2 + [("rstdF", p), "fnw_bc"], [yk])
            P.dma(y_d[row0 + t * 128:row0 + (t + 1) * 128, :], yb[:], reads=[yk], eng=POOL)

        for fg in range(4):
            w1, w1k = w1_next
            w2, w2k = w2_next
            if fg < 3:
                w1_next = load_w1(fg + 1)
            for ffc in range(8):
                for half in range(2):
                    hs = slice(half * 512, (half + 1) * 512)
                    pf, pfk = ps_get("mm")
                    for kc in range(8):
                        P.op(PE, (lambda e, p_=pf, l=w1[ffc // 4][:, kc, (ffc % 4) * 128:(ffc % 4 + 1) * 128], r=hT[:, kc, hs], st=(kc == 0), sp=(kc == 7):
                                  e.matmul(p_[:], lhsT=l, rhs=r, start=st, stop=sp)),
                             [w1k[ffc // 4]] + hT_all[half * 4:half * 4 + 4], [pfk])
                    rb = rl[rli % 2]
                    rbk = ("rl", rli % 2)
                    rli += 1
                    P.op(ACT, (lambda e, p_=pf, o=rb: e.activation(out=o, in_=p_[:], func=AF.Relu)), [pfk], [rbk])
                    P.op(DVE, (lambda e, o=aTf[:, ffc, hs], i=rb: e.tensor_tensor(out=o, in0=i, in1=i, op=ALU.mult)), [rbk], [("mT", ffc, half)])
                    pump(1)
            flush()
            free_block(w1k[0])
            free_block(w1k[1])
            if fg < 3:
                w2_next = load_w2(fg + 1)
            elif ps_id == 0:
                a0_next = a0_stage_gen(1)
            if fg == 3:
                P.dma(fnw_bc, fnw_d.partition_broadcast(128), writes=["fnw_bc"])
            for t in range(NT):
                for chh in range(2):
                    if fg == 3 and a0_next is not None:
                        next(a0_next, None)
                    p2, p2k = ps_get("tp")
                    for ffc in range(8):
                        P.op(PE, (lambda e, p_=p2, l=aTf[:, ffc, t * 128:(t + 1) * 128], r=w2[chh][:, ffc, :], st=(ffc == 0), sp=(ffc == 7): e.matmul(p_[:], lhsT=l, rhs=r, start=st, stop=sp)),
                             [("mT", ffc, t // 4), w2k[chh]], [p2k])
                    xs_ = x1[:, t, chh * 512:(chh + 1) * 512]
                    P.op(DVE, (lambda e, p_=p2, o=xs_: e.tensor_tensor(out=o, in0=p_[:], in1=o, op=ALU.add)), [p2k, ("x1", t, chh)], [("x1", t, chh)])
                    pump(1)
                if fg == 3 and ps_id == 1:
                    final_norm_tile(t)
            flush()
            free_block(w2k[0])
            free_block(w2k[1])
        if a0_next is not None:
            for _ in a0_next:
                pass
        if ps_id == 0:
            for t in range(NT):
                final_norm_tile(t)
        mark("p%d D end" % ps_id)
        if stop_after == "pass0":
            break

    P.finalize()
    return nc, dbg_out


def _consts():
    ident = np.eye(128, dtype=np.float32)
    j = np.arange(128)[:, None]
    i = np.arange(128)[None, :]
    same = (j // CH) == (i // CH)
    maskf = (same & (j <= i)).astype(np.float32)
    maskb = (same & (j >= i)).astype(np.float32)
    n_tokens = 1024
    gw = 64
    row = np.repeat(np.arange(n_tokens // gw, dtype=np.float32), gw)
    col = np.tile(np.arange(gw, dtype=np.float32), n_tokens // gw)
    axis_dim = 32
    freqs = (np.float32(10000.0) ** (-np.arange(0, axis_dim, 2, dtype=np.float32) / np.float32(axis_dim))).astype(np.float32)
    ang_r = (row[:, None] * freqs).astype(np.float32)
    ang_c = (col[:, None] * freqs).astype(np.float32)
    cr, sr = np.cos(ang_r).astype(np.float32), np.sin(ang_r).astype(np.float32)
    cc, sc = np.cos(ang_c).astype(np.float32), np.sin(ang_c).astype(np.float32)
    C = np.concatenate([cr, cr, cc, cc], axis=1).astype(np.float32)
    S = np.concatenate([-sr, sr, -sc, sc], axis=1).astype(np.float32)
    return ident, maskf, maskb, C, S


def make_in_maps(inp):
    f = lambda a: np.ascontiguousarray(np.asarray(a, dtype=np.float32))
    ident, maskf, maskb, C, S = _consts()
    shared = {
        "w_ada": f(inp["w_ada"][0]),
        "b_adaT": f(np.asarray(inp["b_ada"][0]).reshape(48, 128).T),
        "nmwT": f(np.asarray(inp["norm_mix_w"][0]).reshape(8, 128).T),
        "nfwT": f(np.asarray(inp["norm_ffn_w"][0]).reshape(8, 128).T),
        "fnw": f(np.asarray(inp["final_norm_w"]).reshape(1, D)),
        "w_in": f(inp["w_in"][0]),
        "qnw": f(np.asarray(inp["q_norm_w"][0]).reshape(1, 64)),
        "knw": f(np.asarray(inp["k_norm_w"][0]).reshape(1, 64)),
        "lblT": f(np.asarray(inp["hgrn_lb_logits"]).reshape(2, 2, 4, 128).transpose(3, 0, 1, 2).reshape(128, 16)),
        "hnw": f(np.asarray(inp["hgrn_norm_w"][0]).reshape(128, 1)),
        "w_ho": f(inp["w_hgrn_out"][0]),
        "w_ao": f(inp["w_attn_out"][0]),
        "w_out": f(inp["w_out"][0]),
        "w_ff1": f(inp["w_ff1"][0]),
        "w_ff2": f(inp["w_ff2"][0]),
        "ident": ident, "maskf": maskf, "maskb": maskb, "ropeC": C, "ropeS": S,
    }
    xp = np.asarray(inp["x_prompt"], dtype=np.float32)
    xs = np.asarray(inp["x_sample"], dtype=np.float32)
    c = np.asarray(inp["c"], dtype=np.float32)
    cctx = np.asarray(inp["c_ctx"], dtype=np.float32)
    maps = []
    for i in range(8):
        m = dict(shared)
        m["x"] = np.ascontiguousarray(np.concatenate([xs[i], xp[4 * i:4 * i + 4].reshape(1024, D)], axis=0))
        m["ck"] = f(np.asarray(inp["cache_k"])[i, 0].reshape(512, 128))
        m["cv"] = f(np.asarray(inp["cache_v"])[i, 0].reshape(512, 128))
        m["s0"] = f(np.asarray(inp["state_hgrn"])[i, 0].reshape(8, 128, 128))
        cond = np.stack([c[i], cctx], axis=1)
        m["condT"] = np.ascontiguousarray(cond.reshape(8, 128, 2).transpose(1, 0, 2).reshape(128, 16))
        maps.append(m)
    return maps


_NC_CACHE = {}


def kernel(**inputs):
    if "nc" not in _NC_CACHE:
        _NC_CACHE["nc"] = build_program()[0]
    nc = _NC_CACHE["nc"]
    maps = make_in_maps(inputs)
    res = run_bass_kernel_spmd(nc, maps, core_ids=list(range(8)))
    rs = res.results
    y_s = np.stack([rs[i]["y"][0:1024] for i in range(8)], axis=0)
    y_p = np.concatenate([rs[i]["y"][1024:2048].reshape(4, 256, D) for i in range(8)], axis=0)
    nk = np.concatenate([rs[i]["nk"].reshape(4, 1, 256, 2, 64) for i in range(8)], axis=0)
    nv = np.concatenate([rs[i]["nv"].reshape(4, 1, 256, 2, 64) for i in range(8)], axis=0)
    ns = np.concatenate([rs[i]["ns"].reshape(4, 1, 2, 4, 128, 128) for i in range(8)], axis=0)
    return (y_p.astype(np.float32), y_s.astype(np.float32), nk.astype(np.float32),
            nv.astype(np.float32), ns.astype(np.float32))
```

```python
from contextlib import ExitStack
import numpy as np
import concourse.bass as bass
import concourse.mybir as mybir
from concourse.bass_utils import run_bass_kernel_spmd

F32 = mybir.dt.float32
BF16 = mybir.dt.bfloat16
AF = mybir.ActivationFunctionType
ALU = mybir.AluOpType
AX = mybir.AxisListType

PE, ACT, DVE, POOL, SP = "tensor", "scalar", "vector", "gpsimd", "sync"
MARKS = []
STRICT_SAME_ENGINE = True
COMPUTE = (PE, ACT, DVE, POOL)
N_DMA_SEMS = 24

D = 1024
EPS = 1e-6
NT = 8
PASS_TOK = 1024
CH = 64
D_IN = 5376


class Op:
    __slots__ = ("idx", "eng", "fn", "reads", "writes", "is_dma", "deps", "need_inc",
                 "sem", "semval", "dma_prev", "phase", "real_writes")

    def __init__(self, idx, eng, fn, reads, writes, is_dma):
        self.phase = 0
        self.idx = idx
        self.eng = eng
        self.fn = fn
        self.reads = reads
        self.writes = writes
        self.is_dma = is_dma
        self.deps = []
        self.need_inc = False
        self.sem = None
        self.semval = 0
        self.dma_prev = None


class Prog:
    def __init__(self, nc):
        self.nc = nc
        self.ops = []
        self.stack = ExitStack()
        self.phase = 0
        self.max_ops = None

    def barrier(self):
        self.phase += 1

    def sb(self, name, shape, dtype):
        return self.stack.enter_context(self.nc.sbuf_tensor("sb_" + name, list(shape), dtype))

    def ps(self, name, shape, dtype):
        return self.stack.enter_context(self.nc.psum_tensor(name, list(shape), dtype))

    def op(self, eng, fn, reads=(), writes=(), dma=False):
        rw = tuple(writes)
        weff = rw + tuple(k for k in reads if isinstance(k, tuple) and k and k[0] == "ps" and k not in rw)
        o = Op(len(self.ops), eng, fn, tuple(reads), weff, dma)
        o.real_writes = rw
        o.phase = self.phase
        if self.max_ops is not None and len(self.ops) >= self.max_ops:
            return o
        self.ops.append(o)
        return o

    def dma(self, out, in_, reads=(), writes=(), eng=SP, **kw):
        return self.op(eng, lambda e: e.dma_start(out=out, in_=in_, **kw), reads, writes, dma=True)

    def finalize(self):
        nc = self.nc
        last_w = {}
        readers = {}
        for o in self.ops:
            deps = set()
            for k in o.reads:
                w = last_w.get(k)
                if w is not None:
                    deps.add(w)
            for k in o.writes:
                w = last_w.get(k)
                if w is not None:
                    deps.add(w)
                for r in readers.get(k, ()):
                    deps.add(r)
            for k in o.writes:
                last_w[k] = o
                readers[k] = []
            for k in o.reads:
                if k in o.writes:
                    continue
                lst = readers.setdefault(k, [])
                if not o.is_dma:
                    lst[:] = [r for r in lst if r.is_dma or r.eng != o.eng]
                lst.append(o)
            deps.discard(o)
            for d in deps:
                if (not d.is_dma) and (not o.is_dma) and d.eng == o.eng:
                    if o.eng == PE:
                        continue
                    raw = any(k in d.real_writes for k in o.reads)
                    if not raw and not STRICT_SAME_ENGINE:
                        continue
                o.deps.append(d)
                d.need_inc = True

        last_comp = {}
        last_dma = {}
        nd0 = 0
        seen_phase = {}
        snap = {}
        cur_phase = 0
        for o in self.ops:
            if o.phase != cur_phase:
                cur_phase = o.phase
                snap[cur_phase] = (dict(last_comp), dict(last_dma))
            if o.phase > 0 and seen_phase.get(o.eng, 0) < o.phase:
                seen_phase[o.eng] = o.phase
                lc, ld = snap[o.phase]
                for d in list(lc.values()) + list(ld.values()):
                    if d is not o and d not in o.deps:
                        if (not d.is_dma) and d.eng == o.eng and not o.is_dma:
                            continue
                        o.deps.append(d)
                        d.need_inc = True
            if o.is_dma:
                last_dma[nd0 % N_DMA_SEMS] = o
                nd0 += 1
            else:
                last_comp[o.eng] = o

        eng_sem = {e: self.stack.enter_context(nc.semaphore("s_" + e)) for e in COMPUTE}
        dma_sems = [self.stack.enter_context(nc.semaphore("d%d" % i)) for i in range(N_DMA_SEMS)]
        cnt = {e: 0 for e in COMPUTE}
        dcnt = [0] * N_DMA_SEMS
        dlast = [None] * N_DMA_SEMS
        nd = 0
        for o in self.ops:
            if o.is_dma:
                s = nd % N_DMA_SEMS
                nd += 1
                o.dma_prev = dlast[s]
                dcnt[s] += 16
                o.sem = dma_sems[s]
                o.semval = dcnt[s]
                dlast[s] = o
                o.need_inc = True
            elif o.need_inc:
                cnt[o.eng] += 1
                o.sem = eng_sem[o.eng]
                o.semval = cnt[o.eng]

        per_eng = {}
        for o in self.ops:
            per_eng.setdefault(o.eng, []).append(o)
        final_waits = [(s, c) for s, c in zip(dma_sems, dcnt) if c > 0]

        def emit(eng_name, engine):
            waited = {}
            for o in per_eng.get(eng_name, []):
                need = {}
                for d in o.deps:
                    if need.get(d.sem, 0) < d.semval:
                        need[d.sem] = d.semval
                if o.is_dma and o.dma_prev is not None:
                    d = o.dma_prev
                    if need.get(d.sem, 0) < d.semval:
                        need[d.sem] = d.semval
                for sem, val in need.items():
                    if waited.get(sem, 0) >= val:
                        continue
                    engine.wait_ge(sem, val)
                    waited[sem] = val
                ins = o.fn(engine)
                if o.need_inc:
                    ins.then_inc(o.sem, 16 if o.is_dma else 1)
            if eng_name == SP:
                for s, c in final_waits:
                    engine.wait_ge(s, c)

        with nc.Block() as block:
            @block.sync
            def _(e):
                emit(SP, e)

            @block.tensor
            def _(e):
                emit(PE, e)

            @block.scalar
            def _(e):
                emit(ACT, e)

            @block.vector
            def _(e):
                emit(DVE, e)

            @block.gpsimd
            def _(e):
                emit(POOL, e)
        self.stack.close()


def build_program(stop_after=None, dbg=(), max_ops=None):
    nc = bass.Bass("TRN2", target_bir_lowering=False)
    P = Prog(nc)
    P.max_ops = max_ops
    MARKS.clear()

    def mark(name):
        MARKS.append((name, sum(1 for o in P.ops if o.eng == PE)))

    def din(name, shape):
        return nc.dram_tensor(name, list(shape), F32, kind="ExternalInput").ap()

    def dout(name, shape):
        return nc.dram_tensor(name, list(shape), F32, kind="ExternalOutput").ap()

    x_d = din("x", [2048, D])
    ck_d = din("ck", [512, 128])
    cv_d = din("cv", [512, 128])
    s0_d = din("s0", [8, 128, 128])
    condT_d = din("condT", [128, 16])
    w_ada_d = din("w_ada", [D, 6 * D])
    b_adaT_d = din("b_adaT", [128, 48])
    nmwT_d = din("nmwT", [128, 8])
    nfwT_d = din("nfwT", [128, 8])
    fnw_d = din("fnw", [1, D])
    w_in_d = din("w_in", [D, D_IN])
    qnw_d = din("qnw", [1, 64])
    knw_d = din("knw", [1, 64])
    lblT_d = din("lblT", [128, 16])
    hnw_d = din("hnw", [128, 1])
    w_ho_d = din("w_ho", [512, D])
    w_ao_d = din("w_ao", [512, D])
    w_out_d = din("w_out", [D, D])
    w_ff1_d = din("w_ff1", [D, 4 * D])
    w_ff2_d = din("w_ff2", [4 * D, D])
    ident_d = din("ident", [128, 128])
    maskf_d = din("maskf", [128, 128])
    maskb_d = din("maskb", [128, 128])
    ropeC_d = din("ropeC", [1024, 64])
    ropeS_d = din("ropeS", [1024, 64])

    y_d = dout("y", [2048, D])
    nk_d = dout("nk", [1024, 128])
    nv_d = dout("nv", [1024, 128])
    ns_d = dout("ns", [4, 8, 128, 128])
    dbg_out = {}

    def dbg_dump(name, ap, shape, reads):
        if name in dbg:
            d = nc.dram_tensor("dbg_" + name, list(shape), ap.dtype, kind="ExternalOutput").ap()
            dbg_out[name] = d
            P.dma(d, ap, reads=reads)

    NSLOT = 6
    W = [P.sb("W%d" % i, [128, 4096], BF16) for i in range(NSLOT)]
    NSTG = 3
    STG = [P.sb("stg%d" % i, [128, 1024], F32) for i in range(NSTG)]
    hT = P.sb("hT", [128, 8, PASS_TOK], BF16)
    o_aT = P.sb("o_aT", [128, 4, PASS_TOK], BF16)
    o_hT = P.sb("o_hT", [128, 4, PASS_TOK], BF16)
    mods = P.sb("mods", [128, 48, 2], F32)
    bcA = P.sb("bcA", [128, D], F32)
    bcS = P.sb("bcS", [128, D], F32)
    vecT = P.sb("vecT", [128, 8], F32)
    ident_f = P.sb("ident_f", [128, 128], F32)
    ident_b = P.sb("ident_b", [128, 128], BF16)
    ones_f = P.sb("ones_f", [128, 128], F32)
    ones_b = P.sb("ones_b", [128, 128], BF16)
    maskf = P.sb("maskf", [128, 128], BF16)
    maskb = P.sb("maskb", [128, 128], BF16)
    mstage = P.sb("mstage", [128, 128], F32)
    mhalf = P.sb("mhalf", [128, 16], F32)
    epsc = P.sb("epsc", [128, 1], F32)
    bcG2 = P.sb("bcG2", [128, D], F32)
    smask_f = P.sb("smask_f", [128, PASS_TOK], F32)
    dg = P.sb("dg", [128, 4, 128], F32)
    condT = P.sb("condT", [128, 16], F32)
    sil = P.sb("sil", [128, 16], BF16)
    b_adaT = P.sb("b_adaT", [128, 48], F32)
    nmwT = P.sb("nmwT", [128, 8], F32)
    nfwT = P.sb("nfwT", [128, 8], F32)
    qnw_bc = P.sb("qnw_bc", [128, 64], F32)
    knw_bc = P.sb("knw_bc", [128, 64], F32)
    lblT = P.sb("lblT", [128, 16], F32)
    oml = P.sb("oml", [128, 8], F32)
    hnw = P.sb("hnw", [128, 1], F32)
    xt = [P.sb("xt%d" % i, [128, D], F32) for i in range(2)]
    junk = P.sb("junk", [128, D], BF16)
    ssq = P.sb("ssq", [128, 16], F32)
    rstd = P.sb("rstd", [128, 16], F32)
    ssq10_l = [P.sb("ssq10_%d" % i, [128, 10], F32) for i in range(2)]
    rstd10_l = [P.sb("rstd10_%d" % i, [128, 10], F32) for i in range(2)]
    decs = [P.sb("decs%d" % i, [128, 16], F32) for i in range(4)]
    htok = [P.sb("htok%d" % i, [128, D], BF16) for i in range(2)]
    ARENA_COLS = 14848 + 1024 + 3584 + 512
    arena = P.sb("arena", [128, ARENA_COLS], F32)

    class Carver:
        def __init__(self):
            self.off = 0

        def _shape(self, v, shape):
            if len(shape) == 2:
                return v
            names = ["a", "b", "c", "d"][:len(shape) - 1]
            pat = "p (%s) -> p %s" % (" ".join(names), " ".join(names))
            kw = {n: s for n, s in zip(names[1:], shape[2:])}
            return v.rearrange(pat, **kw)

        def f32(self, shape):
            n = int(np.prod(shape[1:]))
            v = arena[:, self.off:self.off + n]
            self.off += n
            assert self.off <= ARENA_COLS, self.off
            return self._shape(v, shape)

        def bf16(self, shape):
            n = int(np.prod(shape[1:]))
            nf = (n + 1) // 2
            v = arena[:, self.off:self.off + nf].bitcast(BF16)[:, 0:n]
            self.off += nf
            assert self.off <= ARENA_COLS, self.off
            return self._shape(v, shape)

    PSALL = P.ps("psall", [128, 8 * 512], F32)
    PSB = [PSALL[:, i * 512:(i + 1) * 512] for i in range(8)]
    ps_rot = {"mm": [0, 1], "mm2": [2, 3], "kv": [4, 5], "tp": [4, 5], "h1": [6, 2, 3], "h2": [7, 5], "acc": [2, 3, 6, 7], "kvr": [2, 3], "tp1": [4], "st4": [0, 4, 1, 5]}
    ps_idx = {k: 0 for k in ps_rot}

    def ps_get(group):
        lst = ps_rot[group]
        b = lst[ps_idx[group] % len(lst)]
        ps_idx[group] += 1
        return PSB[b], ("ps", b)

    cv_ = Carver()
    cv_.off = 17920
    h1_a = [cv_.f32([128, D]) for i in range(2)]
    cv_ = Carver()
    v_tok = cv_.bf16([128, NT, 512])
    qT = [cv_.bf16([128, 4, PASS_TOK]) for i in range(2)]
    kT2 = cv_.bf16([128, 2, 1536])
    v_ext = cv_.bf16([128, 12, 2, 192])
    ropeC_l = [cv_.f32([128, 64]) for i in range(2)]
    ropeS_l = [cv_.f32([128, 64]) for i in range(2)]
    sq512_l = [cv_.f32([128, 640]) for i in range(2)]
    qn_l = [cv_.f32([128, 512]) for i in range(2)]
    kn_l = [cv_.f32([128, 128]) for i in range(2)]
    vraw_l = [cv_.f32([128, 128]) for i in range(2)]
    rt1_l = [cv_.f32([128, 512]) for i in range(2)]
    rt2_l = [cv_.f32([128, 512]) for i in range(2)]
    qr_l = [cv_.bf16([128, 512]) for i in range(2)]
    kt1_l = [cv_.f32([128, 128]) for i in range(2)]
    kt2_l = [cv_.f32([128, 128]) for i in range(2)]
    kdup_l = [cv_.bf16([128, 2, 2, 64]) for i in range(2)]
    ckst = cv_.f32([128, 4, 128])
    ckdup = cv_.bf16([128, 4, 2, 128])
    pt = [cv_.bf16([128, 512]) for i in range(4)]
    recs = cv_.f32([128, 512])
    cv_ = Carver()
    _v_tok_same = cv_.bf16([128, NT, 512])
    qTh = cv_.bf16([128, PASS_TOK])
    hgTh = [cv_.bf16([128, PASS_TOK]) for i in range(2)]
    uT = cv_.f32([128, PASS_TOK])
    lf = cv_.f32([128, PASS_TOK])
    aT = cv_.f32([128, PASS_TOK])
    qd = [cv_.bf16([128, PASS_TOK]) for i in range(4)]
    kd = [cv_.bf16([128, PASS_TOK]) for i in range(4)]
    kdtok = [cv_.bf16([128, NT, 128]) for i in range(4)]
    oT = cv_.f32([128, PASS_TOK])
    osq = cv_.bf16([128, 512])
    ors = cv_.f32([128, 512])
    ot1 = cv_.f32([128, 512])
    Tst = [cv_.f32([128, 128]) for i in range(8)]
    Sall = cv_.bf16([128, 2, 16, 128])
    Sinit = [cv_.bf16([128, 128]) for i in range(2)]
    S0f = [cv_.f32([128, 128]) for i in range(2)]
    sfin_l = [cv_.f32([128, 128]) for i in range(4)]
    sfin_i = [0]
    smk = [cv_.bf16([128, 128]) for i in range(4)]
    cv_ = Carver()
    x1 = cv_.f32([128, NT, D])
    g0 = cv_.f32([128, 512])
    g1 = cv_.f32([128, 512])
    m0 = cv_.f32([128, 512])
    m1 = cv_.f32([128, 512])
    mT = cv_.bf16([128, 8, PASS_TOK])
    aTf = mT
    rl = [cv_.bf16([128, 512]) for i in range(2)]
    h1_d = [cv_.f32([128, D]) for i in range(2)]
    fnw_bc = cv_.f32([128, D])
    assert cv_.off <= 17920, cv_.off

    stg_i = [0]
    free_slots = list(range(NSLOT))

    def free_block(wkey):
        free_slots.append(wkey[1])

    pending = []

    def pump(n=1):
        for _ in range(n):
            if pending:
                pending.pop(0)()

    def flush():
        while pending:
            pending.pop(0)()

    def load_block(src3, nk, ncols, fold_bc=None, fold_key=None, defer=False, cast_eng=DVE, queue=None):
        slot = free_slots.pop(0)
        wv = W[slot][:, 0:nk * ncols].rearrange("p (k n) -> p k n", n=ncols)
        kper = max(1, 1024 // ncols)
        wkey = ("W", slot)

        def step(k0, kk):
            gi = stg_i[0] % NSTG
            stg_i[0] += 1
            sv = STG[gi][:, 0:kk * ncols].rearrange("p (k n) -> p k n", n=ncols)
            P.dma(sv, src3[:, k0:k0 + kk, :], writes=[("stg", gi)])
            if fold_bc is None:
                if cast_eng == ACT:
                    P.op(ACT, (lambda e, o=wv[:, k0:k0 + kk, :], i=sv: e.activation(out=o, in_=i, func=AF.Copy)),
                         [("stg", gi)], [wkey])
                else:
                    P.op(DVE, (lambda e, o=wv[:, k0:k0 + kk, :], i=sv: e.tensor_copy(out=o, in_=i)),
                         [("stg", gi)], [wkey])
            else:
                fb = fold_bc.unsqueeze(1).broadcast_to([128, kk, ncols])
                P.op(DVE, (lambda e, o=wv[:, k0:k0 + kk, :], i=sv, f=fb: e.tensor_tensor(out=o, in0=i, in1=f, op=ALU.mult)),
                     [("stg", gi), fold_key], [wkey])

        for k0 in range(0, nk, kper):
            kk = min(kper, nk - k0)
            if defer:
                (pending if queue is None else queue).append(lambda k0=k0, kk=kk: step(k0, kk))
            else:
                step(k0, kk)
        return wv, wkey

    def wcols(w_d, c0, ncols, r0=0, nrows=1024):
        return w_d[r0:r0 + nrows, c0:c0 + ncols].rearrange("(k p) n -> p k n", p=128)

    P.dma(ident_f[:], ident_d, writes=["ident_f"])
    P.op(DVE, lambda e: e.tensor_copy(out=ident_b[:], in_=ident_f[:]), ["ident_f"], ["ident_b"])
    P.op(POOL, lambda e: e.memset(ones_f[:], 1.0), [], ["ones_f"])
    P.op(POOL, lambda e: e.memset(ones_b[:], 1.0), [], ["ones_b"])
    P.op(POOL, lambda e: e.memset(mhalf[:], -0.5), [], ["mhalf"])
    P.op(POOL, lambda e: e.memset(epsc[:], EPS), [], ["epsc"])
    P.dma(mstage[:], maskf_d, writes=["mstage"])
    P.op(DVE, lambda e: e.tensor_copy(out=maskf[:], in_=mstage[:]), ["mstage"], ["maskf"])
    P.dma(mstage[:], maskb_d, reads=[], writes=["mstage"])
    P.op(DVE, lambda e: e.tensor_copy(out=maskb[:], in_=mstage[:]), ["mstage"], ["maskb"])
    P.op(POOL, lambda e: e.memset(smask_f[:], 1.0), [], ["smask_f"])
    P.op(POOL, lambda e: e.memset(smask_f[:].rearrange("p (c t) -> p c t", t=CH)[:, :, 0:1], 0.0), ["smask_f"], ["smask_f"])
    P.dma(condT[:], condT_d, writes=["condT"])
    P.dma(b_adaT[:], b_adaT_d, writes=["b_adaT"])
    P.dma(nmwT[:], nmwT_d, writes=["nmwT"])
    P.dma(nfwT[:], nfwT_d, writes=["nfwT"])
    P.dma(qnw_bc[:], qnw_d.partition_broadcast(128), writes=["qnw_bc"])
    P.dma(knw_bc[:], knw_d.partition_broadcast(128), writes=["knw_bc"])
    P.dma(lblT[:], lblT_d, writes=["lblT"])
    P.dma(hnw[:], hnw_d, writes=["hnw"])
    lv = lblT[:].rearrange("p (d l h) -> p d l h", d=2, l=2)
    P.op(DVE, lambda e: e.tensor_tensor(out=oml[:].rearrange("p (d h) -> p d h", d=2), in0=lv[:, :, 1, :], in1=lv[:, :, 0, :], op=ALU.subtract),
         ["lblT"], ["oml"])
    P.op(ACT, lambda e: e.activation(out=oml[:], in_=oml[:], func=AF.Sigmoid), ["oml"], ["oml"])

    P.op(ACT, lambda e: e.activation(out=sil[:], in_=condT[:], func=AF.Silu), ["condT"], ["sil"])
    silv = sil[:].rearrange("p (k j) -> p k j", j=2)
    pM, pMk = ps_get("h2")

    def ada_block(blk, pM_, pMk_, first_chunk, slack=0):
        q_ = []
        wv, wkey = load_block(wcols(w_ada_d, blk * 512, 512), 8, 512, defer=True, queue=q_)
        for st_ in q_:
            st_()
            yield None
        for _ in range(slack):
            yield None
        for cc in range(4):
            chunk = blk * 4 + cc
            for kc in range(8):
                P.op(PE, (lambda e, o=pM_[:, chunk * 2:chunk * 2 + 2], l=wv[:, kc, cc * 128:(cc + 1) * 128], r=silv[:, kc, :], st=(kc == 0 and chunk == first_chunk), sp=(kc == 7):
                          e.matmul(o, lhsT=l, rhs=r, start=st, stop=sp, skip_group_check=True)),
                     [wkey, "sil"], [pMk_])
            yield None
        free_block(wkey)

    for blk in range(4):
        for _ in ada_block(blk, pM, pMk, 0):
            pass
    P.op(DVE, lambda e: e.tensor_tensor(out=mods[:, 0:16, :], in0=pM[:, 0:32].rearrange("p (c j) -> p c j", j=2),
                                        in1=b_adaT[:, 0:16].unsqueeze(2).broadcast_to([128, 16, 2]), op=ALU.add),
         [pMk, "b_adaT"], ["mods_a"])
    pM2, pM2k = PSB[6], ("ps", 6)

    def ada2_gen():
        for blk in range(4, 12):
            for _ in ada_block(blk, pM2, pM2k, 16, slack=4):
                yield None
        P.op(DVE, lambda e: e.tensor_tensor(out=mods[:, 16:48, :], in0=pM2[:, 32:96].rearrange("p (c j) -> p c j", j=2),
                                            in1=b_adaT[:, 16:48].unsqueeze(2).broadcast_to([128, 32, 2]), op=ALU.add),
             [pM2k, "b_adaT"], ["mods_b"])

    ada2 = ada2_gen()
    if "mods" in dbg:
        for _ in ada2:
            pass
    dbg_dump("mods", mods[:].rearrange("p c j -> p (c j)"), [128, 96], ["mods_a", "mods_b"])

    def expand(vec_ap, vec_keys, dst, dst_key):
        for half in range(2):
            pE, pEk = ps_get("h2")
            for c4 in range(4):
                fc = half * 4 + c4
                P.op(DVE, (lambda e, o=dg[:, c4, :], s=vec_ap[:, fc:fc + 1]: e.tensor_scalar(out=o, in0=ident_f[:], scalar1=s, scalar2=None, op0=ALU.mult)),
                     ["ident_f"] + list(vec_keys), [("dg", c4)])
                P.op(PE, (lambda e, o=pE[:, c4 * 128:(c4 + 1) * 128], r=dg[:, c4, :]: e.matmul(o, lhsT=ones_f[:], rhs=r, start=True, stop=True)),
                     ["ones_f", ("dg", c4)], [pEk])
            P.op(ACT, (lambda e, o=dst[:, half * 512:(half + 1) * 512], i=pE[:]: e.activation(out=o, in_=i, func=AF.Copy)),
                 [pEk], [dst_key])

    mark("adaln end")
    if stop_after == "adaln":
        P.finalize()
        return nc, dbg_out

    def pipeline_gen(make_gen, n, gap):
        active = []
        started = 0
        rnd = 0
        while started < n or active:
            if started < n and rnd % gap == 0:
                active.append(make_gen(started))
                started += 1
            for g_ in list(active):
                try:
                    next(g_)
                except StopIteration:
                    active.remove(g_)
            rnd += 1
            yield None

    def pipeline(make_gen, n, gap):
        for _ in pipeline_gen(make_gen, n, gap):
            pass

    def norm_mod_transpose(t, src_ap, src_keys, h1l, pre=None):
        p = t % 2
        h1 = h1l[p]
        if pre is not None:
            pre()
        P.op(ACT, (lambda e, i=src_ap: e.activation(out=junk[:], in_=i, func=AF.Square, accum_out=ssq[:, p:p + 1])), src_keys, ["junk", ("ssq", p)])
        yield None
        P.op(POOL, lambda e: e.tensor_scalar(out=rstd[:, 2 + p:3 + p], in0=ssq[:, p:p + 1], scalar1=1.0 / D, scalar2=EPS, op0=ALU.mult, op1=ALU.add), [("ssq", p)], [("rstd1", p)])
        P.op(POOL, lambda e: e.tensor_tensor(out=rstd[:, p:p + 1], in0=rstd[:, 2 + p:3 + p], in1=mhalf[:, 0:1], op=ALU.pow), [("rstd1", p), "mhalf"], [("rstd", p)])
        P.op(DVE, (lambda e, i=src_ap: e.scalar_tensor_tensor(out=h1, in0=i, scalar=rstd[:, p:p + 1], in1=bcA[:], op0=ALU.mult, op1=ALU.mult)),
             list(src_keys) + [("rstd", p), "bcA"], [("h1", p)])
        yield None
        hb = htok[p]
        hk = ("htok", p)
        P.op(DVE, (lambda e, o=hb[:]: e.tensor_tensor(out=o, in0=h1, in1=bcS[:], op=ALU.add)), [("h1", p), "bcS"], [hk])
        yield None
        pT, pTk = ps_get("tp")
        pTb = pT.bitcast(BF16)
        for kc in range(8):
            P.op(PE, (lambda e, o=pTb[:, kc * 128:(kc + 1) * 128], i=hb[:, kc * 128:(kc + 1) * 128]: e.transpose(o, i, ident_b[:])),
                 [hk, "ident_b"], [pTk])
        P.op(ACT, (lambda e, o=hT[:, :, t * 128:(t + 1) * 128], i=pTb[:, 0:1024].rearrange("p (k n) -> p k n", n=128): e.activation(out=o, in_=i, func=AF.Copy)),
             [pTk], [("hT", t)])

    def rms_rstd_only(src_ap, src_keys):
        P.op(ACT, (lambda e, i=src_ap: e.activation(out=junk[:], in_=i, func=AF.Square, accum_out=ssq[:, 8:9])), src_keys, ["junk", "ssqF"])
        P.op(POOL, lambda e: e.tensor_scalar(out=rstd[:, 9:10], in0=ssq[:, 8:9], scalar1=1.0 / D, scalar2=EPS, op0=ALU.mult, op1=ALU.add), ["ssqF"], ["rstd1F"])
        P.op(POOL, lambda e: e.tensor_tensor(out=rstd[:, 8:9], in0=rstd[:, 9:10], in1=mhalf[:, 0:1], op=ALU.pow), ["rstd1F", "mhalf"], ["rstdF"])

    hT_all = [("hT", t) for t in range(NT)]
    a0_res = {}

    def a0_stage_gen(pid):
        jj = pid
        r0 = pid * PASS_TOK
        P.op(DVE, (lambda e: e.scalar_tensor_tensor(out=vecT[:], in0=mods[:, 8:16, jj], scalar=1.0, in1=nmwT[:], op0=ALU.add, op1=ALU.mult)),
             ["mods_a", "nmwT"], ["vecT"])
        expand(vecT, ["vecT"], bcA, "bcA")
        expand(mods[:, 0:8, jj], ["mods_a"], bcS, "bcS")
        yield None
        a0_res[pid] = (load_block(wcols(w_in_d, 2560, 512), 8, 512, defer=True), load_block(wcols(w_in_d, 3072, 256), 8, 256, defer=True),
                       load_block(wcols(w_in_d, 1536, 512), 8, 512, defer=True))

        def a0_gen(t):
            xb = xt[t % 2]
            xk = ("xt", t % 2)
            return norm_mod_transpose(t, xb[:], [xk], h1_a,
                                      pre=lambda: P.dma(xb[:], x_d[r0 + t * 128:r0 + (t + 1) * 128, :], writes=[xk]))
        for _ in pipeline_gen(a0_gen, NT, 2):
            pump(1)
            yield None
        flush()

    a0_next = None

    for ps_id in range(2):
        latent = (ps_id == 0)
        j = ps_id
        row0 = ps_id * PASS_TOK
        seqs = [(0, 8)] if latent else [(0, 2), (2, 2), (4, 2), (6, 2)]
        koff = 512 if latent else 0
        ktoff = 4 if latent else 0

        if a0_next is None:
            a0_next = a0_stage_gen(ps_id)
        for _ in a0_next:
            pass
        a0_next = None
        (waq, waqk), (wakv, wakvk), (whi, whik) = a0_res[ps_id]
        if ps_id == 1:
            dbg_dump("hT1", hT[:].rearrange("p k n -> p (k n)"), [128, 8 * PASS_TOK], hT_all)
        P.barrier()
        P.op(POOL, lambda e: e.memset(v_ext[:], 1.0), [], ["v_ext_all"])
        P.op(POOL, lambda e: e.memset(qT[0][64:128], 0.0), [], ["qT_zero0"])
        P.op(POOL, lambda e: e.memset(qT[1][0:64], 0.0), [], ["qT_zero1"])
        if ps_id == 0:
            dbg_dump("hT", hT[:].rearrange("p k n -> p (k n)"), [128, 8 * PASS_TOK], hT_all)
        mark("p%d A0 end" % ps_id)
        if stop_after == "A0":
            break

        if latent:
            P.dma(ckst[:], ck_d.rearrange("(t p) c -> p t c", p=128), writes=["ckst"])
            ckv = ckst.rearrange("p t (g d) -> p t g d", g=2)
            for dup in range(2):
                P.op(DVE, (lambda e, o=ckdup[:, :, :, dup * 64:(dup + 1) * 64]: e.tensor_copy(out=o, in_=ckv)), ["ckst"], [("ckdup", dup)])
            pT, pTk = ps_get("tp")
            pTb = pT.bitcast(BF16)
            for kt in range(4):
                for g in range(2):
                    P.op(PE, (lambda e, o=pTb[:, (kt * 2 + g) * 128:(kt * 2 + g + 1) * 128], i=ckdup[:, kt, g, :]: e.transpose(o, i, ident_b[:])),
                         [("ckdup", 0), ("ckdup", 1), "ident_b"], [pTk])
            P.op(ACT, (lambda e, o=kT2[:, :, 0:512].rearrange("p g (t n) -> p g t n", n=128),
                       i=pTb[:, 0:1024].rearrange("p (t g n) -> p g t n", g=2, n=128): e.activation(out=o, in_=i, func=AF.Copy)),
                 [pTk], [("kT2", kt) for kt in range(4)])
            P.dma(ckst[:], cv_d.rearrange("(t p) c -> p t c", p=128), writes=["ckst"])
            P.op(DVE, lambda e: e.tensor_copy(out=v_ext[:, 0:4, :, 64:128], in_=ckst.rearrange("p t (g d) -> p t g d", g=2)),
                 ["ckst", "v_ext_all"], [("v_ext", kt) for kt in range(4)])
        def aproj_tile(t):
            hk = ("hT", t)
            tp_ = t % 2
            ropeC, ropeS, sq512, qn, kn, vraw, rt1, rt2, qr, kt1, kt2, kdup = (ropeC_l[tp_], ropeS_l[tp_], sq512_l[tp_], qn_l[tp_], kn_l[tp_], vraw_l[tp_],
                                                                               rt1_l[tp_], rt2_l[tp_], qr_l[tp_], kt1_l[tp_], kt2_l[tp_], kdup_l[tp_])
            ssq10, rstd10 = ssq10_l[tp_], rstd10_l[tp_]
            pq, pqk = ps_get("mm")
            for kc in range(8):
                P.op(PE, (lambda e, p_=pq, l=hT[:, kc, t * 128:(t + 1) * 128], r=waq[:, kc, :], st=(kc == 0), sp=(kc == 7): e.matmul(p_[:], lhsT=l, rhs=r, start=st, stop=sp)),
                     [hk, waqk], [pqk])
            pkv, pkvk = ps_get("mm2")
            for kc in range(8):
                P.op(PE, (lambda e, p_=pkv, l=hT[:, kc, t * 128:(t + 1) * 128], r=wakv[:, kc, :], st=(kc == 0), sp=(kc == 7): e.matmul(p_[:, 0:256], lhsT=l, rhs=r, start=st, stop=sp)),
                     [hk, wakvk], [pkvk])
            pv, pvk = ps_get("h2")
            for kc in range(8):
                P.op(PE, (lambda e, p_=pv, l=hT[:, kc, t * 128:(t + 1) * 128], r=whi[:, kc, :], st=(kc == 0), sp=(kc == 7): e.matmul(p_[:], lhsT=l, rhs=r, start=st, stop=sp)),
                     [hk, whik], [pvk])
            P.op(ACT, (lambda e, o=v_tok[:, t, :], i=pv[:]: e.activation(out=o, in_=i, func=AF.Copy)), [pvk], [("v_tok", t)])
            yield None
            P.op(ACT, (lambda e, p_=pq: e.activation(out=sq512[:, 0:512], in_=p_[:], func=AF.Square)), [pqk], [("sq512a", tp_)])
            P.op(ACT, (lambda e, p_=pkv: e.activation(out=sq512[:, 512:640], in_=p_[:, 0:128], func=AF.Square)), [pkvk], [("sq512b", tp_)])
            P.op(DVE, lambda e: e.tensor_reduce(out=ssq10[:], in_=sq512.rearrange("p (g d) -> p g d", d=64), axis=AX.X, op=ALU.add),
                 [("sq512a", tp_), ("sq512b", tp_)], [("ssq10", tp_)])
            yield None
            P.op(POOL, lambda e: e.tensor_scalar(out=ssq10[:], in0=ssq10[:], scalar1=1.0 / 64, scalar2=EPS, op0=ALU.mult, op1=ALU.add), [("ssq10", tp_)], [("ssq10", tp_)])
            P.op(POOL, lambda e: e.tensor_tensor(out=rstd10[:], in0=ssq10[:], in1=mhalf[:, 0:10], op=ALU.pow), [("ssq10", tp_), "mhalf"], [("rstd10", tp_)])
            yield None
            qn3 = qn.rearrange("p (g d) -> p g d", d=64)
            kn3 = kn.rearrange("p (g d) -> p g d", d=64)
            P.op(DVE, (lambda e, p_=pq: e.tensor_tensor(out=qn3, in0=p_[:].rearrange("p (g d) -> p g d", d=64),
                                                        in1=rstd10[:, 0:8].unsqueeze(2).broadcast_to([128, 8, 64]), op=ALU.mult)), [pqk, ("rstd10", tp_)], [("qn", tp_)])
            P.op(DVE, lambda e: e.tensor_tensor(out=qn3, in0=qn3, in1=qnw_bc[:].unsqueeze(1).broadcast_to([128, 8, 64]), op=ALU.mult), [("qn", tp_), "qnw_bc"], [("qn", tp_)])
            P.op(DVE, (lambda e, p_=pkv: e.tensor_tensor(out=kn3, in0=p_[:, 0:128].rearrange("p (g d) -> p g d", d=64),
                                                         in1=rstd10[:, 8:10].unsqueeze(2).broadcast_to([128, 2, 64]), op=ALU.mult)), [pkvk, ("rstd10", tp_)], [("kn", tp_)])
            P.op(DVE, lambda e: e.tensor_tensor(out=kn3, in0=kn3, in1=knw_bc[:].unsqueeze(1).broadcast_to([128, 2, 64]), op=ALU.mult), [("kn", tp_), "knw_bc"], [("kn", tp_)])
            kt_i = t + ktoff
            P.op(ACT, (lambda e, p_=pkv, o=v_ext[:, kt_i, :, 64:128]: e.activation(out=o, in_=p_[:, 128:256].rearrange("p (g d) -> p g d", g=2), func=AF.Copy)),
                 [pkvk, "v_ext_all"], [("v_ext", kt_i)])
            yield None
            if not latent:
                P.op(ACT, (lambda e, p_=pkv: e.activation(out=vraw, in_=p_[:, 128:256], func=AF.Copy)), [pkvk], [("vraw", tp_)])
                P.dma(nk_d[t * 128:(t + 1) * 128, :], kn, reads=[("kn", tp_)], eng=POOL)
                P.dma(nv_d[t * 128:(t + 1) * 128, :], vraw, reads=[("vraw", tp_)], eng=POOL)
                P.op(DVE, lambda e: e.tensor_copy(out=qr, in_=qn), [("qn", tp_)], [("qr", tp_)])
                for dup in range(2):
                    P.op(ACT, (lambda e, o=kdup[:, :, dup, :]: e.activation(out=o, in_=kn.rearrange("p (g d) -> p g d", g=2), func=AF.Copy)), [("kn", tp_)], [("kdup", dup, tp_)])
            else:
                P.dma(ropeC, ropeC_d[t * 128:(t + 1) * 128, :], writes=[("ropeC", tp_)])
                P.dma(ropeS, ropeS_d[t * 128:(t + 1) * 128, :], writes=[("ropeS", tp_)])
                q5 = qn.rearrange("p (g r h f) -> p g r h f", g=8, r=2, h=2)
                t25 = rt2.rearrange("p (g r h f) -> p g r h f", g=8, r=2, h=2)
                S4 = ropeS.rearrange("p (r h f) -> p r h f", r=2, h=2)
                P.op(DVE, lambda e: e.tensor_tensor(out=rt1.rearrange("p (g d) -> p g d", d=64), in0=qn3,
                                                    in1=ropeC.unsqueeze(1).broadcast_to([128, 8, 64]), op=ALU.mult), [("qn", tp_), ("ropeC", tp_)], [("rt1", tp_)])
                for hh in range(2):
                    P.op(DVE, (lambda e, o=t25[:, :, :, hh, :], i=q5[:, :, :, 1 - hh, :], s=S4[:, :, hh, :].unsqueeze(1).broadcast_to([128, 8, 2, 16]):
                               e.tensor_tensor(out=o, in0=i, in1=s, op=ALU.mult)), [("qn", tp_), ("ropeS", tp_)], [("rt2", hh, tp_)])
                P.op(DVE, lambda e: e.tensor_tensor(out=qr, in0=rt1, in1=rt2, op=ALU.add), [("rt1", tp_), ("rt2", 0, tp_), ("rt2", 1, tp_)], [("qr", tp_)])
                k5 = kn.rearrange("p (g r h f) -> p g r h f", g=2, r=2, h=2)
                k25 = kt2.rearrange("p (g r h f) -> p g r h f", g=2, r=2, h=2)
                P.op(DVE, lambda e: e.tensor_tensor(out=kt1.rearrange("p (g d) -> p g d", d=64), in0=kn3,
                                                     in1=ropeC.unsqueeze(1).broadcast_to([128, 2, 64]), op=ALU.mult), [("kn", tp_), ("ropeC", tp_)], [("kt1", tp_)])
                for hh in range(2):
                    P.op(DVE, (lambda e, o=k25[:, :, :, hh, :], i=k5[:, :, :, 1 - hh, :], s=S4[:, :, hh, :].unsqueeze(1).broadcast_to([128, 2, 2, 16]):
                                e.tensor_tensor(out=o, in0=i, in1=s, op=ALU.mult)), [("kn", tp_), ("ropeS", tp_)], [("kt2", hh, tp_)])
                for dup in range(2):
                    P.op(DVE, (lambda e, o=kdup[:, :, dup, :]: e.tensor_tensor(out=o, in0=kt1.rearrange("p (g d) -> p g d", g=2),
                                                                                in1=kt2.rearrange("p (g d) -> p g d", g=2), op=ALU.add)),
                         [("kt1", tp_), ("kt2", 0, tp_), ("kt2", 1, tp_)], [("kdup", dup, tp_)])
            yield None
            pT, pTk = ps_get("tp")
            pTb = pT.bitcast(BF16)
            for jp in range(4):
                P.op(PE, (lambda e, o=pTb[:, jp * 128:(jp + 1) * 128], i=qr[:, jp * 128:(jp + 1) * 128]: e.transpose(o, i, ident_b[:])), [("qr", tp_), "ident_b"], [pTk])
            for g in range(2):
                P.op(PE, (lambda e, o=pTb[:, (4 + g) * 128:(5 + g) * 128], i=kdup[:, g, :, :].rearrange("p a d -> p (a d)"): e.transpose(o, i, ident_b[:])),
                     [("kdup", 0, tp_), ("kdup", 1, tp_), "ident_b"], [pTk])
            P.op(ACT, (lambda e, o=qT[0][0:64, :, t * 128:(t + 1) * 128], i=pTb[0:64, 0:512].rearrange("p (k n) -> p k n", n=128): e.activation(out=o, in_=i, func=AF.Copy)),
                 [pTk, "qT_zero0"], [("qT", t, 0)])
            P.op(ACT, (lambda e, o=qT[1][64:128, :, t * 128:(t + 1) * 128], i=pTb[64:128, 0:512].rearrange("p (k n) -> p k n", n=128): e.activation(out=o, in_=i, func=AF.Copy)),
                 [pTk, "qT_zero1"], [("qT", t, 1)])
            P.op(DVE, (lambda e, o=kT2[:, :, koff + t * 128:koff + (t + 1) * 128], i=pTb[:, 512:768].rearrange("p (k n) -> p k n", n=128): e.tensor_copy(out=o, in_=i)),
                 [pTk], [("kT2", kt_i)])
        pipeline(aproj_tile, NT, 3)
        if ps_id == 0:
            dbg_dump("qT", qT[0].rearrange("p k n -> p (k n)"), [128, 4 * PASS_TOK], [("qT", t, 0) for t in range(NT)])
            dbg_dump("kT2", kT2.rearrange("p k n -> p (k n)"), [128, 2 * 1536], [("kT2", t) for t in range(12)])
        mark("p%d Aproj end" % ps_id)
        if stop_after == "Aproj":
            break
        free_block(waqk)
        free_block(wakvk)
        free_block(whik)
        whq, whqk = load_block(wcols(w_in_d, 0, 512), 8, 512, defer=True)
        whf = [None, None]
        whfk = [None, None]
        whf[0], whfk[0] = load_block(wcols(w_in_d, 512, 512), 8, 512, defer=True)
        whf[1], whfk[1] = load_block(wcols(w_in_d, 1024, 512), 8, 512, defer=True)
        whg, whgk = load_block(wcols(w_in_d, 2048, 512), 8, 512, defer=True)
        wgl = [None] * 4
        wglk = [None] * 4

        pti = 0
        stpair = [0]
        groups = []
        for (t0, ntl) in seqs:
            ktiles = (list(range(0, 4)) if latent else []) + [t0 + i + ktoff for i in range(ntl)]
            for g in range(2):
                for tq in range(t0, t0 + ntl):
                    groups.append((g, tq, ktiles))
        asteps = [(gi, si) for gi, (g, tq, kt_) in enumerate(groups) for si in range(len(kt_))]

        def emit_scores(step):
            gi, si = step
            g, tq, ktiles = groups[gi]
            s = ktiles[si]
            nonlocal_pti = pti_box[0]
            pb = pt[nonlocal_pti % 4]
            pbk = ("pt", nonlocal_pti % 4)
            pti_box[0] += 1
            kcol = slice(s * 128, (s + 1) * 128)
            st, stk = ps_get("st4")
            stpair[0] += 1
            for par in range(2):
                P.op(PE, (lambda e, o=st[:, par * 256:(par + 1) * 256], l=kT2[:, g, kcol], r=qT[par][:, 2 * g:2 * g + 2, tq * 128:(tq + 1) * 128]:
                          e.matmul(o, lhsT=l, rhs=r, start=True, stop=True, skip_group_check=True)),
                     [("kT2", s), ("qT", tq, par)], [stk])
            P.op(ACT, (lambda e, o=pb[:, 0:512], i=st[:, 0:512]: e.activation(out=o, in_=i, func=AF.Exp, scale=0.125)),
                 [stk], [(pbk, 0), (pbk, 1)])
            return pb, pbk

        pti_box = [0]
        LOOK = 2
        inflight = [emit_scores(asteps[k]) for k in range(min(LOOK, len(asteps)))]
        acc = acck = None
        for i, (gi, si) in enumerate(asteps):
            g, tq, ktiles = groups[gi]
            nkt = len(ktiles)
            s = ktiles[si]
            if si == 0:
                pump(2)
                acc, acck = ps_get("mm2")
            if i % 2 == 1:
                next(ada2, None)
            if i + LOOK < len(asteps):
                inflight.append(emit_scores(asteps[i + LOOK]))
            pb, pbk = inflight.pop(0)
            P.op(PE, (lambda e, a=acc, l=v_ext[:, s, g, 64:192], r=pb[:, 0:256], st_=(si == 0), sp=(si == nkt - 1):
                      e.matmul(a[:, 0:256], lhsT=l, rhs=r, start=st_, stop=sp, skip_group_check=True)),
                 [("v_ext", s), "v_ext_all", (pbk, 0)], [acck])
            P.op(PE, (lambda e, a=acc, l=v_ext[:, s, g, 0:128], r=pb[:, 256:512], sp=(si == nkt - 1):
                      e.matmul(a[:, 256:512], lhsT=l, rhs=r, start=False, stop=sp, skip_group_check=True)),
                 [("v_ext", s), "v_ext_all", (pbk, 1)], [acck])
            if si == nkt - 1:
                tqs = slice(tq * 128, (tq + 1) * 128)
                if latent:
                    P.op(DVE, lambda e, a=acc: e.reciprocal(out=recs[64:128, 0:256], in_=a[64:128, 0:256]), [acck], ["recs_a"])
                else:
                    P.op(ACT, lambda e, a=acc: e.activation(out=recs[64:128, 0:256], in_=a[64:128, 0:256], func=AF.Ln), [acck], ["recs_a"])
                    P.op(ACT, lambda e: e.activation(out=recs[64:128, 0:256], in_=recs[64:128, 0:256], func=AF.Exp, scale=-1.0), ["recs_a"], ["recs_a"])
                P.op(DVE, lambda e, a=acc: e.reciprocal(out=recs[0:64, 256:512], in_=a[0:64, 256:512]), [acck], ["recs_b"])
                P.op(DVE, (lambda e, a=acc, o=o_aT[0:64, 2 * g:2 * g + 2, tqs]: e.tensor_tensor(out=o, in0=a[0:64, 0:256].rearrange("p (k n) -> p k n", n=128),
                                                                                             in1=recs[64:128, 0:256].rearrange("p (k n) -> p k n", n=128), op=ALU.mult)),
                     [acck, "recs_a"], [("o_aT", tq, g, 0)])
                P.op(DVE, (lambda e, a=acc, o=o_aT[64:128, 2 * g:2 * g + 2, tqs]: e.tensor_tensor(out=o, in0=a[64:128, 256:512].rearrange("p (k n) -> p k n", n=128),
                                                                                               in1=recs[0:64, 256:512].rearrange("p (k n) -> p k n", n=128), op=ALU.mult)),
                     [acck, "recs_b"], [("o_aT", tq, g, 1)])
        flush()
        for _ in ada2:
            pass
        o_aT_keys = [("o_aT", tq, g, p_) for tq in range(NT) for g in range(2) for p_ in range(2)]
        if ps_id == 0:
            dbg_dump("o_aT", o_aT[:].rearrange("p k n -> p (k n)"), [128, 4 * PASS_TOK], o_aT_keys)
        mark("p%d attn end" % ps_id)
        if stop_after == "attn":
            break

        P.barrier()
        wgl[0], wglk[0] = load_block(wcols(w_in_d, 3328, 512), 8, 512, defer=True)
        wgl[2], wglk[2] = load_block(wcols(w_in_d, 3328 + 2 * 512, 512), 8, 512, defer=True)

        def proj_fm(wv, wk, h, half, group="mm"):
            pp, ppk = ps_get(group)
            for kc in range(8):
                P.op(PE, (lambda e, p_=pp, l=wv[:, kc, h * 128:(h + 1) * 128], r=hT[:, kc, half * 512:(half + 1) * 512], st=(kc == 0), sp=(kc == 7):
                          e.matmul(p_[:], lhsT=l, rhs=r, start=st, stop=sp)),
                     [wk] + hT_all[half * 4:half * 4 + 4], [ppk])
            return pp, ppk

        def prep_gen(h):
            hb = h % 2
            for half in range(2):
                hs = slice(half * 512, (half + 1) * 512)
                pp, ppk = proj_fm(whq, whqk, h, half)
                P.op(ACT, (lambda e, o=qTh[:, hs], i=pp[:]: e.activation(out=o, in_=i, func=AF.Copy, scale=128.0 ** -0.5)), [ppk], [("qTh", half)])
                yield None
            for d in range(2):
                for half in range(2):
                    hs = slice(half * 512, (half + 1) * 512)
                    pp, ppk = proj_fm(whf[d], whfk[d], h, half)
                    P.op(ACT, (lambda e, o=uT[:, hs], i=pp[:]: e.activation(out=o, in_=i, func=AF.Sigmoid, scale=-1.0)), [ppk], [("uT", half)])
                    yield None
                if d == 0:
                    assert hg_owner.get(hb) in (None, h - 2) and (h < 2 or norm_done.get(h - 2)), (h, hg_owner, norm_done)
                    hg_owner[hb] = h
                    for half in range(2):
                        hs = slice(half * 512, (half + 1) * 512)
                        pp, ppk = proj_fm(whg, whgk, h, half)
                        P.op(ACT, (lambda e, o=hgTh[hb][:, hs], i=pp[:]: e.activation(out=o, in_=i, func=AF.Silu)), [ppk], [("hgTh", hb, half)])
                        yield None
                ukeys = [("uT", 0), ("uT", 1)]
                P.op(DVE, (lambda e, s=oml[:, d * 4 + h:d * 4 + h + 1]: e.tensor_scalar(out=uT, in0=uT, scalar1=s, scalar2=None, op0=ALU.mult)),
                     ukeys + ["oml"], ukeys)
                P.op(ACT, lambda e: e.activation(out=lf, in_=uT, func=AF.Ln, scale=-1.0, bias=1.0), ukeys, ["lf"])
                yield None
                if d == 0:
                    P.op(DVE, lambda e: e.tensor_tensor_scan(out=aT, data0=smask_f[:], data1=lf, initial=0.0, op0=ALU.mult, op1=ALU.add),
                         ["smask_f", "lf"], ["aT"])
                else:
                    P.op(DVE, lambda e: e.tensor_tensor_scan(out=aT[:, ::-1], data0=smask_f[:], data1=lf[:, ::-1], initial=0.0, op0=ALU.mult, op1=ALU.add),
                         ["smask_f", "lf"], ["aT"])
                yield None
                P.op(ACT, lambda e: e.activation(out=lf, in_=aT, func=AF.Exp, scale=-1.0), ["aT"], ["lf"])
                P.op(ACT, lambda e: e.activation(out=aT, in_=aT, func=AF.Exp), ["aT", "lf"], ["aT"])
                yield None
                a3 = aT.rearrange("p (c t) -> p c t", t=CH)
                dsel = (CH - 1) if d == 0 else 0
                P.op(POOL, (lambda e, o=decs[hb * 2 + d][:], i=a3[:, :, dsel]: e.tensor_copy(out=o, in_=i)), ["aT"], [("decs", hb, d)])
                P.op(DVE, (lambda e, o=qd[hb * 2 + d]: e.tensor_tensor(out=o, in0=qTh, in1=aT, op=ALU.mult)), [("qTh", 0), ("qTh", 1), "aT"], [("qd", hb, d)])
                yield None
                P.op(DVE, (lambda e, o=kd[hb * 2 + d]: e.tensor_tensor(out=o, in0=uT, in1=lf, op=ALU.mult)), ukeys + ["lf"], [("kd", hb, d)])
                yield None
                pT, pTk = ps_get("tp1")
                pTb = pT.bitcast(BF16)
                for t in range(NT):
                    P.op(PE, (lambda e, o=pTb[:, t * 128:(t + 1) * 128], i=kd[hb * 2 + d][:, t * 128:(t + 1) * 128]: e.transpose(o, i, ident_b[:])), [("kd", hb, d), "ident_b"], [pTk])
                P.op(ACT, (lambda e, o=kdtok[hb * 2 + d], i=pTb[:, 0:1024].rearrange("p (k n) -> p k n", n=128): e.activation(out=o, in_=i, func=AF.Copy)), [pTk], [("kdtok", hb, d)])
                yield None

        def scan_gen(h, pre_norm=None):
            hb = h % 2
            nround = [0]

            def state_chain(d, si_, t0, ntl, slot):
                Tl = Tst[2 * slot:2 * slot + 2]
                dk_ = ("decs", hb, d)
                chunks = list(range(t0 * 2, (t0 + ntl) * 2))
                if d == 1:
                    chunks = chunks[::-1]
                Sin = Sinit[d]
                Sink = ("Sinit", d)
                S0 = S0f[d]
                S0k = ("S0f", d)
                if latent:
                    P.dma(S0, s0_d[d * 4 + h, :, :], writes=[S0k])
                    P.op(ACT, (lambda e, o=Sin, i=S0: e.activation(out=o, in_=i, func=AF.Copy)), [S0k], [Sink])
                tcur = None
                tcurk = None
                prev_dec = None
                for i, c in enumerate(chunks):
                    t, n = c // 2, c % 2
                    pk, pkk = ps_get("kvr")
                    P.op(PE, (lambda e, p_=pk, l=kdtok[hb * 2 + d][n * CH:(n + 1) * CH, t, :], r=v_tok[n * CH:(n + 1) * CH, t, h * 128:(h + 1) * 128]:
                              e.matmul(p_[:, 0:128], lhsT=l, rhs=r, start=True, stop=True)),
                         [("kdtok", hb, d), ("v_tok", t)], [pkk])
                    dec = decs[hb * 2 + d][:, c:c + 1]
                    tn = Tl[i % 2]
                    tnk = ("Tst", slot, i % 2)
                    if i == 0:
                        if latent:
                            P.op(DVE, (lambda e, o=tn, p_=pk, i_=S0: e.tensor_tensor(out=o, in0=p_[:, 0:128], in1=i_, op=ALU.add)), [pkk, S0k], [tnk])
                        else:
                            P.op(DVE, (lambda e, o=tn, p_=pk: e.tensor_copy(out=o, in_=p_[:, 0:128])), [pkk], [tnk])
                    else:
                        P.op(DVE, (lambda e, o=tn, i_=tcur, s=prev_dec, p_=pk: e.scalar_tensor_tensor(out=o, in0=i_, scalar=s, in1=p_[:, 0:128], op0=ALU.mult, op1=ALU.add)),
                             [tcurk, dk_, pkk], [tnk])
                    tcur, tcurk, prev_dec = tn, tnk, dec
                    if i % 2 == 0:
                        P.op(ACT, (lambda e, o=Sall[:, d, c, :], i_=tcur, s=dec: e.activation(out=o, in_=i_, func=AF.Copy, scale=s)), [tcurk, dk_], [("Sall", d, c)])
                    else:
                        P.op(DVE, (lambda e, o=Sall[:, d, c, :], i_=tcur, s=dec: e.tensor_scalar(out=o, in0=i_, scalar1=s, scalar2=None, op0=ALU.mult)), [tcurk, dk_], [("Sall", d, c)])
                    yield None
                if not latent:
                    sfin = sfin_l[sfin_i[0] % 4]
                    sfk = ("sfin", sfin_i[0] % 4)
                    sfin_i[0] += 1
                    P.op(DVE, (lambda e, o=sfin, i_=tcur, s=prev_dec: e.tensor_scalar(out=o, in0=i_, scalar1=s, scalar2=None, op0=ALU.mult)), [tcurk, dk_], [sfk])
                    P.dma(ns_d[si_, d * 4 + h, :, :], sfin, reads=[sfk], eng=POOL)

            seq_groups = [seqs] if latent else [seqs[0:2], seqs[2:4]]
            sbase = 0
            for grp in seq_groups:
                gens = []
                for gi_, (t0, ntl) in enumerate(grp):
                    for d in range(2):
                        gens.append(state_chain(d, sbase + gi_, t0, ntl, gi_ * 2 + d))
                sbase += len(grp)
                active = list(gens)
                while active:
                    for gi2, gch in enumerate(list(active)):
                        try:
                            next(gch)
                        except StopIteration:
                            active.remove(gch)
                        if gi2 == 1 and len(gens) > 2:
                            yield None
                    pump(1)
                    nround[0] += 1
                    if nround[0] == 2 and pre_norm is not None:
                        pre_norm()
                    yield None

            steps = [(t0, ntl, t, d) for (t0, ntl) in seqs for t in range(t0, t0 + ntl) for d in range(2)]
            smi = [0]

            def emit_scores(step):
                t0, ntl, t, d = step
                ts_ = slice(t * 128, (t + 1) * 128)
                mask = maskf if d == 0 else maskb
                mkey = "maskf" if d == 0 else "maskb"
                sc, sck = ps_get("h1")
                P.op(PE, (lambda e, p_=sc, l=kd[hb * 2 + d][:, ts_], r=qd[hb * 2 + d][:, ts_]: e.matmul(p_[:, 0:128], lhsT=l, rhs=r, start=True, stop=True)),
                     [("kd", hb, d), ("qd", hb, d)], [sck])
                sm = smk[smi[0] % 4]
                smkk = ("smk", smi[0] % 4)
                smi[0] += 1
                P.op(DVE, (lambda e, o=sm, p_=sc, m=mask: e.tensor_tensor(out=o, in0=p_[:, 0:128], in1=m[:], op=ALU.mult)), [sck, mkey], [smkk])
                return sm, smkk

            OLOOK = 2
            infl = [emit_scores(steps[k]) for k in range(min(OLOOK, len(steps)))]
            po = pok = None
            for i, (t0, ntl, t, d) in enumerate(steps):
                if i + OLOOK < len(steps):
                    infl.append(emit_scores(steps[i + OLOOK]))
                sm, smkk = infl.pop(0)
                ts_ = slice(t * 128, (t + 1) * 128)
                if d == 0:
                    po, pok = ps_get("h2")
                P.op(PE, (lambda e, p_=po, l=v_tok[:, t, h * 128:(h + 1) * 128], r=sm, st=(d == 0): e.matmul(p_[:, 0:128], lhsT=l, rhs=r, start=st, stop=False, skip_group_check=True)),
                     [("v_tok", t), smkk], [pok])
                for n in range(2):
                    c = t * 2 + n
                    cs = slice(c * CH, (c + 1) * CH)
                    if d == 0:
                        cprev = c - 1 if c > t0 * 2 else None
                    else:
                        cprev = c + 1 if c < (t0 + ntl) * 2 - 1 else None
                    if cprev is None:
                        if not latent:
                            continue
                        Sb, Sbk = Sinit[d], ("Sinit", d)
                    else:
                        Sb, Sbk = Sall[:, d, cprev, :], ("Sall", d, cprev)
                    P.op(PE, (lambda e, o=po[:, n * CH:(n + 1) * CH], l=Sb, r=qd[hb * 2 + d][:, cs]:
                              e.matmul(o, lhsT=l, rhs=r, start=False, stop=True, skip_group_check=True)),
                         [Sbk, ("qd", hb, d)], [pok])
                if d == 1:
                    P.op(ACT, (lambda e, o=oT[:, ts_], p_=po: e.activation(out=o, in_=p_[:, 0:128], func=AF.Copy)), [pok], [("oT", t)])
                    pump(2)
                    yield None

        def norm_head(h):
            hb = h % 2
            assert hg_owner.get(hb) == h, (h, hg_owner)
            norm_done[h] = True
            for half in range(2):
                hs = slice(half * 512, (half + 1) * 512)
                okeys = [("oT", t) for t in range(half * 4, half * 4 + 4)]
                P.op(POOL, (lambda e, i=oT[:, hs]: e.tensor_tensor(out=osq, in0=i, in1=i, op=ALU.mult)), okeys, ["osq"])
                pn, pnk = ps_get("mm")
                P.op(PE, (lambda e, p_=pn: e.matmul(p_[:], lhsT=ones_b[:], rhs=osq, start=True, stop=True)), ["ones_b", "osq"], [pnk])
                P.op(ACT, (lambda e, p_=pn: e.activation(out=ors, in_=p_[:], func=AF.Ln, scale=1.0 / 128, bias=epsc[:, 0:1])), [pnk, "epsc"], ["ors"])
                P.op(ACT, lambda e: e.activation(out=ors, in_=ors, func=AF.Exp, scale=-0.5), ["ors"], ["ors"])
                P.op(DVE, (lambda e, i=oT[:, hs]: e.scalar_tensor_tensor(out=ot1, in0=i, scalar=hnw[:, 0:1], in1=ors, op0=ALU.mult, op1=ALU.mult)),
                     okeys + ["hnw", "ors"], ["ot1"])
                P.op(DVE, (lambda e, o=o_hT[:, h, hs], g_=hgTh[hb][:, hs]: e.tensor_tensor(out=o, in0=ot1, in1=g_, op=ALU.mult)), ["ot1", ("hgTh", hb, half)], [("o_hT", h, half)])

        def drain(g):
            for _ in g:
                pass

        def interleave(ga, gb, na=1, nb=1):
            da = db = False
            while not (da and db):
                for _ in range(na):
                    if not da:
                        try:
                            next(ga)
                        except StopIteration:
                            da = True
                for _ in range(nb):
                    if not db:
                        try:
                            next(gb)
                        except StopIteration:
                            db = True

        hg_owner = {}
        norm_done = {}
        drain(prep_gen(0))
        if ps_id == 0:
            dbg_dump("qd0", qd[0], [128, PASS_TOK], [("qd", 0, 0)])
            dbg_dump("kd1", kd[1], [128, PASS_TOK], [("kd", 0, 1)])
            dbg_dump("decs1", decs[1][:], [128, 16], [("decs", 0, 1)])
        for h in range(4):
            pn_ = (lambda hh=h - 1: norm_head(hh)) if h > 0 else None
            if h < 3:
                interleave(scan_gen(h, pn_), prep_gen(h + 1), 1, 1)
            else:
                for k_ in (whqk, whfk[0], whfk[1], whgk):
                    free_block(k_)
                who, whok = load_block(w_ho_d.rearrange("(k p) n -> p k n", p=128), 4, 1024, defer=True)
                wao, waok = load_block(w_ao_d.rearrange("(k p) n -> p k n", p=128), 4, 1024, defer=True)
                wgl[1], wglk[1] = load_block(wcols(w_in_d, 3328 + 1 * 512, 512), 8, 512, defer=True)
                wgl[3], wglk[3] = load_block(wcols(w_in_d, 3328 + 3 * 512, 512), 8, 512, defer=True)
                drain(scan_gen(h, pn_))
        norm_head(3)
        o_hT_keys = [("o_hT", h, half) for h in range(4) for half in range(2)]
        if ps_id == 0:
            dbg_dump("o_hT", o_hT[:].rearrange("p k n -> p (k n)"), [128, 4 * PASS_TOK], o_hT_keys)
        mark("p%d hgrn end" % ps_id)
        if stop_after == "hgrn":
            break

        P.barrier()
        expand(mods[:, 16:24, j], ["mods_b"], bcS, "bcS")
        wo = [None, None]
        wok = [None, None]
        for f in range(8):
            if f == 4:
                flush()
                free_block(wglk[0])
                free_block(wglk[2])
                for chh in range(2):
                    wo[chh], wok[chh] = load_block(wcols(w_out_d, chh * 512, 512), 8, 512, fold_bc=bcS[:, chh * 512:(chh + 1) * 512], fold_key="bcS", defer=True)
            for half in range(2):
                hs = slice(half * 512, (half + 1) * 512)
                hkeys = hT_all[half * 4:half * 4 + 4]
                fcs = slice((f % 4) * 128, (f % 4 + 1) * 128)
                pg0, pg0k = ps_get("mm")
                for kc in range(8):
                    P.op(PE, (lambda e, p_=pg0, l=wgl[f // 4][:, kc, fcs], r=hT[:, kc, hs], st=(kc == 0), sp=(kc == 7): e.matmul(p_[:], lhsT=l, rhs=r, start=st, stop=sp)),
                         [wglk[f // 4]] + hkeys, [pg0k])
                P.op(ACT, (lambda e, p_=pg0: e.activation(out=g0, in_=p_[:], func=AF.Sigmoid)), [pg0k], ["g0"])
                pg1, pg1k = ps_get("mm")
                for kc in range(8):
                    P.op(PE, (lambda e, p_=pg1, l=wgl[2 + f // 4][:, kc, fcs], r=hT[:, kc, hs], st=(kc == 0), sp=(kc == 7): e.matmul(p_[:], lhsT=l, rhs=r, start=st, stop=sp)),
                         [wglk[2 + f // 4]] + hkeys, [pg1k])
                P.op(ACT, (lambda e, p_=pg1: e.activation(out=g1, in_=p_[:], func=AF.Sigmoid)), [pg1k], ["g1"])
                ph, phk = ps_get("mm2")
                for hh in range(4):
                    P.op(PE, (lambda e, p_=ph, l=who[:, hh, f * 128:(f + 1) * 128], r=o_hT[:, hh, hs], st=(hh == 0), sp=(hh == 3): e.matmul(p_[:], lhsT=l, rhs=r, start=st, stop=sp)),
                         [whok, ("o_hT", hh, half)], [phk])
                pa, pak = ps_get("kv")
                for pr_ in range(4):
                    P.op(PE, (lambda e, p_=pa, l=wao[:, pr_, f * 128:(f + 1) * 128], r=o_aT[:, pr_, hs], st=(pr_ == 0), sp=(pr_ == 3): e.matmul(p_[:], lhsT=l, rhs=r, start=st, stop=sp)),
                         [waok] + [k for k in o_aT_keys if k[1] // 4 == half], [pak])
                P.op(DVE, (lambda e, p_=ph: e.tensor_tensor(out=m0, in0=p_[:], in1=g0, op=ALU.mult)), [phk, "g0"], ["m0"])
                P.op(DVE, (lambda e, p_=pa: e.tensor_tensor(out=m1, in0=p_[:], in1=g1, op=ALU.mult)), [pak, "g1"], ["m1"])
                P.op(POOL, (lambda e, o=mT[:, f, hs]: e.tensor_tensor(out=o, in0=m0, in1=m1, op=ALU.add)), ["m0", "m1"], [("mT", f, half)])
                pump(1)
        flush()
        if ps_id == 0:
            dbg_dump("mT", mT.rearrange("p k n -> p (k n)"), [128, 8 * PASS_TOK], [("mT", f, half) for f in range(8) for half in range(2)])
        mark("p%d C1 end" % ps_id)
        if stop_after == "C1":
            break

        for k_ in (wglk[1], wglk[3], whok, waok):
            free_block(k_)
        expand(mods[:, 40:48, j], ["mods_b"], bcG2, "bcG2")

        def load_w1(fg):
            r = [load_block(wcols(w_ff1_d, fg * 1024 + b * 512, 512), 8, 512, defer=True, cast_eng=ACT) for b in range(2)]
            return [x[0] for x in r], [x[1] for x in r]

        def load_w2(fg):
            r = [load_block(wcols(w_ff2_d, chh * 512, 512, r0=fg * 1024), 8, 512, fold_bc=bcG2[:, chh * 512:(chh + 1) * 512], fold_key="bcG2", defer=True) for chh in range(2)]
            return [x[0] for x in r], [x[1] for x in r]

        w1_next = load_w1(0)
        w2_next = load_w2(0)
        P.op(DVE, (lambda e, j=j: e.scalar_tensor_tensor(out=vecT[:], in0=mods[:, 32:40, j], scalar=1.0, in1=nfwT[:], op0=ALU.add, op1=ALU.mult)),
             ["mods_b", "nfwT"], ["vecT"])
        expand(vecT, ["vecT"], bcA, "bcA")
        expand(mods[:, 24:32, j], ["mods_b"], bcS, "bcS")
        n2 = pipeline_gen(lambda t: norm_mod_transpose(t, x1[:, t, :], [("x1", t, 0), ("x1", t, 1)], h1_d), NT, 2)
        for t in range(NT):
            xb = xt[t % 2]
            xk = ("xt", t % 2)
            P.dma(xb[:], x_d[row0 + t * 128:row0 + (t + 1) * 128, :], writes=[xk])
            if t > 0:
                next(n2, None)
                next(n2, None)
            for chh in range(2):
                po, pok = ps_get("mm")
                for kc in range(8):
                    P.op(PE, (lambda e, p_=po, l=mT[:, kc, t * 128:(t + 1) * 128], r=wo[chh][:, kc, :], st=(kc == 0), sp=(kc == 7): e.matmul(p_[:], lhsT=l, rhs=r, start=st, stop=sp)),
                         [("mT", kc, t // 4), wok[chh]], [pok])
                P.op(DVE, (lambda e, p_=po, o=x1[:, t, chh * 512:(chh + 1) * 512], i=xb[:, chh * 512:(chh + 1) * 512]: e.tensor_tensor(out=o, in0=p_[:], in1=i, op=ALU.add)),
                     [pok, xk], [("x1", t, chh)])
                pump(1)
        flush()
        for _ in n2:
            pass
        if ps_id == 0:
            dbg_dump("x1", x1.rearrange("p k n -> p (k n)"), [128, NT * D], [("x1", t, c) for t in range(NT) for c in range(2)])
        mark("p%d C2 end" % ps_id)
        if stop_after == "C2":
            break

        free_block(wok[0])
        free_block(wok[1])
        rli = 0

        def final_norm_tile(t):
            p = t % 2
            xk2 = [("x1", t, 0), ("x1", t, 1)]
            P.op(ACT, (lambda e, i=x1[:, t, :]: e.activation(out=junk[:], in_=i, func=AF.Square, accum_out=ssq[:, 8 + p:9 + p])), xk2, ["junk", ("ssqF", p)])
            P.op(POOL, lambda e: e.tensor_scalar(out=rstd[:, 10 + 2 * p:11 + 2 * p], in0=ssq[:, 8 + p:9 + p], scalar1=1.0 / D, scalar2=EPS, op0=ALU.mult, op1=ALU.add), [("ssqF", p)], [("rstd1F", p)])
            P.op(POOL, lambda e: e.tensor_tensor(out=rstd[:, 11 + 2 * p:12 + 2 * p], in0=rstd[:, 10 + 2 * p:11 + 2 * p], in1=mhalf[:, 0:1], op=ALU.pow), [("rstd1F", p), "mhalf"], [("rstdF", p)])
            yb = h1_d[p]
            yk = ("h1", p)
            P.op(DVE, (lambda e, i=x1[:, t, :], o=yb[:]: e.scalar_tensor_tensor(out=o, in0=i, scalar=rstd[:, 11 + 2 * p:12 + 2 * p], in1=fnw_bc, op0=ALU.mult, op1=ALU.mult)),
                 xk2 + [("rstdF", p), "fnw_bc"], [yk])
            P.dma(y_d[row0 + t * 128:row0 + (t + 1) * 128, :], yb[:], reads=[yk], eng=POOL)

        for fg in range(4):
            w1, w1k = w1_next
            w2, w2k = w2_next
            if fg < 3:
                w1_next = load_w1(fg + 1)
            for ffc in range(8):
                for half in range(2):
                    hs = slice(half * 512, (half + 1) * 512)
                    pf, pfk = ps_get("mm")
                    for kc in range(8):
                        P.op(PE, (lambda e, p_=pf, l=w1[ffc // 4][:, kc, (ffc % 4) * 128:(ffc % 4 + 1) * 128], r=hT[:, kc, hs], st=(kc == 0), sp=(kc == 7):
                                  e.matmul(p_[:], lhsT=l, rhs=r, start=st, stop=sp)),
                             [w1k[ffc // 4]] + hT_all[half * 4:half * 4 + 4], [pfk])
                    rb = rl[rli % 2]
                    rbk = ("rl", rli % 2)
                    rli += 1
                    P.op(ACT, (lambda e, p_=pf, o=rb: e.activation(out=o, in_=p_[:], func=AF.Relu)), [pfk], [rbk])
                    P.op(DVE, (lambda e, o=aTf[:, ffc, hs], i=rb: e.tensor_tensor(out=o, in0=i, in1=i, op=ALU.mult)), [rbk], [("mT", ffc, half)])
                    pump(1)
            flush()
            free_block(w1k[0])
            free_block(w1k[1])
            if fg < 3:
                w2_next = load_w2(fg + 1)
            elif ps_id == 0:
                a0_next = a0_stage_gen(1)
            if fg == 3:
                P.dma(fnw_bc, fnw_d.partition_broadcast(128), writes=["fnw_bc"])
            for t in range(NT):
                for chh in range(2):
                    if fg == 3 and a0_next is not None:
                        next(a0_next, None)
                    p2, p2k = ps_get("tp")
                    for ffc in range(8):
                        P.op(PE, (lambda e, p_=p2, l=aTf[:, ffc, t * 128:(t + 1) * 128], r=w2[chh][:, ffc, :], st=(ffc == 0), sp=(ffc == 7): e.matmul(p_[:], lhsT=l, rhs=r, start=st, stop=sp)),
                             [("mT", ffc, t // 4), w2k[chh]], [p2k])
                    xs_ = x1[:, t, chh * 512:(chh + 1) * 512]
                    P.op(DVE, (lambda e, p_=p2, o=xs_: e.tensor_tensor(out=o, in0=p_[:], in1=o, op=ALU.add)), [p2k, ("x1", t, chh)], [("x1", t, chh)])
                    pump(1)
                if fg == 3 and ps_id == 1:
                    final_norm_tile(t)
            flush()
            free_block(w2k[0])
            free_block(w2k[1])
        if a0_next is not None:
            for _ in a0_next:
                pass
        if ps_id == 0:
            for t in range(NT):
                final_norm_tile(t)
        mark("p%d D end" % ps_id)
        if stop_after == "pass0":
            break

    P.finalize()
    return nc, dbg_out


def _consts():
    ident = np.eye(128, dtype=np.float32)
    j = np.arange(128)[:, None]
    i = np.arange(128)[None, :]
    same = (j // CH) == (i // CH)
    maskf = (same & (j <= i)).astype(np.float32)
    maskb = (same & (j >= i)).astype(np.float32)
    n_tokens = 1024
    gw = 64
    row = np.repeat(np.arange(n_tokens // gw, dtype=np.float32), gw)
    col = np.tile(np.arange(gw, dtype=np.float32), n_tokens // gw)
    axis_dim = 32
    freqs = (np.float32(10000.0) ** (-np.arange(0, axis_dim, 2, dtype=np.float32) / np.float32(axis_dim))).astype(np.float32)
    ang_r = (row[:, None] * freqs).astype(np.float32)
    ang_c = (col[:, None] * freqs).astype(np.float32)
    cr, sr = np.cos(ang_r).astype(np.float32), np.sin(ang_r).astype(np.float32)
    cc, sc = np.cos(ang_c).astype(np.float32), np.sin(ang_c).astype(np.float32)
    C = np.concatenate([cr, cr, cc, cc], axis=1).astype(np.float32)
    S = np.concatenate([-sr, sr, -sc, sc], axis=1).astype(np.float32)
    return ident, maskf, maskb, C, S


def make_in_maps(inp):
    f = lambda a: np.ascontiguousarray(np.asarray(a, dtype=np.float32))
    ident, maskf, maskb, C, S = _consts()
    shared = {
        "w_ada": f(inp["w_ada"][0]),
        "b_adaT": f(np.asarray(inp["b_ada"][0]).reshape(48, 128).T),
        "nmwT": f(np.asarray(inp["norm_mix_w"][0]).reshape(8, 128).T),
        "nfwT": f(np.asarray(inp["norm_ffn_w"][0]).reshape(8, 128).T),
        "fnw": f(np.asarray(inp["final_norm_w"]).reshape(1, D)),
        "w_in": f(inp["w_in"][0]),
        "qnw": f(np.asarray(inp["q_norm_w"][0]).reshape(1, 64)),
        "knw": f(np.asarray(inp["k_norm_w"][0]).reshape(1, 64)),
        "lblT": f(np.asarray(inp["hgrn_lb_logits"]).reshape(2, 2, 4, 128).transpose(3, 0, 1, 2).reshape(128, 16)),
        "hnw": f(np.asarray(inp["hgrn_norm_w"][0]).reshape(128, 1)),
        "w_ho": f(inp["w_hgrn_out"][0]),
        "w_ao": f(inp["w_attn_out"][0]),
        "w_out": f(inp["w_out"][0]),
        "w_ff1": f(inp["w_ff1"][0]),
        "w_ff2": f(inp["w_ff2"][0]),
        "ident": ident, "maskf": maskf, "maskb": maskb, "ropeC": C, "ropeS": S,
    }
    xp = np.asarray(inp["x_prompt"], dtype=np.float32)
    xs = np.asarray(inp["x_sample"], dtype=np.float32)
    c = np.asarray(inp["c"], dtype=np.float32)
    cctx = np.asarray(inp["c_ctx"], dtype=np.float32)
    maps = []
    for i in range(8):
        m = dict(shared)
        m["x"] = np.ascontiguousarray(np.concatenate([xs[i], xp[4 * i:4 * i + 4].reshape(1024, D)], axis=0))
        m["ck"] = f(np.asarray(inp["cache_k"])[i, 0].reshape(512, 128))
        m["cv"] = f(np.asarray(inp["cache_v"])[i, 0].reshape(512, 128))
        m["s0"] = f(np.asarray(inp["state_hgrn"])[i, 0].reshape(8, 128, 128))
        cond = np.stack([c[i], cctx], axis=1)
        m["condT"] = np.ascontiguousarray(cond.reshape(8, 128, 2).transpose(1, 0, 2).reshape(128, 16))
        maps.append(m)
    return maps


_NC_CACHE = {}


def kernel(**inputs):
    if "nc" not in _NC_CACHE:
        _NC_CACHE["nc"] = build_program()[0]
    nc = _NC_CACHE["nc"]
    maps = make_in_maps(inputs)
    res = run_bass_kernel_spmd(nc, maps, core_ids=list(range(8)))
    rs = res.results
    y_s = np.stack([rs[i]["y"][0:1024] for i in range(8)], axis=0)
    y_p = np.concatenate([rs[i]["y"][1024:2048].reshape(4, 256, D) for i in range(8)], axis=0)
    nk = np.concatenate([rs[i]["nk"].reshape(4, 1, 256, 2, 64) for i in range(8)], axis=0)
    nv = np.concatenate([rs[i]["nv"].reshape(4, 1, 256, 2, 64) for i in range(8)], axis=0)
    ns = np.concatenate([rs[i]["ns"].reshape(4, 1, 2, 4, 128, 128) for i in range(8)], axis=0)
    return (y_p.astype(np.float32), y_s.astype(np.float32), nk.astype(np.float32),
            nv.astype(np.float32), ns.astype(np.float32))
```

```python
from contextlib import ExitStack
import numpy as np
import concourse.bass as bass
import concourse.mybir as mybir
from concourse.bass_utils import run_bass_kernel_spmd

F32 = mybir.dt.float32
BF16 = mybir.dt.bfloat16
AF = mybir.ActivationFunctionType
ALU = mybir.AluOpType
AX = mybir.AxisListType

PE, ACT, DVE, POOL, SP = "tensor", "scalar", "vector", "gpsimd", "sync"
MARKS = []
STRICT_SAME_ENGINE = True
COMPUTE = (PE, ACT, DVE, POOL)
N_DMA_SEMS = 24

D = 1024
EPS = 1e-6
NT = 8
PASS_TOK = 1024
CH = 64
D_IN = 5376


class Op:
    __slots__ = ("idx", "eng", "fn", "reads", "writes", "is_dma", "deps", "need_inc",
                 "sem", "semval", "dma_prev", "phase", "real_writes")

    def __init__(self, idx, eng, fn, reads, writes, is_dma):
        self.phase = 0
        self.idx = idx
        self.eng = eng
        self.fn = fn
        self.reads = reads
        self.writes = writes
        self.is_dma = is_dma
        self.deps = []
        self.need_inc = False
        self.sem = None
        self.semval = 0
        self.dma_prev = None


class Prog:
    def __init__(self, nc):
        self.nc = nc
        self.ops = []
        self.stack = ExitStack()
        self.phase = 0
        self.max_ops = None

    def barrier(self):
        self.phase += 1

    def sb(self, name, shape, dtype):
        return self.stack.enter_context(self.nc.sbuf_tensor("sb_" + name, list(shape), dtype))

    def ps(self, name, shape, dtype):
        return self.stack.enter_context(self.nc.psum_tensor(name, list(shape), dtype))

    def op(self, eng, fn, reads=(), writes=(), dma=False):
        rw = tuple(writes)
        weff = rw + tuple(k for k in reads if isinstance(k, tuple) and k and k[0] == "ps" and k not in rw)
        o = Op(len(self.ops), eng, fn, tuple(reads), weff, dma)
        o.real_writes = rw
        o.phase = self.phase
        if self.max_ops is not None and len(self.ops) >= self.max_ops:
            return o
        self.ops.append(o)
        return o

    def dma(self, out, in_, reads=(), writes=(), eng=SP, **kw):
        return self.op(eng, lambda e: e.dma_start(out=out, in_=in_, **kw), reads, writes, dma=True)

    def finalize(self):
        nc = self.nc
        last_w = {}
        readers = {}
        for o in self.ops:
            deps = set()
            for k in o.reads:
                w = last_w.get(k)
                if w is not None:
                    deps.add(w)
            for k in o.writes:
                w = last_w.get(k)
                if w is not None:
                    deps.add(w)
                for r in readers.get(k, ()):
                    deps.add(r)
            for k in o.writes:
                last_w[k] = o
                readers[k] = []
            for k in o.reads:
                if k in o.writes:
                    continue
                lst = readers.setdefault(k, [])
                if not o.is_dma:
                    lst[:] = [r for r in lst if r.is_dma or r.eng != o.eng]
                lst.append(o)
            deps.discard(o)
            for d in deps:
                if (not d.is_dma) and (not o.is_dma) and d.eng == o.eng:
                    if o.eng == PE:
                        continue
                    raw = any(k in d.real_writes for k in o.reads)
                    if not raw and not STRICT_SAME_ENGINE:
                        continue
                o.deps.append(d)
                d.need_inc = True

        last_comp = {}
        last_dma = {}
        nd0 = 0
        seen_phase = {}
        snap = {}
        cur_phase = 0
        for o in self.ops:
            if o.phase != cur_phase:
                cur_phase = o.phase
                snap[cur_phase] = (dict(last_comp), dict(last_dma))
            if o.phase > 0 and seen_phase.get(o.eng, 0) < o.phase:
                seen_phase[o.eng] = o.phase
                lc, ld = snap[o.phase]
                for d in list(lc.values()) + list(ld.values()):
                    if d is not o and d not in o.deps:
                        if (not d.is_dma) and d.eng == o.eng and not o.is_dma:
                            continue
                        o.deps.append(d)
                        d.need_inc = True
            if o.is_dma:
                last_dma[nd0 % N_DMA_SEMS] = o
                nd0 += 1
            else:
                last_comp[o.eng] = o

        eng_sem = {e: self.stack.enter_context(nc.semaphore("s_" + e)) for e in COMPUTE}
        dma_sems = [self.stack.enter_context(nc.semaphore("d%d" % i)) for i in range(N_DMA_SEMS)]
        cnt = {e: 0 for e in COMPUTE}
        dcnt = [0] * N_DMA_SEMS
        dlast = [None] * N_DMA_SEMS
        nd = 0
        for o in self.ops:
            if o.is_dma:
                s = nd % N_DMA_SEMS
                nd += 1
                o.dma_prev = dlast[s]
                dcnt[s] += 16
                o.sem = dma_sems[s]
                o.semval = dcnt[s]
                dlast[s] = o
                o.need_inc = True
            elif o.need_inc:
                cnt[o.eng] += 1
                o.sem = eng_sem[o.eng]
                o.semval = cnt[o.eng]

        per_eng = {}
        for o in self.ops:
            per_eng.setdefault(o.eng, []).append(o)
        final_waits = [(s, c) for s, c in zip(dma_sems, dcnt) if c > 0]

        def emit(eng_name, engine):
            waited = {}
            for o in per_eng.get(eng_name, []):
                need = {}
                for d in o.deps:
                    if need.get(d.sem, 0) < d.semval:
                        need[d.sem] = d.semval
                if o.is_dma and o.dma_prev is not None:
                    d = o.dma_prev
                    if need.get(d.sem, 0) < d.semval:
                        need[d.sem] = d.semval
                for sem, val in need.items():
                    if waited.get(sem, 0) >= val:
                        continue
                    engine.wait_ge(sem, val)
                    waited[sem] = val
                ins = o.fn(engine)
                if o.need_inc:
                    ins.then_inc(o.sem, 16 if o.is_dma else 1)
            if eng_name == SP:
                for s, c in final_waits:
                    engine.wait_ge(s, c)

        with nc.Block() as block:
            @block.sync
            def _(e):
                emit(SP, e)

            @block.tensor
            def _(e):
                emit(PE, e)

            @block.scalar
            def _(e):
                emit(ACT, e)

            @block.vector
            def _(e):
                emit(DVE, e)

            @block.gpsimd
            def _(e):
                emit(POOL, e)
        self.stack.close()


def build_program(stop_after=None, dbg=(), max_ops=None):
    nc = bass.Bass("TRN2", target_bir_lowering=False)
    P = Prog(nc)
    P.max_ops = max_ops
    MARKS.clear()

    def mark(name):
        MARKS.append((name, sum(1 for o in P.ops if o.eng == PE)))

    def din(name, shape):
        return nc.dram_tensor(name, list(shape), F32, kind="ExternalInput").ap()

    def dout(name, shape):
        return nc.dram_tensor(name, list(shape), F32, kind="ExternalOutput").ap()

    x_d = din("x", [2048, D])
    ck_d = din("ck", [512, 128])
    cv_d = din("cv", [512, 128])
    s0_d = din("s0", [8, 128, 128])
    condT_d = din("condT", [128, 16])
    w_ada_d = din("w_ada", [D, 6 * D])
    b_adaT_d = din("b_adaT", [128, 48])
    nmwT_d = din("nmwT", [128, 8])
    nfwT_d = din("nfwT", [128, 8])
    fnw_d = din("fnw", [1, D])
    w_in_d = din("w_in", [D, D_IN])
    qnw_d = din("qnw", [1, 64])
    knw_d = din("knw", [1, 64])
    lblT_d = din("lblT", [128, 16])
    hnw_d = din("hnw", [128, 1])
    w_ho_d = din("w_ho", [512, D])
    w_ao_d = din("w_ao", [512, D])
    w_out_d = din("w_out", [D, D])
    w_ff1_d = din("w_ff1", [D, 4 * D])
    w_ff2_d = din("w_ff2", [4 * D, D])
    ident_d = din("ident", [128, 128])
    maskf_d = din("maskf", [128, 128])
    maskb_d = din("maskb", [128, 128])
    ropeC_d = din("ropeC", [1024, 64])
    ropeS_d = din("ropeS", [1024, 64])

    y_d = dout("y", [2048, D])
    nk_d = dout("nk", [1024, 128])
    nv_d = dout("nv", [1024, 128])
    ns_d = dout("ns", [4, 8, 128, 128])
    dbg_out = {}

    def dbg_dump(name, ap, shape, reads):
        if name in dbg:
            d = nc.dram_tensor("dbg_" + name, list(shape), ap.dtype, kind="ExternalOutput").ap()
            dbg_out[name] = d
            P.dma(d, ap, reads=reads)

    NSLOT = 6
    W = [P.sb("W%d" % i, [128, 4096], BF16) for i in range(NSLOT)]
    NSTG = 3
    STG = [P.sb("stg%d" % i, [128, 1024], F32) for i in range(NSTG)]
    hT = P.sb("hT", [128, 8, PASS_TOK], BF16)
    o_aT = P.sb("o_aT", [128, 4, PASS_TOK], BF16)
    o_hT = P.sb("o_hT", [128, 4, PASS_TOK], BF16)
    mods = P.sb("mods", [128, 48, 2], F32)
    bcA = P.sb("bcA", [128, D], F32)
    bcS = P.sb("bcS", [128, D], F32)
    vecT = P.sb("vecT", [128, 8], F32)
    ident_f = P.sb("ident_f", [128, 128], F32)
    ident_b = P.sb("ident_b", [128, 128], BF16)
    ones_f = P.sb("ones_f", [128, 128], F32)
    ones_b = P.sb("ones_b", [128, 128], BF16)
    maskf = P.sb("maskf", [128, 128], BF16)
    maskb = P.sb("maskb", [128, 128], BF16)
    mstage = P.sb("mstage", [128, 128], F32)
    mhalf = P.sb("mhalf", [128, 16], F32)
    epsc = P.sb("epsc", [128, 1], F32)
    bcG2 = P.sb("bcG2", [128, D], F32)
    smask_f = P.sb("smask_f", [128, PASS_TOK], F32)
    dg = P.sb("dg", [128, 4, 128], F32)
    condT = P.sb("condT", [128, 16], F32)
    sil = P.sb("sil", [128, 16], BF16)
    b_adaT = P.sb("b_adaT", [128, 48], F32)
    nmwT = P.sb("nmwT", [128, 8], F32)
    nfwT = P.sb("nfwT", [128, 8], F32)
    qnw_bc = P.sb("qnw_bc", [128, 64], F32)
    knw_bc = P.sb("knw_bc", [128, 64], F32)
    lblT = P.sb("lblT", [128, 16], F32)
    oml = P.sb("oml", [128, 8], F32)
    hnw = P.sb("hnw", [128, 1], F32)
    xt = [P.sb("xt%d" % i, [128, D], F32) for i in range(2)]
    junk = P.sb("junk", [128, D], BF16)
    ssq = P.sb("ssq", [128, 16], F32)
    rstd = P.sb("rstd", [128, 16], F32)
    ssq10_l = [P.sb("ssq10_%d" % i, [128, 10], F32) for i in range(2)]
    rstd10_l = [P.sb("rstd10_%d" % i, [128, 10], F32) for i in range(2)]
    decs = [P.sb("decs%d" % i, [128, 16], F32) for i in range(4)]
    htok = [P.sb("htok%d" % i, [128, D], BF16) for i in range(2)]
    ARENA_COLS = 14848 + 1024 + 3584 + 512
    arena = P.sb("arena", [128, ARENA_COLS], F32)

    class Carver:
        def __init__(self):
            self.off = 0

        def _shape(self, v, shape):
            if len(shape) == 2:
                return v
            names = ["a", "b", "c", "d"][:len(shape) - 1]
            pat = "p (%s) -> p %s" % (" ".join(names), " ".join(names))
            kw = {n: s for n, s in zip(names[1:], shape[2:])}
            return v.rearrange(pat, **kw)

        def f32(self, shape):
            n = int(np.prod(shape[1:]))
            v = arena[:, self.off:self.off + n]
            self.off += n
            assert self.off <= ARENA_COLS, self.off
            return self._shape(v, shape)

        def bf16(self, shape):
            n = int(np.prod(shape[1:]))
            nf = (n + 1) // 2
            v = arena[:, self.off:self.off + nf].bitcast(BF16)[:, 0:n]
            self.off += nf
            assert self.off <= ARENA_COLS, self.off
            return self._shape(v, shape)

    PSALL = P.ps("psall", [128, 8 * 512], F32)
    PSB = [PSALL[:, i * 512:(i + 1) * 512] for i in range(8)]
    ps_rot = {"mm": [0, 1], "mm2": [2, 3], "kv": [4, 5], "tp": [4, 5], "h1": [6, 2, 3], "h2": [7, 5], "acc": [2, 3, 6, 7], "kvr": [2, 3], "tp1": [4], "st4": [0, 4, 1, 5]}
    ps_idx = {k: 0 for k in ps_rot}

    def ps_get(group):
        lst = ps_rot[group]
        b = lst[ps_idx[group] % len(lst)]
        ps_idx[group] += 1
        return PSB[b], ("ps", b)

    cv_ = Carver()
    cv_.off = 17920
    h1_a = [cv_.f32([128, D]) for i in range(2)]
    cv_ = Carver()
    v_tok = cv_.bf16([128, NT, 512])
    qT = [cv_.bf16([128, 4, PASS_TOK]) for i in range(2)]
    kT2 = cv_.bf16([128, 2, 1536])
    v_ext = cv_.bf16([128, 12, 2, 192])
    ropeC_l = [cv_.f32([128, 64]) for i in range(2)]
    ropeS_l = [cv_.f32([128, 64]) for i in range(2)]
    sq512_l = [cv_.f32([128, 640]) for i in range(2)]
    qn_l = [cv_.f32([128, 512]) for i in range(2)]
    kn_l = [cv_.f32([128, 128]) for i in range(2)]
    vraw_l = [cv_.f32([128, 128]) for i in range(2)]
    rt1_l = [cv_.f32([128, 512]) for i in range(2)]
    rt2_l = [cv_.f32([128, 512]) for i in range(2)]
    qr_l = [cv_.bf16([128, 512]) for i in range(2)]
    kt1_l = [cv_.f32([128, 128]) for i in range(2)]
    kt2_l = [cv_.f32([128, 128]) for i in range(2)]
    kdup_l = [cv_.bf16([128, 2, 2, 64]) for i in range(2)]
    ckst = cv_.f32([128, 4, 128])
    ckdup = cv_.bf16([128, 4, 2, 128])
    pt = [cv_.bf16([128, 512]) for i in range(4)]
    recs = cv_.f32([128, 512])
    cv_ = Carver()
    _v_tok_same = cv_.bf16([128, NT, 512])
    qTh = cv_.bf16([128, PASS_TOK])
    hgTh = [cv_.bf16([128, PASS_TOK]) for i in range(2)]
    uT = cv_.f32([128, PASS_TOK])
    lf = cv_.f32([128, PASS_TOK])
    aT = cv_.f32([128, PASS_TOK])
    qd = [cv_.bf16([128, PASS_TOK]) for i in range(4)]
    kd = [cv_.bf16([128, PASS_TOK]) for i in range(4)]
    kdtok = [cv_.bf16([128, NT, 128]) for i in range(4)]
    oT = cv_.f32([128, PASS_TOK])
    osq = cv_.bf16([128, 512])
    ors = cv_.f32([128, 512])
    ot1 = cv_.f32([128, 512])
    Tst = [cv_.f32([128, 128]) for i in range(8)]
    Sall = cv_.bf16([128, 2, 16, 128])
    Sinit = [cv_.bf16([128, 128]) for i in range(2)]
    S0f = [cv_.f32([128, 128]) for i in range(2)]
    sfin_l = [cv_.f32([128, 128]) for i in range(4)]
    sfin_i = [0]
    smk = [cv_.bf16([128, 128]) for i in range(4)]
    cv_ = Carver()
    x1 = cv_.f32([128, NT, D])
    g0 = cv_.f32([128, 512])
    g1 = cv_.f32([128, 512])
    m0 = cv_.f32([128, 512])
    m1 = cv_.f32([128, 512])
    mT = cv_.bf16([128, 8, PASS_TOK])
    aTf = mT
    rl = [cv_.bf16([128, 512]) for i in range(2)]
    h1_d = [cv_.f32([128, D]) for i in range(2)]
    fnw_bc = cv_.f32([128, D])
    assert cv_.off <= 17920, cv_.off

    stg_i = [0]
    free_slots = list(range(NSLOT))

    def free_block(wkey):
        free_slots.append(wkey[1])

    pending = []

    def pump(n=1):
        for _ in range(n):
            if pending:
                pending.pop(0)()

    def flush():
        while pending:
            pending.pop(0)()

    def load_block(src3, nk, ncols, fold_bc=None, fold_key=None, defer=False, cast_eng=DVE, queue=None):
        slot = free_slots.pop(0)
        wv = W[slot][:, 0:nk * ncols].rearrange("p (k n) -> p k n", n=ncols)
        kper = max(1, 1024 // ncols)
        wkey = ("W", slot)

        def step(k0, kk):
            gi = stg_i[0] % NSTG
            stg_i[0] += 1
            sv = STG[gi][:, 0:kk * ncols].rearrange("p (k n) -> p k n", n=ncols)
            P.dma(sv, src3[:, k0:k0 + kk, :], writes=[("stg", gi)])
            if fold_bc is None:
                if cast_eng == ACT:
                    P.op(ACT, (lambda e, o=wv[:, k0:k0 + kk, :], i=sv: e.activation(out=o, in_=i, func=AF.Copy)),
                         [("stg", gi)], [wkey])
                else:
                    P.op(DVE, (lambda e, o=wv[:, k0:k0 + kk, :], i=sv: e.tensor_copy(out=o, in_=i)),
                         [("stg", gi)], [wkey])
            else:
                fb = fold_bc.unsqueeze(1).broadcast_to([128, kk, ncols])
                P.op(DVE, (lambda e, o=wv[:, k0:k0 + kk, :], i=sv, f=fb: e.tensor_tensor(out=o, in0=i, in1=f, op=ALU.mult)),
                     [("stg", gi), fold_key], [wkey])

        for k0 in range(0, nk, kper):
            kk = min(kper, nk - k0)
            if defer:
                (pending if queue is None else queue).append(lambda k0=k0, kk=kk: step(k0, kk))
            else:
                step(k0, kk)
        return wv, wkey

    def wcols(w_d, c0, ncols, r0=0, nrows=1024):
        return w_d[r0:r0 + nrows, c0:c0 + ncols].rearrange("(k p) n -> p k n", p=128)

    P.dma(ident_f[:], ident_d, writes=["ident_f"])
    P.op(DVE, lambda e: e.tensor_copy(out=ident_b[:], in_=ident_f[:]), ["ident_f"], ["ident_b"])
    P.op(POOL, lambda e: e.memset(ones_f[:], 1.0), [], ["ones_f"])
    P.op(POOL, lambda e: e.memset(ones_b[:], 1.0), [], ["ones_b"])
    P.op(POOL, lambda e: e.memset(mhalf[:], -0.5), [], ["mhalf"])
    P.op(POOL, lambda e: e.memset(epsc[:], EPS), [], ["epsc"])
    P.dma(mstage[:], maskf_d, writes=["mstage"])
    P.op(DVE, lambda e: e.tensor_copy(out=maskf[:], in_=mstage[:]), ["mstage"], ["maskf"])
    P.dma(mstage[:], maskb_d, reads=[], writes=["mstage"])
    P.op(DVE, lambda e: e.tensor_copy(out=maskb[:], in_=mstage[:]), ["mstage"], ["maskb"])
    P.op(POOL, lambda e: e.memset(smask_f[:], 1.0), [], ["smask_f"])
    P.op(POOL, lambda e: e.memset(smask_f[:].rearrange("p (c t) -> p c t", t=CH)[:, :, 0:1], 0.0), ["smask_f"], ["smask_f"])
    P.dma(condT[:], condT_d, writes=["condT"])
    P.dma(b_adaT[:], b_adaT_d, writes=["b_adaT"])
    P.dma(nmwT[:], nmwT_d, writes=["nmwT"])
    P.dma(nfwT[:], nfwT_d, writes=["nfwT"])
    P.dma(qnw_bc[:], qnw_d.partition_broadcast(128), writes=["qnw_bc"])
    P.dma(knw_bc[:], knw_d.partition_broadcast(128), writes=["knw_bc"])
    P.dma(lblT[:], lblT_d, writes=["lblT"])
    P.dma(hnw[:], hnw_d, writes=["hnw"])
    lv = lblT[:].rearrange("p (d l h) -> p d l h", d=2, l=2)
    P.op(DVE, lambda e: e.tensor_tensor(out=oml[:].rearrange("p (d h) -> p d h", d=2), in0=lv[:, :, 1, :], in1=lv[:, :, 0, :], op=ALU.subtract),
         ["lblT"], ["oml"])
    P.op(ACT, lambda e: e.activation(out=oml[:], in_=oml[:], func=AF.Sigmoid), ["oml"], ["oml"])

    P.op(ACT, lambda e: e.activation(out=sil[:], in_=condT[:], func=AF.Silu), ["condT"], ["sil"])
    silv = sil[:].rearrange("p (k j) -> p k j", j=2)
    pM, pMk = ps_get("h2")

    def ada_block(blk, pM_, pMk_, first_chunk, slack=0):
        q_ = []
        wv, wkey = load_block(wcols(w_ada_d, blk * 512, 512), 8, 512, defer=True, queue=q_)
        for st_ in q_:
            st_()
            yield None
        for _ in range(slack):
            yield None
        for cc in range(4):
            chunk = blk * 4 + cc
            for kc in range(8):
                P.op(PE, (lambda e, o=pM_[:, chunk * 2:chunk * 2 + 2], l=wv[:, kc, cc * 128:(cc + 1) * 128], r=silv[:, kc, :], st=(kc == 0 and chunk == first_chunk), sp=(kc == 7):
                          e.matmul(o, lhsT=l, rhs=r, start=st, stop=sp, skip_group_check=True)),
                     [wkey, "sil"], [pMk_])
            yield None
        free_block(wkey)

    for blk in range(4):
        for _ in ada_block(blk, pM, pMk, 0):
            pass
    P.op(DVE, lambda e: e.tensor_tensor(out=mods[:, 0:16, :], in0=pM[:, 0:32].rearrange("p (c j) -> p c j", j=2),
                                        in1=b_adaT[:, 0:16].unsqueeze(2).broadcast_to([128, 16, 2]), op=ALU.add),
         [pMk, "b_adaT"], ["mods_a"])
    pM2, pM2k = PSB[6], ("ps", 6)

    def ada2_gen():
        for blk in range(4, 12):
            for _ in ada_block(blk, pM2, pM2k, 16, slack=4):
                yield None
        P.op(DVE, lambda e: e.tensor_tensor(out=mods[:, 16:48, :], in0=pM2[:, 32:96].rearrange("p (c j) -> p c j", j=2),
                                            in1=b_adaT[:, 16:48].unsqueeze(2).broadcast_to([128, 32, 2]), op=ALU.add),
             [pM2k, "b_adaT"], ["mods_b"])

    ada2 = ada2_gen()
    if "mods" in dbg:
        for _ in ada2:
            pass
    dbg_dump("mods", mods[:].rearrange("p c j -> p (c j)"), [128, 96], ["mods_a", "mods_b"])

    def expand(vec_ap, vec_keys, dst, dst_key):
        for half in range(2):
            pE, pEk = ps_get("h2")
            for c4 in range(4):
                fc = half * 4 + c4
                P.op(DVE, (lambda e, o=dg[:, c4, :], s=vec_ap[:, fc:fc + 1]: e.tensor_scalar(out=o, in0=ident_f[:], scalar1=s, scalar2=None, op0=ALU.mult)),
                     ["ident_f"] + list(vec_keys), [("dg", c4)])
                P.op(PE, (lambda e, o=pE[:, c4 * 128:(c4 + 1) * 128], r=dg[:, c4, :]: e.matmul(o, lhsT=ones_f[:], rhs=r, start=True, stop=True)),
                     ["ones_f", ("dg", c4)], [pEk])
            P.op(ACT, (lambda e, o=dst[:, half * 512:(half + 1) * 512], i=pE[:]: e.activation(out=o, in_=i, func=AF.Copy)),
                 [pEk], [dst_key])

    mark("adaln end")
    if stop_after == "adaln":
        P.finalize()
        return nc, dbg_out

    def pipeline_gen(make_gen, n, gap):
        active = []
        started = 0
        rnd = 0
        while started < n or active:
            if started < n and rnd % gap == 0:
                active.append(make_gen(started))
                started += 1
            for g_ in list(active):
                try:
                    next(g_)
                except StopIteration:
                    active.remove(g_)
            rnd += 1
            yield None

    def pipeline(make_gen, n, gap):
        for _ in pipeline_gen(make_gen, n, gap):
            pass

    def norm_mod_transpose(t, src_ap, src_keys, h1l, pre=None):
        p = t % 2
        h1 = h1l[p]
        if pre is not None:
            pre()
        P.op(ACT, (lambda e, i=src_ap: e.activation(out=junk[:], in_=i, func=AF.Square, accum_out=ssq[:, p:p + 1])), src_keys, ["junk", ("ssq", p)])
        yield None
        P.op(POOL, lambda e: e.tensor_scalar(out=rstd[:, 2 + p:3 + p], in0=ssq[:, p:p + 1], scalar1=1.0 / D, scalar2=EPS, op0=ALU.mult, op1=ALU.add), [("ssq", p)], [("rstd1", p)])
        P.op(POOL, lambda e: e.tensor_tensor(out=rstd[:, p:p + 1], in0=rstd[:, 2 + p:3 + p], in1=mhalf[:, 0:1], op=ALU.pow), [("rstd1", p), "mhalf"], [("rstd", p)])
        P.op(DVE, (lambda e, i=src_ap: e.scalar_tensor_tensor(out=h1, in0=i, scalar=rstd[:, p:p + 1], in1=bcA[:], op0=ALU.mult, op1=ALU.mult)),
             list(src_keys) + [("rstd", p), "bcA"], [("h1", p)])
        yield None
        hb = htok[p]
        hk = ("htok", p)
        P.op(DVE, (lambda e, o=hb[:]: e.tensor_tensor(out=o, in0=h1, in1=bcS[:], op=ALU.add)), [("h1", p), "bcS"], [hk])
        yield None
        pT, pTk = ps_get("tp")
        pTb = pT.bitcast(BF16)
        for kc in range(8):
            P.op(PE, (lambda e, o=pTb[:, kc * 128:(kc + 1) * 128], i=hb[:, kc * 128:(kc + 1) * 128]: e.transpose(o, i, ident_b[:])),
                 [hk, "ident_b"], [pTk])
        P.op(ACT, (lambda e, o=hT[:, :, t * 128:(t + 1) * 128], i=pTb[:, 0:1024].rearrange("p (k n) -> p k n", n=128): e.activation(out=o, in_=i, func=AF.Copy)),
             [pTk], [("hT", t)])

    def rms_rstd_only(src_ap, src_keys):
        P.op(ACT, (lambda e, i=src_ap: e.activation(out=junk[:], in_=i, func=AF.Square, accum_out=ssq[:, 8:9])), src_keys, ["junk", "ssqF"])
        P.op(POOL, lambda e: e.tensor_scalar(out=rstd[:, 9:10], in0=ssq[:, 8:9], scalar1=1.0 / D, scalar2=EPS, op0=ALU.mult, op1=ALU.add), ["ssqF"], ["rstd1F"])
        P.op(POOL, lambda e: e.tensor_tensor(out=rstd[:, 8:9], in0=rstd[:, 9:10], in1=mhalf[:, 0:1], op=ALU.pow), ["rstd1F", "mhalf"], ["rstdF"])

    hT_all = [("hT", t) for t in range(NT)]
    a0_res = {}

    def a0_stage_gen(pid):
        jj = pid
        r0 = pid * PASS_TOK
        P.op(DVE, (lambda e: e.scalar_tensor_tensor(out=vecT[:], in0=mods[:, 8:16, jj], scalar=1.0, in1=nmwT[:], op0=ALU.add, op1=ALU.mult)),
             ["mods_a", "nmwT"], ["vecT"])
        expand(vecT, ["vecT"], bcA, "bcA")
        expand(mods[:, 0:8, jj], ["mods_a"], bcS, "bcS")
        yield None
        a0_res[pid] = (load_block(wcols(w_in_d, 2560, 512), 8, 512, defer=True), load_block(wcols(w_in_d, 3072, 256), 8, 256, defer=True),
                       load_block(wcols(w_in_d, 1536, 512), 8, 512, defer=True))

        def a0_gen(t):
            xb = xt[t % 2]
            xk = ("xt", t % 2)
            return norm_mod_transpose(t, xb[:], [xk], h1_a,
                                      pre=lambda: P.dma(xb[:], x_d[r0 + t * 128:r0 + (t + 1) * 128, :], writes=[xk]))
        for _ in pipeline_gen(a0_gen, NT, 2):
            pump(1)
            yield None
        flush()

    a0_next = None

    for ps_id in range(2):
        latent = (ps_id == 0)
        j = ps_id
        row0 = ps_id * PASS_TOK
        seqs = [(0, 8)] if latent else [(0, 2), (2, 2), (4, 2), (6, 2)]
        koff = 512 if latent else 0
        ktoff = 4 if latent else 0

        if a0_next is None:
            a0_next = a0_stage_gen(ps_id)
        for _ in a0_next:
            pass
        a0_next = None
        (waq, waqk), (wakv, wakvk), (whi, whik) = a0_res[ps_id]
        if ps_id == 1:
            dbg_dump("hT1", hT[:].rearrange("p k n -> p (k n)"), [128, 8 * PASS_TOK], hT_all)
        P.barrier()
        P.op(POOL, lambda e: e.memset(v_ext[:], 1.0), [], ["v_ext_all"])
        P.op(POOL, lambda e: e.memset(qT[0][64:128], 0.0), [], ["qT_zero0"])
        P.op(POOL, lambda e: e.memset(qT[1][0:64], 0.0), [], ["qT_zero1"])
        if ps_id == 0:
            dbg_dump("hT", hT[:].rearrange("p k n -> p (k n)"), [128, 8 * PASS_TOK], hT_all)
        mark("p%d A0 end" % ps_id)
        if stop_after == "A0":
            break

        if latent:
            P.dma(ckst[:], ck_d.rearrange("(t p) c -> p t c", p=128), writes=["ckst"])
            ckv = ckst.rearrange("p t (g d) -> p t g d", g=2)
            for dup in range(2):
                P.op(DVE, (lambda e, o=ckdup[:, :, :, dup * 64:(dup + 1) * 64]: e.tensor_copy(out=o, in_=ckv)), ["ckst"], [("ckdup", dup)])
            pT, pTk = ps_get("tp")
            pTb = pT.bitcast(BF16)
            for kt in range(4):
                for g in range(2):
                    P.op(PE, (lambda e, o=pTb[:, (kt * 2 + g) * 128:(kt * 2 + g + 1) * 128], i=ckdup[:, kt, g, :]: e.transpose(o, i, ident_b[:])),
                         [("ckdup", 0), ("ckdup", 1), "ident_b"], [pTk])
            P.op(ACT, (lambda e, o=kT2[:, :, 0:512].rearrange("p g (t n) -> p g t n", n=128),
                       i=pTb[:, 0:1024].rearrange("p (t g n) -> p g t n", g=2, n=128): e.activation(out=o, in_=i, func=AF.Copy)),
                 [pTk], [("kT2", kt) for kt in range(4)])
            P.dma(ckst[:], cv_d.rearrange("(t p) c -> p t c", p=128), writes=["ckst"])
            P.op(DVE, lambda e: e.tensor_copy(out=v_ext[:, 0:4, :, 64:128], in_=ckst.rearrange("p t (g d) -> p t g d", g=2)),
                 ["ckst", "v_ext_all"], [("v_ext", kt) for kt in range(4)])
        def aproj_tile(t):
            hk = ("hT", t)
            tp_ = t % 2
            ropeC, ropeS, sq512, qn, kn, vraw, rt1, rt2, qr, kt1, kt2, kdup = (ropeC_l[tp_], ropeS_l[tp_], sq512_l[tp_], qn_l[tp_], kn_l[tp_], vraw_l[tp_],
                                                                               rt1_l[tp_], rt2_l[tp_], qr_l[tp_], kt1_l[tp_], kt2_l[tp_], kdup_l[tp_])
            ssq10, rstd10 = ssq10_l[tp_], rstd10_l[tp_]
            pq, pqk = ps_get("mm")
            for kc in range(8):
                P.op(PE, (lambda e, p_=pq, l=hT[:, kc, t * 128:(t + 1) * 128], r=waq[:, kc, :], st=(kc == 0), sp=(kc == 7): e.matmul(p_[:], lhsT=l, rhs=r, start=st, stop=sp)),
                     [hk, waqk], [pqk])
            pkv, pkvk = ps_get("mm2")
            for kc in range(8):
                P.op(PE, (lambda e, p_=pkv, l=hT[:, kc, t * 128:(t + 1) * 128], r=wakv[:, kc, :], st=(kc == 0), sp=(kc == 7): e.matmul(p_[:, 0:256], lhsT=l, rhs=r, start=st, stop=sp)),
                     [hk, wakvk], [pkvk])
            pv, pvk = ps_get("h2")
            for kc in range(8):
                P.op(PE, (lambda e, p_=pv, l=hT[:, kc, t * 128:(t + 1) * 128], r=whi[:, kc, :], st=(kc == 0), sp=(kc == 7): e.matmul(p_[:], lhsT=l, rhs=r, start=st, stop=sp)),
                     [hk, whik], [pvk])
            P.op(ACT, (lambda e, o=v_tok[:, t, :], i=pv[:]: e.activation(out=o, in_=i, func=AF.Copy)), [pvk], [("v_tok", t)])
            yield None
            P.op(ACT, (lambda e, p_=pq: e.activation(out=sq512[:, 0:512], in_=p_[:], func=AF.Square)), [pqk], [("sq512a", tp_)])
            P.op(ACT, (lambda e, p_=pkv: e.activation(out=sq512[:, 512:640], in_=p_[:, 0:128], func=AF.Square)), [pkvk], [("sq512b", tp_)])
            P.op(DVE, lambda e: e.tensor_reduce(out=ssq10[:], in_=sq512.rearrange("p (g d) -> p g d", d=64), axis=AX.X, op=ALU.add),
                 [("sq512a", tp_), ("sq512b", tp_)], [("ssq10", tp_)])
            yield None
            P.op(POOL, lambda e: e.tensor_scalar(out=ssq10[:], in0=ssq10[:], scalar1=1.0 / 64, scalar2=EPS, op0=ALU.mult, op1=ALU.add), [("ssq10", tp_)], [("ssq10", tp_)])
            P.op(POOL, lambda e: e.tensor_tensor(out=rstd10[:], in0=ssq10[:], in1=mhalf[:, 0:10], op=ALU.pow), [("ssq10", tp_), "mhalf"], [("rstd10", tp_)])
            yield None
            qn3 = qn.rearrange("p (g d) -> p g d", d=64)
            kn3 = kn.rearrange("p (g d) -> p g d", d=64)
            P.op(DVE, (lambda e, p_=pq: e.tensor_tensor(out=qn3, in0=p_[:].rearrange("p (g d) -> p g d", d=64),
                                                        in1=rstd10[:, 0:8].unsqueeze(2).broadcast_to([128, 8, 64]), op=ALU.mult)), [pqk, ("rstd10", tp_)], [("qn", tp_)])
            P.op(DVE, lambda e: e.tensor_tensor(out=qn3, in0=qn3, in1=qnw_bc[:].unsqueeze(1).broadcast_to([128, 8, 64]), op=ALU.mult), [("qn", tp_), "qnw_bc"], [("qn", tp_)])
            P.op(DVE, (lambda e, p_=pkv: e.tensor_tensor(out=kn3, in0=p_[:, 0:128].rearrange("p (g d) -> p g d", d=64),
                                                         in1=rstd10[:, 8:10].unsqueeze(2).broadcast_to([128, 2, 64]), op=ALU.mult)), [pkvk, ("rstd10", tp_)], [("kn", tp_)])
            P.op(DVE, lambda e: e.tensor_tensor(out=kn3, in0=kn3, in1=knw_bc[:].unsqueeze(1).broadcast_to([128, 2, 64]), op=ALU.mult), [("kn", tp_), "knw_bc"], [("kn", tp_)])
            kt_i = t + ktoff
            P.op(ACT, (lambda e, p_=pkv, o=v_ext[:, kt_i, :, 64:128]: e.activation(out=o, in_=p_[:, 128:256].rearrange("p (g d) -> p g d", g=2), func=AF.Copy)),
                 [pkvk, "v_ext_all"], [("v_ext", kt_i)])
            yield None
            if not latent:
                P.op(ACT, (lambda e, p_=pkv: e.activation(out=vraw, in_=p_[:, 128:256], func=AF.Copy)), [pkvk], [("vraw", tp_)])
                P.dma(nk_d[t * 128:(t + 1) * 128, :], kn, reads=[("kn", tp_)])
                P.dma(nv_d[t * 128:(t + 1) * 128, :], vraw, reads=[("vraw", tp_)])
                P.op(DVE, lambda e: e.tensor_copy(out=qr, in_=qn), [("qn", tp_)], [("qr", tp_)])
                for dup in range(2):
                    P.op(ACT, (lambda e, o=kdup[:, :, dup, :]: e.activation(out=o, in_=kn.rearrange("p (g d) -> p g d", g=2), func=AF.Copy)), [("kn", tp_)], [("kdup", dup, tp_)])
            else:
                P.dma(ropeC, ropeC_d[t * 128:(t + 1) * 128, :], writes=[("ropeC", tp_)])
                P.dma(ropeS, ropeS_d[t * 128:(t + 1) * 128, :], writes=[("ropeS", tp_)])
                q5 = qn.rearrange("p (g r h f) -> p g r h f", g=8, r=2, h=2)
                t25 = rt2.rearrange("p (g r h f) -> p g r h f", g=8, r=2, h=2)
                S4 = ropeS.rearrange("p (r h f) -> p r h f", r=2, h=2)
                P.op(DVE, lambda e: e.tensor_tensor(out=rt1.rearrange("p (g d) -> p g d", d=64), in0=qn3,
                                                    in1=ropeC.unsqueeze(1).broadcast_to([128, 8, 64]), op=ALU.mult), [("qn", tp_), ("ropeC", tp_)], [("rt1", tp_)])
                for hh in range(2):
                    P.op(DVE, (lambda e, o=t25[:, :, :, hh, :], i=q5[:, :, :, 1 - hh, :], s=S4[:, :, hh, :].unsqueeze(1).broadcast_to([128, 8, 2, 16]):
                               e.tensor_tensor(out=o, in0=i, in1=s, op=ALU.mult)), [("qn", tp_), ("ropeS", tp_)], [("rt2", hh, tp_)])
                P.op(DVE, lambda e: e.tensor_tensor(out=qr, in0=rt1, in1=rt2, op=ALU.add), [("rt1", tp_), ("rt2", 0, tp_), ("rt2", 1, tp_)], [("qr", tp_)])
                k5 = kn.rearrange("p (g r h f) -> p g r h f", g=2, r=2, h=2)
                k25 = kt2.rearrange("p (g r h f) -> p g r h f", g=2, r=2, h=2)
                P.op(DVE, lambda e: e.tensor_tensor(out=kt1.rearrange("p (g d) -> p g d", d=64), in0=kn3,
                                                     in1=ropeC.unsqueeze(1).broadcast_to([128, 2, 64]), op=ALU.mult), [("kn", tp_), ("ropeC", tp_)], [("kt1", tp_)])
                for hh in range(2):
                    P.op(DVE, (lambda e, o=k25[:, :, :, hh, :], i=k5[:, :, :, 1 - hh, :], s=S4[:, :, hh, :].unsqueeze(1).broadcast_to([128, 2, 2, 16]):
                                e.tensor_tensor(out=o, in0=i, in1=s, op=ALU.mult)), [("kn", tp_), ("ropeS", tp_)], [("kt2", hh, tp_)])
                for dup in range(2):
                    P.op(DVE, (lambda e, o=kdup[:, :, dup, :]: e.tensor_tensor(out=o, in0=kt1.rearrange("p (g d) -> p g d", g=2),
                                                                                in1=kt2.rearrange("p (g d) -> p g d", g=2), op=ALU.add)),
                         [("kt1", tp_), ("kt2", 0, tp_), ("kt2", 1, tp_)], [("kdup", dup, tp_)])
            yield None
            pT, pTk = ps_get("tp")
            pTb = pT.bitcast(BF16)
            for jp in range(4):
                P.op(PE, (lambda e, o=pTb[:, jp * 128:(jp + 1) * 128], i=qr[:, jp * 128:(jp + 1) * 128]: e.transpose(o, i, ident_b[:])), [("qr", tp_), "ident_b"], [pTk])
            for g in range(2):
                P.op(PE, (lambda e, o=pTb[:, (4 + g) * 128:(5 + g) * 128], i=kdup[:, g, :, :].rearrange("p a d -> p (a d)"): e.transpose(o, i, ident_b[:])),
                     [("kdup", 0, tp_), ("kdup", 1, tp_), "ident_b"], [pTk])
            P.op(ACT, (lambda e, o=qT[0][0:64, :, t * 128:(t + 1) * 128], i=pTb[0:64, 0:512].rearrange("p (k n) -> p k n", n=128): e.activation(out=o, in_=i, func=AF.Copy)),
                 [pTk, "qT_zero0"], [("qT", t, 0)])
            P.op(ACT, (lambda e, o=qT[1][64:128, :, t * 128:(t + 1) * 128], i=pTb[64:128, 0:512].rearrange("p (k n) -> p k n", n=128): e.activation(out=o, in_=i, func=AF.Copy)),
                 [pTk, "qT_zero1"], [("qT", t, 1)])
            P.op(DVE, (lambda e, o=kT2[:, :, koff + t * 128:koff + (t + 1) * 128], i=pTb[:, 512:768].rearrange("p (k n) -> p k n", n=128): e.tensor_copy(out=o, in_=i)),
                 [pTk], [("kT2", kt_i)])
        pipeline(aproj_tile, NT, 3)
        if ps_id == 0:
            dbg_dump("qT", qT[0].rearrange("p k n -> p (k n)"), [128, 4 * PASS_TOK], [("qT", t, 0) for t in range(NT)])
            dbg_dump("kT2", kT2.rearrange("p k n -> p (k n)"), [128, 2 * 1536], [("kT2", t) for t in range(12)])
        mark("p%d Aproj end" % ps_id)
        if stop_after == "Aproj":
            break
        free_block(waqk)
        free_block(wakvk)
        free_block(whik)
        whq, whqk = load_block(wcols(w_in_d, 0, 512), 8, 512, defer=True)
        whf = [None, None]
        whfk = [None, None]
        whf[0], whfk[0] = load_block(wcols(w_in_d, 512, 512), 8, 512, defer=True)
        whf[1], whfk[1] = load_block(wcols(w_in_d, 1024, 512), 8, 512, defer=True)
        whg, whgk = load_block(wcols(w_in_d, 2048, 512), 8, 512, defer=True)
        wgl = [None] * 4
        wglk = [None] * 4

        pti = 0
        stpair = [0]
        groups = []
        for (t0, ntl) in seqs:
            ktiles = (list(range(0, 4)) if latent else []) + [t0 + i + ktoff for i in range(ntl)]
            for g in range(2):
                for tq in range(t0, t0 + ntl):
                    groups.append((g, tq, ktiles))
        asteps = [(gi, si) for gi, (g, tq, kt_) in enumerate(groups) for si in range(len(kt_))]

        def emit_scores(step):
            gi, si = step
            g, tq, ktiles = groups[gi]
            s = ktiles[si]
            nonlocal_pti = pti_box[0]
            pb = pt[nonlocal_pti % 4]
            pbk = ("pt", nonlocal_pti % 4)
            pti_box[0] += 1
            kcol = slice(s * 128, (s + 1) * 128)
            st, stk = ps_get("st4")
            stpair[0] += 1
            for par in range(2):
                P.op(PE, (lambda e, o=st[:, par * 256:(par + 1) * 256], l=kT2[:, g, kcol], r=qT[par][:, 2 * g:2 * g + 2, tq * 128:(tq + 1) * 128]:
                          e.matmul(o, lhsT=l, rhs=r, start=True, stop=True, skip_group_check=True)),
                     [("kT2", s), ("qT", tq, par)], [stk])
            P.op(ACT, (lambda e, o=pb[:, 0:512], i=st[:, 0:512]: e.activation(out=o, in_=i, func=AF.Exp, scale=0.125)),
                 [stk], [(pbk, 0), (pbk, 1)])
            return pb, pbk

        pti_box = [0]
        LOOK = 2
        inflight = [emit_scores(asteps[k]) for k in range(min(LOOK, len(asteps)))]
        acc = acck = None
        for i, (gi, si) in enumerate(asteps):
            g, tq, ktiles = groups[gi]
            nkt = len(ktiles)
            s = ktiles[si]
            if si == 0:
                pump(2)
                acc, acck = ps_get("mm2")
            if i % 2 == 1:
                next(ada2, None)
            if i + LOOK < len(asteps):
                inflight.append(emit_scores(asteps[i + LOOK]))
            pb, pbk = inflight.pop(0)
            P.op(PE, (lambda e, a=acc, l=v_ext[:, s, g, 64:192], r=pb[:, 0:256], st_=(si == 0), sp=(si == nkt - 1):
                      e.matmul(a[:, 0:256], lhsT=l, rhs=r, start=st_, stop=sp, skip_group_check=True)),
                 [("v_ext", s), "v_ext_all", (pbk, 0)], [acck])
            P.op(PE, (lambda e, a=acc, l=v_ext[:, s, g, 0:128], r=pb[:, 256:512], sp=(si == nkt - 1):
                      e.matmul(a[:, 256:512], lhsT=l, rhs=r, start=False, stop=sp, skip_group_check=True)),
                 [("v_ext", s), "v_ext_all", (pbk, 1)], [acck])
            if si == nkt - 1:
                tqs = slice(tq * 128, (tq + 1) * 128)
                if latent:
                    P.op(DVE, lambda e, a=acc: e.reciprocal(out=recs[64:128, 0:256], in_=a[64:128, 0:256]), [acck], ["recs_a"])
                else:
                    P.op(ACT, lambda e, a=acc: e.activation(out=recs[64:128, 0:256], in_=a[64:128, 0:256], func=AF.Ln), [acck], ["recs_a"])
                    P.op(ACT, lambda e: e.activation(out=recs[64:128, 0:256], in_=recs[64:128, 0:256], func=AF.Exp, scale=-1.0), ["recs_a"], ["recs_a"])
                P.op(DVE, lambda e, a=acc: e.reciprocal(out=recs[0:64, 256:512], in_=a[0:64, 256:512]), [acck], ["recs_b"])
                P.op(DVE, (lambda e, a=acc, o=o_aT[0:64, 2 * g:2 * g + 2, tqs]: e.tensor_tensor(out=o, in0=a[0:64, 0:256].rearrange("p (k n) -> p k n", n=128),
                                                                                             in1=recs[64:128, 0:256].rearrange("p (k n) -> p k n", n=128), op=ALU.mult)),
                     [acck, "recs_a"], [("o_aT", tq, g, 0)])
                P.op(DVE, (lambda e, a=acc, o=o_aT[64:128, 2 * g:2 * g + 2, tqs]: e.tensor_tensor(out=o, in0=a[64:128, 256:512].rearrange("p (k n) -> p k n", n=128),
                                                                                               in1=recs[0:64, 256:512].rearrange("p (k n) -> p k n", n=128), op=ALU.mult)),
                     [acck, "recs_b"], [("o_aT", tq, g, 1)])
        flush()
        for _ in ada2:
            pass
        o_aT_keys = [("o_aT", tq, g, p_) for tq in range(NT) for g in range(2) for p_ in range(2)]
        if ps_id == 0:
            dbg_dump("o_aT", o_aT[:].rearrange("p k n -> p (k n)"), [128, 4 * PASS_TOK], o_aT_keys)
        mark("p%d attn end" % ps_id)
        if stop_after == "attn":
            break

        P.barrier()
        wgl[0], wglk[0] = load_block(wcols(w_in_d, 3328, 512), 8, 512, defer=True)
        wgl[2], wglk[2] = load_block(wcols(w_in_d, 3328 + 2 * 512, 512), 8, 512, defer=True)

        def proj_fm(wv, wk, h, half, group="mm"):
            pp, ppk = ps_get(group)
            for kc in range(8):
                P.op(PE, (lambda e, p_=pp, l=wv[:, kc, h * 128:(h + 1) * 128], r=hT[:, kc, half * 512:(half + 1) * 512], st=(kc == 0), sp=(kc == 7):
                          e.matmul(p_[:], lhsT=l, rhs=r, start=st, stop=sp)),
                     [wk] + hT_all[half * 4:half * 4 + 4], [ppk])
            return pp, ppk

        def prep_gen(h):
            hb = h % 2
            for half in range(2):
                hs = slice(half * 512, (half + 1) * 512)
                pp, ppk = proj_fm(whq, whqk, h, half)
                P.op(ACT, (lambda e, o=qTh[:, hs], i=pp[:]: e.activation(out=o, in_=i, func=AF.Copy, scale=128.0 ** -0.5)), [ppk], [("qTh", half)])
                yield None
            for d in range(2):
                for half in range(2):
                    hs = slice(half * 512, (half + 1) * 512)
                    pp, ppk = proj_fm(whf[d], whfk[d], h, half)
                    P.op(ACT, (lambda e, o=uT[:, hs], i=pp[:]: e.activation(out=o, in_=i, func=AF.Sigmoid, scale=-1.0)), [ppk], [("uT", half)])
                    yield None
                if d == 0:
                    assert hg_owner.get(hb) in (None, h - 2) and (h < 2 or norm_done.get(h - 2)), (h, hg_owner, norm_done)
                    hg_owner[hb] = h
                    for half in range(2):
                        hs = slice(half * 512, (half + 1) * 512)
                        pp, ppk = proj_fm(whg, whgk, h, half)
                        P.op(ACT, (lambda e, o=hgTh[hb][:, hs], i=pp[:]: e.activation(out=o, in_=i, func=AF.Silu)), [ppk], [("hgTh", hb, half)])
                        yield None
                ukeys = [("uT", 0), ("uT", 1)]
                P.op(DVE, (lambda e, s=oml[:, d * 4 + h:d * 4 + h + 1]: e.tensor_scalar(out=uT, in0=uT, scalar1=s, scalar2=None, op0=ALU.mult)),
                     ukeys + ["oml"], ukeys)
                P.op(ACT, lambda e: e.activation(out=lf, in_=uT, func=AF.Ln, scale=-1.0, bias=1.0), ukeys, ["lf"])
                yield None
                if d == 0:
                    P.op(DVE, lambda e: e.tensor_tensor_scan(out=aT, data0=smask_f[:], data1=lf, initial=0.0, op0=ALU.mult, op1=ALU.add),
                         ["smask_f", "lf"], ["aT"])
                else:
                    P.op(DVE, lambda e: e.tensor_tensor_scan(out=aT[:, ::-1], data0=smask_f[:], data1=lf[:, ::-1], initial=0.0, op0=ALU.mult, op1=ALU.add),
                         ["smask_f", "lf"], ["aT"])
                yield None
                P.op(ACT, lambda e: e.activation(out=lf, in_=aT, func=AF.Exp, scale=-1.0), ["aT"], ["lf"])
                P.op(ACT, lambda e: e.activation(out=aT, in_=aT, func=AF.Exp), ["aT", "lf"], ["aT"])
                yield None
                a3 = aT.rearrange("p (c t) -> p c t", t=CH)
                dsel = (CH - 1) if d == 0 else 0
                P.op(POOL, (lambda e, o=decs[hb * 2 + d][:], i=a3[:, :, dsel]: e.tensor_copy(out=o, in_=i)), ["aT"], [("decs", hb, d)])
                P.op(DVE, (lambda e, o=qd[hb * 2 + d]: e.tensor_tensor(out=o, in0=qTh, in1=aT, op=ALU.mult)), [("qTh", 0), ("qTh", 1), "aT"], [("qd", hb, d)])
                yield None
                P.op(DVE, (lambda e, o=kd[hb * 2 + d]: e.tensor_tensor(out=o, in0=uT, in1=lf, op=ALU.mult)), ukeys + ["lf"], [("kd", hb, d)])
                yield None
                pT, pTk = ps_get("tp1")
                pTb = pT.bitcast(BF16)
                for t in range(NT):
                    P.op(PE, (lambda e, o=pTb[:, t * 128:(t + 1) * 128], i=kd[hb * 2 + d][:, t * 128:(t + 1) * 128]: e.transpose(o, i, ident_b[:])), [("kd", hb, d), "ident_b"], [pTk])
                P.op(ACT, (lambda e, o=kdtok[hb * 2 + d], i=pTb[:, 0:1024].rearrange("p (k n) -> p k n", n=128): e.activation(out=o, in_=i, func=AF.Copy)), [pTk], [("kdtok", hb, d)])
                yield None

        def scan_gen(h, pre_norm=None):
            hb = h % 2
            nround = [0]

            def state_chain(d, si_, t0, ntl, slot):
                Tl = Tst[2 * slot:2 * slot + 2]
                dk_ = ("decs", hb, d)
                chunks = list(range(t0 * 2, (t0 + ntl) * 2))
                if d == 1:
                    chunks = chunks[::-1]
                Sin = Sinit[d]
                Sink = ("Sinit", d)
                S0 = S0f[d]
                S0k = ("S0f", d)
                if latent:
                    P.dma(S0, s0_d[d * 4 + h, :, :], writes=[S0k])
                    P.op(ACT, (lambda e, o=Sin, i=S0: e.activation(out=o, in_=i, func=AF.Copy)), [S0k], [Sink])
                tcur = None
                tcurk = None
                prev_dec = None
                for i, c in enumerate(chunks):
                    t, n = c // 2, c % 2
                    pk, pkk = ps_get("kvr")
                    P.op(PE, (lambda e, p_=pk, l=kdtok[hb * 2 + d][n * CH:(n + 1) * CH, t, :], r=v_tok[n * CH:(n + 1) * CH, t, h * 128:(h + 1) * 128]:
                              e.matmul(p_[:, 0:128], lhsT=l, rhs=r, start=True, stop=True)),
                         [("kdtok", hb, d), ("v_tok", t)], [pkk])
                    dec = decs[hb * 2 + d][:, c:c + 1]
                    tn = Tl[i % 2]
                    tnk = ("Tst", slot, i % 2)
                    if i == 0:
                        if latent:
                            P.op(DVE, (lambda e, o=tn, p_=pk, i_=S0: e.tensor_tensor(out=o, in0=p_[:, 0:128], in1=i_, op=ALU.add)), [pkk, S0k], [tnk])
                        else:
                            P.op(DVE, (lambda e, o=tn, p_=pk: e.tensor_copy(out=o, in_=p_[:, 0:128])), [pkk], [tnk])
                    else:
                        P.op(DVE, (lambda e, o=tn, i_=tcur, s=prev_dec, p_=pk: e.scalar_tensor_tensor(out=o, in0=i_, scalar=s, in1=p_[:, 0:128], op0=ALU.mult, op1=ALU.add)),
                             [tcurk, dk_, pkk], [tnk])
                    tcur, tcurk, prev_dec = tn, tnk, dec
                    if i % 2 == 0:
                        P.op(ACT, (lambda e, o=Sall[:, d, c, :], i_=tcur, s=dec: e.activation(out=o, in_=i_, func=AF.Copy, scale=s)), [tcurk, dk_], [("Sall", d, c)])
                    else:
                        P.op(DVE, (lambda e, o=Sall[:, d, c, :], i_=tcur, s=dec: e.tensor_scalar(out=o, in0=i_, scalar1=s, scalar2=None, op0=ALU.mult)), [tcurk, dk_], [("Sall", d, c)])
                    yield None
                if not latent:
                    sfin = sfin_l[sfin_i[0] % 4]
                    sfk = ("sfin", sfin_i[0] % 4)
                    sfin_i[0] += 1
                    P.op(DVE, (lambda e, o=sfin, i_=tcur, s=prev_dec: e.tensor_scalar(out=o, in0=i_, scalar1=s, scalar2=None, op0=ALU.mult)), [tcurk, dk_], [sfk])
                    P.dma(ns_d[si_, d * 4 + h, :, :], sfin, reads=[sfk], eng=POOL)

            seq_groups = [seqs] if latent else [seqs[0:2], seqs[2:4]]
            sbase = 0
            for grp in seq_groups:
                gens = []
                for gi_, (t0, ntl) in enumerate(grp):
                    for d in range(2):
                        gens.append(state_chain(d, sbase + gi_, t0, ntl, gi_ * 2 + d))
                sbase += len(grp)
                active = list(gens)
                while active:
                    for gi2, gch in enumerate(list(active)):
                        try:
                            next(gch)
                        except StopIteration:
                            active.remove(gch)
                        if gi2 == 1 and len(gens) > 2:
                            yield None
                    pump(1)
                    nround[0] += 1
                    if nround[0] == 2 and pre_norm is not None:
                        pre_norm()
                    yield None

            steps = [(t0, ntl, t, d) for (t0, ntl) in seqs for t in range(t0, t0 + ntl) for d in range(2)]
            smi = [0]

            def emit_scores(step):
                t0, ntl, t, d = step
                ts_ = slice(t * 128, (t + 1) * 128)
                mask = maskf if d == 0 else maskb
                mkey = "maskf" if d == 0 else "maskb"
                sc, sck = ps_get("h1")
                P.op(PE, (lambda e, p_=sc, l=kd[hb * 2 + d][:, ts_], r=qd[hb * 2 + d][:, ts_]: e.matmul(p_[:, 0:128], lhsT=l, rhs=r, start=True, stop=True)),
                     [("kd", hb, d), ("qd", hb, d)], [sck])
                sm = smk[smi[0] % 4]
                smkk = ("smk", smi[0] % 4)
                smi[0] += 1
                P.op(DVE, (lambda e, o=sm, p_=sc, m=mask: e.tensor_tensor(out=o, in0=p_[:, 0:128], in1=m[:], op=ALU.mult)), [sck, mkey], [smkk])
                return sm, smkk

            OLOOK = 2
            infl = [emit_scores(steps[k]) for k in range(min(OLOOK, len(steps)))]
            po = pok = None
            for i, (t0, ntl, t, d) in enumerate(steps):
                if i + OLOOK < len(steps):
                    infl.append(emit_scores(steps[i + OLOOK]))
                sm, smkk = infl.pop(0)
                ts_ = slice(t * 128, (t + 1) * 128)
                if d == 0:
                    po, pok = ps_get("h2")
                P.op(PE, (lambda e, p_=po, l=v_tok[:, t, h * 128:(h + 1) * 128], r=sm, st=(d == 0): e.matmul(p_[:, 0:128], lhsT=l, rhs=r, start=st, stop=False, skip_group_check=True)),
                     [("v_tok", t), smkk], [pok])
                for n in range(2):
                    c = t * 2 + n
                    cs = slice(c * CH, (c + 1) * CH)
                    if d == 0:
                        cprev = c - 1 if c > t0 * 2 else None
                    else:
                        cprev = c + 1 if c < (t0 + ntl) * 2 - 1 else None
                    if cprev is None:
                        if not latent:
                            continue
                        Sb, Sbk = Sinit[d], ("Sinit", d)
                    else:
                        Sb, Sbk = Sall[:, d, cprev, :], ("Sall", d, cprev)
                    P.op(PE, (lambda e, o=po[:, n * CH:(n + 1) * CH], l=Sb, r=qd[hb * 2 + d][:, cs]:
                              e.matmul(o, lhsT=l, rhs=r, start=False, stop=True, skip_group_check=True)),
                         [Sbk, ("qd", hb, d)], [pok])
                if d == 1:
                    P.op(ACT, (lambda e, o=oT[:, ts_], p_=po: e.activation(out=o, in_=p_[:, 0:128], func=AF.Copy)), [pok], [("oT", t)])
                    pump(2)
                    yield None

        def norm_head(h):
            hb = h % 2
            assert hg_owner.get(hb) == h, (h, hg_owner)
            norm_done[h] = True
            for half in range(2):
                hs = slice(half * 512, (half + 1) * 512)
                okeys = [("oT", t) for t in range(half * 4, half * 4 + 4)]
                P.op(POOL, (lambda e, i=oT[:, hs]: e.tensor_tensor(out=osq, in0=i, in1=i, op=ALU.mult)), okeys, ["osq"])
                pn, pnk = ps_get("mm")
                P.op(PE, (lambda e, p_=pn: e.matmul(p_[:], lhsT=ones_b[:], rhs=osq, start=True, stop=True)), ["ones_b", "osq"], [pnk])
                P.op(ACT, (lambda e, p_=pn: e.activation(out=ors, in_=p_[:], func=AF.Ln, scale=1.0 / 128, bias=epsc[:, 0:1])), [pnk, "epsc"], ["ors"])
                P.op(ACT, lambda e: e.activation(out=ors, in_=ors, func=AF.Exp, scale=-0.5), ["ors"], ["ors"])
                P.op(DVE, (lambda e, i=oT[:, hs]: e.scalar_tensor_tensor(out=ot1, in0=i, scalar=hnw[:, 0:1], in1=ors, op0=ALU.mult, op1=ALU.mult)),
                     okeys + ["hnw", "ors"], ["ot1"])
                P.op(DVE, (lambda e, o=o_hT[:, h, hs], g_=hgTh[hb][:, hs]: e.tensor_tensor(out=o, in0=ot1, in1=g_, op=ALU.mult)), ["ot1", ("hgTh", hb, half)], [("o_hT", h, half)])

        def drain(g):
            for _ in g:
                pass

        def interleave(ga, gb, na=1, nb=1):
            da = db = False
            while not (da and db):
                for _ in range(na):
                    if not da:
                        try:
                            next(ga)
                        except StopIteration:
                            da = True
                for _ in range(nb):
                    if not db:
                        try:
                            next(gb)
                        except StopIteration:
                            db = True

        hg_owner = {}
        norm_done = {}
        drain(prep_gen(0))
        if ps_id == 0:
            dbg_dump("qd0", qd[0], [128, PASS_TOK], [("qd", 0, 0)])
            dbg_dump("kd1", kd[1], [128, PASS_TOK], [("kd", 0, 1)])
            dbg_dump("decs1", decs[1][:], [128, 16], [("decs", 0, 1)])
        for h in range(4):
            pn_ = (lambda hh=h - 1: norm_head(hh)) if h > 0 else None
            if h < 3:
                interleave(scan_gen(h, pn_), prep_gen(h + 1), 1, 1)
            else:
                for k_ in (whqk, whfk[0], whfk[1], whgk):
                    free_block(k_)
                who, whok = load_block(w_ho_d.rearrange("(k p) n -> p k n", p=128), 4, 1024, defer=True)
                wao, waok = load_block(w_ao_d.rearrange("(k p) n -> p k n", p=128), 4, 1024, defer=True)
                wgl[1], wglk[1] = load_block(wcols(w_in_d, 3328 + 1 * 512, 512), 8, 512, defer=True)
                wgl[3], wglk[3] = load_block(wcols(w_in_d, 3328 + 3 * 512, 512), 8, 512, defer=True)
                drain(scan_gen(h, pn_))
        norm_head(3)
        o_hT_keys = [("o_hT", h, half) for h in range(4) for half in range(2)]
        if ps_id == 0:
            dbg_dump("o_hT", o_hT[:].rearrange("p k n -> p (k n)"), [128, 4 * PASS_TOK], o_hT_keys)
        mark("p%d hgrn end" % ps_id)
        if stop_after == "hgrn":
            break

        P.barrier()
        expand(mods[:, 16:24, j], ["mods_b"], bcS, "bcS")
        wo = [None, None]
        wok = [None, None]
        for f in range(8):
            if f == 4:
                flush()
                free_block(wglk[0])
                free_block(wglk[2])
                for chh in range(2):
                    wo[chh], wok[chh] = load_block(wcols(w_out_d, chh * 512, 512), 8, 512, fold_bc=bcS[:, chh * 512:(chh + 1) * 512], fold_key="bcS", defer=True)
            for half in range(2):
                hs = slice(half * 512, (half + 1) * 512)
                hkeys = hT_all[half * 4:half * 4 + 4]
                fcs = slice((f % 4) * 128, (f % 4 + 1) * 128)
                pg0, pg0k = ps_get("mm")
                for kc in range(8):
                    P.op(PE, (lambda e, p_=pg0, l=wgl[f // 4][:, kc, fcs], r=hT[:, kc, hs], st=(kc == 0), sp=(kc == 7): e.matmul(p_[:], lhsT=l, rhs=r, start=st, stop=sp)),
                         [wglk[f // 4]] + hkeys, [pg0k])
                P.op(ACT, (lambda e, p_=pg0: e.activation(out=g0, in_=p_[:], func=AF.Sigmoid)), [pg0k], ["g0"])
                pg1, pg1k = ps_get("mm")
                for kc in range(8):
                    P.op(PE, (lambda e, p_=pg1, l=wgl[2 + f // 4][:, kc, fcs], r=hT[:, kc, hs], st=(kc == 0), sp=(kc == 7): e.matmul(p_[:], lhsT=l, rhs=r, start=st, stop=sp)),
                         [wglk[2 + f // 4]] + hkeys, [pg1k])
                P.op(ACT, (lambda e, p_=pg1: e.activation(out=g1, in_=p_[:], func=AF.Sigmoid)), [pg1k], ["g1"])
                ph, phk = ps_get("mm2")
                for hh in range(4):
                    P.op(PE, (lambda e, p_=ph, l=who[:, hh, f * 128:(f + 1) * 128], r=o_hT[:, hh, hs], st=(hh == 0), sp=(hh == 3): e.matmul(p_[:], lhsT=l, rhs=r, start=st, stop=sp)),
                         [whok, ("o_hT", hh, half)], [phk])
                pa, pak = ps_get("kv")
                for pr_ in range(4):
                    P.op(PE, (lambda e, p_=pa, l=wao[:, pr_, f * 128:(f + 1) * 128], r=o_aT[:, pr_, hs], st=(pr_ == 0), sp=(pr_ == 3): e.matmul(p_[:], lhsT=l, rhs=r, start=st, stop=sp)),
                         [waok] + [k for k in o_aT_keys if k[1] // 4 == half], [pak])
                P.op(DVE, (lambda e, p_=ph: e.tensor_tensor(out=m0, in0=p_[:], in1=g0, op=ALU.mult)), [phk, "g0"], ["m0"])
                P.op(DVE, (lambda e, p_=pa: e.tensor_tensor(out=m1, in0=p_[:], in1=g1, op=ALU.mult)), [pak, "g1"], ["m1"])
                P.op(POOL, (lambda e, o=mT[:, f, hs]: e.tensor_tensor(out=o, in0=m0, in1=m1, op=ALU.add)), ["m0", "m1"], [("mT", f, half)])
                pump(1)
        flush()
        if ps_id == 0:
            dbg_dump("mT", mT.rearrange("p k n -> p (k n)"), [128, 8 * PASS_TOK], [("mT", f, half) for f in range(8) for half in range(2)])
        mark("p%d C1 end" % ps_id)
        if stop_after == "C1":
            break

        for k_ in (wglk[1], wglk[3], whok, waok):
            free_block(k_)
        expand(mods[:, 40:48, j], ["mods_b"], bcG2, "bcG2")

        def load_w1(fg):
            r = [load_block(wcols(w_ff1_d, fg * 1024 + b * 512, 512), 8, 512, defer=True, cast_eng=ACT) for b in range(2)]
            return [x[0] for x in r], [x[1] for x in r]

        def load_w2(fg):
            r = [load_block(wcols(w_ff2_d, chh * 512, 512, r0=fg * 1024), 8, 512, fold_bc=bcG2[:, chh * 512:(chh + 1) * 512], fold_key="bcG2", defer=True) for chh in range(2)]
            return [x[0] for x in r], [x[1] for x in r]

        w1_next = load_w1(0)
        w2_next = load_w2(0)
        P.op(DVE, (lambda e, j=j: e.scalar_tensor_tensor(out=vecT[:], in0=mods[:, 32:40, j], scalar=1.0, in1=nfwT[:], op0=ALU.add, op1=ALU.mult)),
             ["mods_b", "nfwT"], ["vecT"])
        expand(vecT, ["vecT"], bcA, "bcA")
        expand(mods[:, 24:32, j], ["mods_b"], bcS, "bcS")
        n2 = pipeline_gen(lambda t: norm_mod_transpose(t, x1[:, t, :], [("x1", t, 0), ("x1", t, 1)], h1_d), NT, 2)
        for t in range(NT):
            xb = xt[t % 2]
            xk = ("xt", t % 2)
            P.dma(xb[:], x_d[row0 + t * 128:row0 + (t + 1) * 128, :], writes=[xk])
            if t > 0:
                next(n2, None)
                next(n2, None)
            for chh in range(2):
                po, pok = ps_get("mm")
                for kc in range(8):
                    P.op(PE, (lambda e, p_=po, l=mT[:, kc, t * 128:(t + 1) * 128], r=wo[chh][:, kc, :], st=(kc == 0), sp=(kc == 7): e.matmul(p_[:], lhsT=l, rhs=r, start=st, stop=sp)),
                         [("mT", kc, t // 4), wok[chh]], [pok])
                P.op(DVE, (lambda e, p_=po, o=x1[:, t, chh * 512:(chh + 1) * 512], i=xb[:, chh * 512:(chh + 1) * 512]: e.tensor_tensor(out=o, in0=p_[:], in1=i, op=ALU.add)),
                     [pok, xk], [("x1", t, chh)])
                pump(1)
        flush()
        for _ in n2:
            pass
        if ps_id == 0:
            dbg_dump("x1", x1.rearrange("p k n -> p (k n)"), [128, NT * D], [("x1", t, c) for t in range(NT) for c in range(2)])
        mark("p%d C2 end" % ps_id)
        if stop_after == "C2":
            break

        free_block(wok[0])
        free_block(wok[1])
        rli = 0

        def final_norm_tile(t):
            p = t % 2
            xk2 = [("x1", t, 0), ("x1", t, 1)]
            P.op(ACT, (lambda e, i=x1[:, t, :]: e.activation(out=junk[:], in_=i, func=AF.Square, accum_out=ssq[:, 8 + p:9 + p])), xk2, ["junk", ("ssqF", p)])
            P.op(POOL, lambda e: e.tensor_scalar(out=rstd[:, 10 + 2 * p:11 + 2 * p], in0=ssq[:, 8 + p:9 + p], scalar1=1.0 / D, scalar2=EPS, op0=ALU.mult, op1=ALU.add), [("ssqF", p)], [("rstd1F", p)])
            P.op(POOL, lambda e: e.tensor_tensor(out=rstd[:, 11 + 2 * p:12 + 2 * p], in0=rstd[:, 10 + 2 * p:11 + 2 * p], in1=mhalf[:, 0:1], op=ALU.pow), [("rstd1F", p), "mhalf"], [("rstdF", p)])
            yb = h1_d[p]
            yk = ("h1", p)
            P.op(DVE, (lambda e, i=x1[:, t, :], o=yb[:]: e.scalar_tensor_tensor(out=o, in0=i, scalar=rstd[:, 11 + 2 * p:12 + 2 * p], in1=fnw_bc, op0=ALU.mult, op1=ALU.mult)),
                 xk2 + [("rstdF", p), "fnw_bc"], [yk])
            P.dma(y_d[row0 + t * 128:row0 + (t + 1) * 128, :], yb[:], reads=[yk])

        for fg in range(4):
            w1, w1k = w1_next
            w2, w2k = w2_next
            if fg < 3:
                w1_next = load_w1(fg + 1)
            for ffc in range(8):
                for half in range(2):
                    hs = slice(half * 512, (half + 1) * 512)
                    pf, pfk = ps_get("mm")
                    for kc in range(8):
                        P.op(PE, (lambda e, p_=pf, l=w1[ffc // 4][:, kc, (ffc % 4) * 128:(ffc % 4 + 1) * 128], r=hT[:, kc, hs], st=(kc == 0), sp=(kc == 7):
                                  e.matmul(p_[:], lhsT=l, rhs=r, start=st, stop=sp)),
                             [w1k[ffc // 4]] + hT_all[half * 4:half * 4 + 4], [pfk])
                    rb = rl[rli % 2]
                    rbk = ("rl", rli % 2)
                    rli += 1
                    P.op(ACT, (lambda e, p_=pf, o=rb: e.activation(out=o, in_=p_[:], func=AF.Relu)), [pfk], [rbk])
                    P.op(DVE, (lambda e, o=aTf[:, ffc, hs], i=rb: e.tensor_tensor(out=o, in0=i, in1=i, op=ALU.mult)), [rbk], [("mT", ffc, half)])
                    pump(1)
            flush()
            free_block(w1k[0])
            free_block(w1k[1])
            if fg < 3:
                w2_next = load_w2(fg + 1)
            elif ps_id == 0:
                a0_next = a0_stage_gen(1)
            if fg == 3:
                P.dma(fnw_bc, fnw_d.partition_broadcast(128), writes=["fnw_bc"])
            for t in range(NT):
                for chh in range(2):
                    if fg == 3 and a0_next is not None:
                        next(a0_next, None)
                    p2, p2k = ps_get("tp")
                    for ffc in range(8):
                        P.op(PE, (lambda e, p_=p2, l=aTf[:, ffc, t * 128:(t + 1) * 128], r=w2[chh][:, ffc, :], st=(ffc == 0), sp=(ffc == 7): e.matmul(p_[:], lhsT=l, rhs=r, start=st, stop=sp)),
                             [("mT", ffc, t // 4), w2k[chh]], [p2k])
                    xs_ = x1[:, t, chh * 512:(chh + 1) * 512]
                    P.op(DVE, (lambda e, p_=p2, o=xs_: e.tensor_tensor(out=o, in0=p_[:], in1=o, op=ALU.add)), [p2k, ("x1", t, chh)], [("x1", t, chh)])
                    pump(1)
                if fg == 3 and ps_id == 1:
                    final_norm_tile(t)
            flush()
            free_block(w2k[0])
            free_block(w2k[1])
        if a0_next is not None:
            for _ in a0_next:
                pass
        if ps_id == 0:
            for t in range(NT):
                final_norm_tile(t)
        mark("p%d D end" % ps_id)
        if stop_after == "pass0":
            break

    P.finalize()
    return nc, dbg_out


def _consts():
    ident = np.eye(128, dtype=np.float32)
    j = np.arange(128)[:, None]
    i = np.arange(128)[None, :]
    same = (j // CH) == (i // CH)
    maskf = (same & (j <= i)).astype(np.float32)
    maskb = (same & (j >= i)).astype(np.float32)
    n_tokens = 1024
    gw = 64
    row = np.repeat(np.arange(n_tokens // gw, dtype=np.float32), gw)
    col = np.tile(np.arange(gw, dtype=np.float32), n_tokens // gw)
    axis_dim = 32
    freqs = (np.float32(10000.0) ** (-np.arange(0, axis_dim, 2, dtype=np.float32) / np.float32(axis_dim))).astype(np.float32)
    ang_r = (row[:, None] * freqs).astype(np.float32)
    ang_c = (col[:, None] * freqs).astype(np.float32)
    cr, sr = np.cos(ang_r).astype(np.float32), np.sin(ang_r).astype(np.float32)
    cc, sc = np.cos(ang_c).astype(np.float32), np.sin(ang_c).astype(np.float32)
    C = np.concatenate([cr, cr, cc, cc], axis=1).astype(np.float32)
    S = np.concatenate([-sr, sr, -sc, sc], axis=1).astype(np.float32)
    return ident, maskf, maskb, C, S


def make_in_maps(inp):
    f = lambda a: np.ascontiguousarray(np.asarray(a, dtype=np.float32))
    ident, maskf, maskb, C, S = _consts()
    shared = {
        "w_ada": f(inp["w_ada"][0]),
        "b_adaT": f(np.asarray(inp["b_ada"][0]).reshape(48, 128).T),
        "nmwT": f(np.asarray(inp["norm_mix_w"][0]).reshape(8, 128).T),
        "nfwT": f(np.asarray(inp["norm_ffn_w"][0]).reshape(8, 128).T),
        "fnw": f(np.asarray(inp["final_norm_w"]).reshape(1, D)),
        "w_in": f(inp["w_in"][0]),
        "qnw": f(np.asarray(inp["q_norm_w"][0]).reshape(1, 64)),
        "knw": f(np.asarray(inp["k_norm_w"][0]).reshape(1, 64)),
        "lblT": f(np.asarray(inp["hgrn_lb_logits"]).reshape(2, 2, 4, 128).transpose(3, 0, 1, 2).reshape(128, 16)),
        "hnw": f(np.asarray(inp["hgrn_norm_w"][0]).reshape(128, 1)),
        "w_ho": f(inp["w_hgrn_out"][0]),
        "w_ao": f(inp["w_attn_out"][0]),
        "w_out": f(inp["w_out"][0]),
        "w_ff1": f(inp["w_ff1"][0]),
        "w_ff2": f(inp["w_ff2"][0]),
        "ident": ident, "maskf": maskf, "maskb": maskb, "ropeC": C, "ropeS": S,
    }
    xp = np.asarray(inp["x_prompt"], dtype=np.float32)
    xs = np.asarray(inp["x_sample"], dtype=np.float32)
    c = np.asarray(inp["c"], dtype=np.float32)
    cctx = np.asarray(inp["c_ctx"], dtype=np.float32)
    maps = []
    for i in range(8):
        m = dict(shared)
        m["x"] = np.ascontiguousarray(np.concatenate([xs[i], xp[4 * i:4 * i + 4].reshape(1024, D)], axis=0))
        m["ck"] = f(np.asarray(inp["cache_k"])[i, 0].reshape(512, 128))
        m["cv"] = f(np.asarray(inp["cache_v"])[i, 0].reshape(512, 128))
        m["s0"] = f(np.asarray(inp["state_hgrn"])[i, 0].reshape(8, 128, 128))
        cond = np.stack([c[i], cctx], axis=1)
        m["condT"] = np.ascontiguousarray(cond.reshape(8, 128, 2).transpose(1, 0, 2).reshape(128, 16))
        maps.append(m)
    return maps


_NC_CACHE = {}


def kernel(**inputs):
    if "nc" not in _NC_CACHE:
        _NC_CACHE["nc"] = build_program()[0]
    nc = _NC_CACHE["nc"]
    maps = make_in_maps(inputs)
    res = run_bass_kernel_spmd(nc, maps, core_ids=list(range(8)))
    rs = res.results
    y_s = np.stack([rs[i]["y"][0:1024] for i in range(8)], axis=0)
    y_p = np.concatenate([rs[i]["y"][1024:2048].reshape(4, 256, D) for i in range(8)], axis=0)
    nk = np.concatenate([rs[i]["nk"].reshape(4, 1, 256, 2, 64) for i in range(8)], axis=0)
    nv = np.concatenate([rs[i]["nv"].reshape(4, 1, 256, 2, 64) for i in range(8)], axis=0)
    ns = np.concatenate([rs[i]["ns"].reshape(4, 1, 2, 4, 128, 128) for i in range(8)], axis=0)
    return (y_p.astype(np.float32), y_s.astype(np.float32), nk.astype(np.float32),
            nv.astype(np.float32), ns.astype(np.float32))
```
